# Optimizing a Trainium2 kernel written in Bass

```python
import jax, jax.numpy as jnp
from jax import lax
import numpy as np


D_MODEL = 1024
BATCH = 4
SEQ = 4096
DEPTH = 4

N_MIXERS = 3
N_A = (DEPTH + 2) // 3
N_B = (DEPTH + 1) // 3
N_C = DEPTH // 3
EPS = 1e-6

GM_CHUNK = 128
GM_GROUPS = 8
GM_INNER = 2 * D_MODEL
GM_GROUP_DIM = GM_INNER // GM_GROUPS

MLA_HEADS = 8
MLA_Q_RANK = D_MODEL // 4
MLA_KV_RANK = D_MODEL // 8
MLA_NOPE = 128
MLA_ROPE = 64
MLA_V = 128
ROPE_THETA = 10000.0
ATTN_BLOCK = 128

HG_HEADS = 8
HG_DK = 128
HG_DV = D_MODEL // HG_HEADS
HG_CHUNK = 64

FF_DIM = 2816
CONV_W = 3

kernel_name = 'hybrid_gmlp_mla_hgrn2_convffn_adaln'


def rmsnorm(x, g):
    xf = x.astype(jnp.float32)
    y = xf * lax.rsqrt(jnp.mean(xf * xf, axis=-1, keepdims=True) + EPS)
    return (y * g.astype(jnp.float32)).astype(x.dtype)


def layernorm(x, g, b):
    xf = x.astype(jnp.float32)
    mu = jnp.mean(xf, axis=-1, keepdims=True)
    var = jnp.mean(jnp.square(xf - mu), axis=-1, keepdims=True)
    y = (xf - mu) * lax.rsqrt(var + EPS) * g.astype(jnp.float32) + b.astype(jnp.float32)
    return y.astype(x.dtype)


def modulate(h, shift, scale):
    return h * (1.0 + scale[:, None, :]) + shift[:, None, :]


def rope_tables(positions):
    inv_freq = ROPE_THETA ** (-jnp.arange(0, MLA_ROPE, 2, dtype=jnp.float32) / MLA_ROPE)
    ang = positions.astype(jnp.float32)[..., None] * inv_freq
    return jnp.cos(ang)[:, :, None, :], jnp.sin(ang)[:, :, None, :]


def apply_rope(x, cos, sin):
    x1, x2 = jnp.split(x, 2, axis=-1)
    cos = cos.astype(x.dtype)
    sin = sin.astype(x.dtype)
    return jnp.concatenate([x1 * cos - x2 * sin, x2 * cos + x1 * sin], axis=-1)


def gmlp_mixer(h, w_in, ln_g, ln_b, w_s, b_s, w_out):
    B, S, _ = h.shape
    z = jax.nn.gelu(h @ w_in)
    u, v = jnp.split(z, 2, axis=-1)
    v = layernorm(v, ln_g, ln_b)
    n = S // GM_CHUNK
    v = v.reshape(B, n, GM_CHUNK, GM_GROUPS, GM_GROUP_DIM)
    mask = jnp.tril(jnp.ones((GM_CHUNK, GM_CHUNK), dtype=bool))
    ws = jnp.where(mask[None], w_s, jnp.zeros_like(w_s))
    sv = jnp.einsum('gts,bnsgd->bntgd', ws, v) + b_s.T[None, None, :, :, None]
    y = u * sv.reshape(B, S, GM_INNER)
    return y @ w_out


def causal_block_attention(q, k, v, scale):
    B, S, H, Dk = q.shape
    nb = S // ATTN_BLOCK
    qb = q.reshape(B, nb, ATTN_BLOCK, H, Dk).transpose(1, 0, 2, 3, 4)
    kpos = jnp.arange(S)

    def one_block(args):
        qi, bi = args
        s = jnp.einsum('bqhd,bkhd->bhqk', qi, k).astype(jnp.float32) * scale
        qpos = bi * ATTN_BLOCK + jnp.arange(ATTN_BLOCK)
        s = jnp.where(kpos[None, :] <= qpos[:, None], s, -jnp.inf)
        p = jax.nn.softmax(s, axis=-1).astype(v.dtype)
        return jnp.einsum('bhqk,bkhd->bqhd', p, v)

    o = lax.map(one_block, (qb, jnp.arange(nb)))
    return o.transpose(1, 0, 2, 3, 4).reshape(B, S, H, v.shape[-1])


def mla_mixer(h, cos, sin, w_a, q_norm_g, kv_norm_g, w_qb, w_kvb, w_o):
    B, S, _ = h.shape
    a = h @ w_a
    cq = a[..., :MLA_Q_RANK]
    ckv = a[..., MLA_Q_RANK:MLA_Q_RANK + MLA_KV_RANK]
    k_rope = a[..., MLA_Q_RANK + MLA_KV_RANK:]
    q = (rmsnorm(cq, q_norm_g) @ w_qb).reshape(B, S, MLA_HEADS, MLA_NOPE + MLA_ROPE)
    q_nope, q_rope = q[..., :MLA_NOPE], q[..., MLA_NOPE:]
    kv = (rmsnorm(ckv, kv_norm_g) @ w_kvb).reshape(B, S, MLA_HEADS, MLA_NOPE + MLA_V)
    k_nope, v = kv[..., :MLA_NOPE], kv[..., MLA_NOPE:]
    q_rope = apply_rope(q_rope, cos, sin)
    k_rope = apply_rope(k_rope[:, :, None, :], cos, sin)
    k_rope = jnp.broadcast_to(k_rope, (B, S, MLA_HEADS, MLA_ROPE))
    q = jnp.concatenate([q_nope, q_rope], axis=-1)
    k = jnp.concatenate([k_nope, k_rope], axis=-1)
    o = causal_block_attention(q, k, v, (MLA_NOPE + MLA_ROPE) ** -0.5)
    return o.reshape(B, S, MLA_HEADS * MLA_V) @ w_o


def hgrn_lower_bounds(lb_param):
    p = jax.nn.softmax(lb_param.astype(jnp.float32), axis=0)
    return jnp.cumsum(p, axis=0) - p[0:1]


def chunked_gated_recurrence(q, k, v, log_f):
    B, S, H, DK = q.shape
    DV = v.shape[-1]
    C = HG_CHUNK
    n = S // C

    def to_chunks(t):
        return t.astype(jnp.float32).reshape(B, n, C, H, t.shape[-1]).transpose(1, 0, 3, 2, 4)

    qc, kc, vc, gc = to_chunks(q), to_chunks(k), to_chunks(v), to_chunks(log_f)
    causal = jnp.tril(jnp.ones((C, C), dtype=bool))[:, :, None]

    def step(state, inp):
        qi, ki, vi, gi = inp
        bcum = jnp.cumsum(gi, axis=2)
        rel = bcum[:, :, :, None, :] - bcum[:, :, None, :, :]
        decay = jnp.exp(jnp.where(causal, rel, -jnp.inf))
        attn = jnp.einsum('bhtk,bhtsk,bhsk->bhts', qi, decay, ki)
        o = jnp.einsum('bhts,bhsv->bhtv', attn, vi) + jnp.einsum('bhtk,bhkv->bhtv', qi * jnp.exp(bcum), state)
        blast = bcum[:, :, -1:, :]
        new_state = jnp.exp(blast[:, :, 0, :])[..., None] * state + jnp.einsum('bhsk,bhsv->bhkv', ki * jnp.exp(blast - bcum), vi)
        return new_state, o

    state0 = jnp.zeros((B, H, DK, DV), jnp.float32)
    _, o = lax.scan(step, state0, (qc, kc, vc, gc))
    return o.transpose(1, 0, 3, 2, 4).reshape(B, S, H, DV).astype(v.dtype)


def hgrn2_mixer(h, lb, w_in, norm_g, w_o):
    B, S, _ = h.shape
    nk = HG_HEADS * HG_DK
    nv = HG_HEADS * HG_DV
    proj = h @ w_in
    q = proj[..., :nk]
    fl = proj[..., nk:2 * nk]
    i = proj[..., 2 * nk:2 * nk + nv]
    g = proj[..., 2 * nk + nv:]
    f = lb + (1.0 - lb) * jax.nn.sigmoid(fl.astype(jnp.float32))
    log_f = jnp.log(f)
    k = 1.0 - f
    o = chunked_gated_recurrence(
        q.reshape(B, S, HG_HEADS, HG_DK), k.reshape(B, S, HG_HEADS, HG_DK),
        i.reshape(B, S, HG_HEADS, HG_DV), log_f.reshape(B, S, HG_HEADS, HG_DK))
    o = rmsnorm(o, norm_g).reshape(B, S, nv) * jax.nn.silu(g)
    return o @ w_o


def conv_ffn(h, w_up, conv_w, conv_b, w_down):
    S = h.shape[1]
    a = h @ w_up
    pad = jnp.pad(a, ((0, 0), (CONV_W - 1, 0), (0, 0)))
    y = conv_b
    for j in range(CONV_W):
        y = y + conv_w[j] * pad[:, j:j + S, :]
    gate, val = jnp.split(y, 2, axis=-1)
    return (jax.nn.silu(gate) * val) @ w_down


def setup_inputs(seed: int = 0) -> dict:
    key = jax.random.key(seed)
    ks = jax.random.split(key, 32)
    f32 = jnp.float32

    def nrm(k, shape, scale):
        return jax.random.normal(k, shape, f32) * scale

    D = D_MODEL
    x = nrm(ks[0], (BATCH, SEQ, D), 1.0)
    c = nrm(ks[1], (BATCH, D), 1.0)
    offset = jax.random.randint(ks[2], (BATCH, 1), 0, 1024, dtype=jnp.int32)
    positions = offset + jnp.arange(SEQ, dtype=jnp.int32)[None, :]
    ada_w = nrm(ks[3], (DEPTH, D, 6 * D), 0.2 * D ** -0.5)
    ada_b = nrm(ks[4], (DEPTH, 6 * D), 0.02)
    mix_norm_g = 1.0 + nrm(ks[5], (DEPTH, D), 0.05)
    ffn_norm_g = 1.0 + nrm(ks[6], (DEPTH, D), 0.05)
    gm_w_in = nrm(ks[7], (N_A, D, 2 * GM_INNER), D ** -0.5)
    gm_ln_g = 1.0 + nrm(ks[8], (N_A, GM_INNER), 0.05)
    gm_ln_b = nrm(ks[9], (N_A, GM_INNER), 0.02)
    gm_w_s = nrm(ks[10], (N_A, GM_GROUPS, GM_CHUNK, GM_CHUNK), GM_CHUNK ** -0.5)
    gm_b_s = 1.0 + nrm(ks[11], (N_A, GM_GROUPS, GM_CHUNK), 0.1)
    gm_w_out = nrm(ks[12], (N_A, GM_INNER, D), GM_INNER ** -0.5)
    mla_w_a = nrm(ks[13], (N_B, D, MLA_Q_RANK + MLA_KV_RANK + MLA_ROPE), D ** -0.5)
    mla_q_norm_g = 1.0 + nrm(ks[14], (N_B, MLA_Q_RANK), 0.05)
    mla_kv_norm_g = 1.0 + nrm(ks[15], (N_B, MLA_KV_RANK), 0.05)
    mla_w_qb = nrm(ks[16], (N_B, MLA_Q_RANK, MLA_HEADS * (MLA_NOPE + MLA_ROPE)), MLA_Q_RANK ** -0.5)
    mla_w_kvb = nrm(ks[17], (N_B, MLA_KV_RANK, MLA_HEADS * (MLA_NOPE + MLA_V)), MLA_KV_RANK ** -0.5)
    mla_w_o = nrm(ks[18], (N_B, MLA_HEADS * MLA_V, D), (MLA_HEADS * MLA_V) ** -0.5)
    hg_lb = nrm(ks[19], (DEPTH, HG_HEADS * HG_DK), 0.5)
    hg_w_in = nrm(ks[20], (N_C, D, 2 * HG_HEADS * HG_DK + 2 * HG_HEADS * HG_DV), D ** -0.5)
    hg_norm_g = 1.0 + nrm(ks[21], (N_C, HG_DV), 0.05)
    hg_w_o = nrm(ks[22], (N_C, HG_HEADS * HG_DV, D), (HG_HEADS * HG_DV) ** -0.5)
    ff_w_up = nrm(ks[23], (DEPTH, D, 2 * FF_DIM), D ** -0.5)
    ff_conv_w = nrm(ks[24], (DEPTH, CONV_W, 2 * FF_DIM), CONV_W ** -0.5)
    ff_conv_b = nrm(ks[25], (DEPTH, 2 * FF_DIM), 0.02)
    ff_w_down = nrm(ks[26], (DEPTH, FF_DIM, D), FF_DIM ** -0.5)
    final_g = 1.0 + nrm(ks[27], (D,), 0.05)
    return {'x': x, 'c': c, 'positions': positions, 'ada_w': ada_w, 'ada_b': ada_b,
            'mix_norm_g': mix_norm_g, 'ffn_norm_g': ffn_norm_g,
            'gm_w_in': gm_w_in, 'gm_ln_g': gm_ln_g, 'gm_ln_b': gm_ln_b, 'gm_w_s': gm_w_s,
            'gm_b_s': gm_b_s, 'gm_w_out': gm_w_out,
            'mla_w_a': mla_w_a, 'mla_q_norm_g': mla_q_norm_g, 'mla_kv_norm_g': mla_kv_norm_g,
            'mla_w_qb': mla_w_qb, 'mla_w_kvb': mla_w_kvb, 'mla_w_o': mla_w_o,
            'hg_lb': hg_lb, 'hg_w_in': hg_w_in, 'hg_norm_g': hg_norm_g, 'hg_w_o': hg_w_o,
            'ff_w_up': ff_w_up, 'ff_conv_w': ff_conv_w, 'ff_conv_b': ff_conv_b, 'ff_w_down': ff_w_down,
            'final_g': final_g}


def reference(x, c, positions, ada_w, ada_b, mix_norm_g, ffn_norm_g,
              gm_w_in, gm_ln_g, gm_ln_b, gm_w_s, gm_b_s, gm_w_out,
              mla_w_a, mla_q_norm_g, mla_kv_norm_g, mla_w_qb, mla_w_kvb, mla_w_o,
              hg_lb, hg_w_in, hg_norm_g, hg_w_o,
              ff_w_up, ff_conv_w, ff_conv_b, ff_w_down, final_g):
    cos, sin = rope_tables(positions)
    lb_all = hgrn_lower_bounds(hg_lb)
    c_act = jax.nn.silu(c)
    for i in range(DEPTH):
        mod = c_act @ ada_w[i] + ada_b[i]
        sh1, sc1, g1, sh2, sc2, g2 = jnp.split(mod, 6, axis=-1)
        h = modulate(rmsnorm(x, mix_norm_g[i]), sh1, sc1)
        kind, j = i % N_MIXERS, i // N_MIXERS
        if kind == 0:
            y = gmlp_mixer(h, gm_w_in[j], gm_ln_g[j], gm_ln_b[j], gm_w_s[j], gm_b_s[j], gm_w_out[j])
        elif kind == 1:
            y = mla_mixer(h, cos, sin, mla_w_a[j], mla_q_norm_g[j], mla_kv_norm_g[j],
                          mla_w_qb[j], mla_w_kvb[j], mla_w_o[j])
        else:
            y = hgrn2_mixer(h, lb_all[i].astype(h.dtype), hg_w_in[j], hg_norm_g[j], hg_w_o[j])
        x = x + (1.0 + g1)[:, None, :] * y
        h = modulate(rmsnorm(x, ffn_norm_g[i]), sh2, sc2)
        x = x + (1.0 + g2)[:, None, :] * conv_ffn(h, ff_w_up[i], ff_conv_w[i], ff_conv_b[i], ff_w_down[i])
    return rmsnorm(x, final_g)
```

```python
import contextlib
import numpy as np
import concourse.bass as bass
import concourse.mybir as mybir
from concourse.bass_utils import run_bass_kernel_spmd

F32 = mybir.dt.float32
BF16 = mybir.dt.bfloat16
I32 = mybir.dt.int32
ALU = mybir.AluOpType
AF = mybir.ActivationFunctionType

D = 1024
NCH = 8
DEPTH = 4
FF = 2816
NFC = 22
EPS = 1e-6
GM_INNER = 2048
TT = 512
TF = 256
N_CORES = 8


class Tok:
    __slots__ = ("name", "last_w", "readers", "sem", "cnt")

    def __init__(self, name):
        self.name = name
        self.last_w = None
        self.readers = {}
        self.sem = None
        self.cnt = 0


class Op:
    __slots__ = ("engine", "fn", "reads", "writes", "dma", "chain", "deps", "signal", "ev")

    def __init__(self, engine, fn, reads, writes, dma, chain):
        self.engine = engine
        self.fn = fn
        self.reads = reads
        self.writes = writes
        self.dma = dma
        self.chain = chain
        self.deps = []
        self.signal = False
        self.ev = None


def _tok(x):
    return x.tok if isinstance(x, Buf) else x


class Ring:
    def __init__(self, bufs):
        self.bufs = bufs
        self.i = 0

    def next(self):
        b = self.bufs[self.i % len(self.bufs)]
        self.i += 1
        return b


class Buf:
    def __init__(self, name, shape, dt, psum):
        self.name, self.shape, self.dt, self.psum = name, list(shape), dt, psum
        self.t = None
        self.tok = Tok(name)

    def __getitem__(self, k):
        return self.t[k]


class Prog:
    def __init__(self, nc):
        self.nc = nc
        self.ops = []
        self.eng = {"pe": nc.tensor, "act": nc.scalar, "dve": nc.vector,
                    "pool": nc.gpsimd, "sp": nc.sync}
        self.in_scope = False

    def _mk(self, name, shape, dt, psum):
        self.uid = getattr(self, "uid", 0) + 1
        b = Buf(f"{name}_{self.uid}", shape, dt, psum)
        self.ops.append(("alloc", b, self.in_scope))
        return b

    def sb(self, name, shape, dt):
        return self._mk(name, shape, dt, False)

    def ps(self, name, shape, dt=F32):
        return self._mk(name, shape, dt, True)

    def ring(self, name, shape, dt, n, psum=False):
        return Ring([self._mk(f"{name}{i}", shape, dt, psum) for i in range(n)])

    def begin_scope(self):
        assert not self.in_scope
        self.in_scope = True
        self.ops.append(("begin",))

    def end_scope(self):
        assert self.in_scope
        self.in_scope = False
        self.ops.append(("end",))

    def op(self, engine, fn, reads=(), writes=(), dma=False, chain=None):
        o = Op(engine, fn, [_tok(r) for r in reads], [_tok(w) for w in writes], dma,
               _tok(chain) if chain is not None else None)
        self.ops.append(o)
        return o

    def dma(self, engine, fn, reads, writes, chain):
        return self.op(engine, fn, reads, writes, dma=True, chain=chain)

    def finalize(self):
        order = {}
        pending = {}
        scope_toks = []
        for idx, op in enumerate(self.ops):
            if isinstance(op, tuple):
                if op[0] == "alloc":
                    op[1].tok.readers = dict(pending)
                    if op[2]:
                        scope_toks.append(op[1].tok)
                elif op[0] == "begin":
                    scope_toks = []
                elif op[0] == "end":
                    for t in scope_toks:
                        cands = list(t.readers.items())
                        if t.last_w is not None:
                            lw = t.last_w
                            cands.append(((("dma", id(lw.chain)) if lw.dma else lw.engine), lw))
                        for k, o in cands:
                            if k not in pending or order[id(pending[k])] < order[id(o)]:
                                pending[k] = o
                    scope_toks = []
                continue
            order[id(op)] = idx
            deps = []
            for t in op.reads:
                if t.last_w is not None:
                    deps.append((t.last_w, True))
            for t in op.writes:
                if t.last_w is not None:
                    deps.append((t.last_w, False))
                deps.extend((r, False) for r in t.readers.values())
            need = []
            seen = set()
            for d, raw in deps:
                if d is op:
                    continue
                if (not d.dma) and (not op.dma) and d.engine == op.engine:
                    if not raw or op.engine == "pe":
                        continue
                if id(d) in seen:
                    continue
                seen.add(id(d))
                need.append(d)
                d.signal = True
            op.deps = need
            key = ("dma", id(op.chain)) if op.dma else op.engine
            for t in op.reads:
                t.readers[key] = op
            for t in op.writes:
                t.last_w = op
                t.readers = {}
        es_global = contextlib.ExitStack()
        es_scope = None

        def newsem(name):
            return es_global.enter_context(self.nc.semaphore(name))

        engsem = {e: newsem(f"sem_{e}") for e in ("pe", "act", "dve", "pool")}
        engcnt = {e: 0 for e in engsem}
        waited = {e: {} for e in self.eng}
        for op in self.ops:
            if isinstance(op, tuple):
                if op[0] == "alloc":
                    b = op[1]
                    st = es_scope if op[2] else es_global
                    f = self.nc.psum_tensor if b.psum else self.nc.sbuf_tensor
                    b.t = st.enter_context(f(b.name, b.shape, b.dt))
                elif op[0] == "begin":
                    es_scope = contextlib.ExitStack()
                elif op[0] == "end":
                    es_scope.close()
                    es_scope = None
                continue
            E = self.eng[op.engine]
            w = waited[op.engine]
            for d in op.deps:
                sem, val = d.ev
                k = id(sem)
                if w.get(k, 0) >= val:
                    continue
                E.wait_ge(sem, val)
                w[k] = val
            insts = op.fn(E) if op.fn is not None else None
            if op.dma:
                ch = op.chain
                if ch.sem is None:
                    ch.sem = newsem(f"dsem_{ch.name}")
                if not isinstance(insts, (list, tuple)):
                    insts = [insts]
                for ins in insts:
                    ins.then_inc(ch.sem, 16)
                    ch.cnt += 1
                op.ev = (ch.sem, 16 * ch.cnt)
            elif op.signal:
                assert insts is not None, "signalling op must emit an instruction"
                if isinstance(insts, (list, tuple)):
                    insts = insts[-1]
                engcnt[op.engine] += 1
                insts.then_inc(engsem[op.engine], 1)
                op.ev = (engsem[op.engine], engcnt[op.engine])
        es_global.close()


class Builder:
    def __init__(self, T, phases):
        self.T = T
        self.phases = phases
        nc = bass.Bass("TRN2", target_bir_lowering=False)
        self.nc = nc
        self.P = Prog(nc)
        self.decl = {}
        self.dbg_toks = []

    def din(self, name, shape, dt=F32):
        if name not in self.decl:
            self.decl[name] = self.nc.dram_tensor(name, list(shape), dt, kind="ExternalInput").ap()
        return self.decl[name]

    def dump(self, name, buf, shape, dt=F32, sl=None):
        if not getattr(self, "debug", False):
            return
        ap = self.nc.dram_tensor(name, list(shape), dt, kind="ExternalOutput").ap()
        tk = Tok("dbg" + name)
        self.dbg_toks.append(tk)
        self.P.dma("sp", lambda e: e.dma_start(out=ap, in_=(sl(buf) if sl else buf[:])), [buf], [tk], buf)

    def L(self, name, idx):
        return self.din(f"{name}__{idx}", LAYER_SHAPES[name])

    def build(self):
        nc, P, T = self.nc, self.P, self.T
        x = self.din("x", [T, D])
        cvec = self.din("c", [NCH, 128])
        final_g = self.din("final_g", [NCH, 128])
        k_ident = self.din("k_ident", [128, 128])
        k_tri = self.din("k_tri", [128, 128])
        y = nc.dram_tensor("y", [T, D], F32, kind="ExternalOutput").ap()
        self.xT_d = nc.dram_tensor("xT_scratch", [NCH, 128, T], F32, kind="Internal").ap()
        self.xT_tok = [Tok(f"xTd{i}") for i in range(T // TF)]

        ident = P.sb("ident", [128, 128], F32)
        ones_b = P.sb("ones_b", [128, 128], BF16)
        tri_f = P.sb("tri_f", [128, 128], F32)
        self.ident, self.ones_b, self.tri_f = ident, ones_b, tri_f
        P.dma("sp", lambda e: e.dma_start(out=ident[:], in_=k_ident[:, :]), [], [ident], ident)
        P.dma("sp", lambda e: e.dma_start(out=tri_f[:], in_=k_tri[:, :]), [], [tri_f], tri_f)
        P.op("dve", lambda e: e.memset(ones_b[:], 1.0), [], [ones_b])
        self.psum = P.ring("ps", [128, 512], F32, 7, psum=True)
        self.psb = P.ps("psb", [128, 1024], BF16)
        self.colscr = P.ring("colscr", [128, 128], F32, 2)

        self.phase_in(x)
        self.prologue(cvec, final_g)
        for kind, l in self.phases:
            if kind == "ffn":
                self.phase_ffn(l)
            elif kind == "gm":
                self.phase_gm(l, l // 3)
            elif kind == "mla":
                self.phase_mla(l)
            elif kind == "hg":
                self.phase_hg(l)
        self.phase_out(y)
        P.finalize()
        return nc

    def work(self, n, nxt=2, sq=True, nfw=3, nrstd=2):
        P = self.P
        self.n = n
        self.xt_ring = P.ring("xt", [128, NCH, n], F32, nxt)
        self.ht_ring = P.ring("ht", [128, NCH, n], BF16, 1)
        if sq:
            self.sq_ring = P.ring("sq", [128, NCH, n], BF16, 1)
        self.fw = P.ring("fw", [128, n], F32, nfw)
        self.rstd_ring = P.ring("rstd", [128, n], F32, nrstd)

    def load_cols(self, name, src_rows_ap, n):
        P = self.P
        dst = P.sb(name, [128, n], F32)
        scr = self.colscr.next()
        P.dma("sp", lambda e: e.dma_start(out=scr[0:n, :], in_=src_rows_ap), [], [scr], scr)
        ps = self.psum.next()
        P.op("pe", lambda e: e.transpose(ps[:, 0:n], scr[0:n, :], self.ident[0:n, 0:n]),
             [scr, self.ident], [ps])
        P.op("dve", lambda e: e.tensor_copy(dst[:], ps[:, 0:n]), [ps], [dst])
        return dst

    def prologue(self, cvec, final_g):
        P = self.P
        cT = self.load_cols("cT", cvec[:, :], NCH)
        cact = P.sb("cact", [128, NCH], F32)
        cact_b = P.sb("cact_b", [128, NCH], BF16)
        P.op("act", lambda e: e.activation(cact[:], cT[:], AF.Silu), [cT], [cact])
        P.op("dve", lambda e: e.tensor_copy(cact_b[:], cact[:]), [cact], [cact_b])
        self.final_g = self.load_cols("fing", final_g[:, :], NCH)
        layers = sorted(set(l for _, l in self.phases))
        self.G1, self.SH1, self.GA1, self.G2, self.SH2, self.GA2 = {}, {}, {}, {}, {}, {}
        mods, mgs, fgs, abs_ = {}, {}, {}, {}
        for l in layers:
            abs_[l] = self.load_cols(f"adab{l}", self.L("ada_b", l)[:, :], 48)
            mgs[l] = self.load_cols(f"mixg{l}", self.L("mix_norm_g", l)[:, :], NCH)
            fgs[l] = self.load_cols(f"ffng{l}", self.L("ffn_norm_g", l)[:, :], NCH)
            mods[l] = P.sb(f"mod{l}", [128, 48], F32)
            for nm, dd in (("Ga", self.G1), ("GAa", self.GA1), ("Gb", self.G2), ("GAb", self.GA2)):
                dd[l] = P.sb(f"{nm}{l}", [128, NCH], F32)
        P.begin_scope()
        wring = P.ring("adaw", [128, NCH, 1536], BF16, 2)
        for l in layers:
            mod, ab = mods[l], abs_[l]
            ps = self.psum.next()
            for q in range(4):
                wb = wring.next()
                src = self.L("ada_w", l)[:, q * 1536:(q + 1) * 1536].rearrange("(k p) n -> p k n", p=128)
                P.dma("pool", lambda e, wb=wb, src=src: e.dma_start(out=wb[:], in_=src), [], [wb], wb)
                for cc in range(12):
                    col = q * 12 + cc
                    for k in range(NCH):
                        P.op("pe", lambda e, wb=wb, cc=cc, k=k, col=col, ps=ps: e.matmul(
                            ps[:, col:col + 1], wb[:, k, cc * 128:(cc + 1) * 128], cact_b[:, k:k + 1],
                            start=(k == 0), stop=(k == NCH - 1)), [wb, cact_b], [ps])
            P.op("dve", lambda e, mod=mod, ps=ps, ab=ab: e.tensor_tensor(mod[:], ps[:, 0:48], ab[:], ALU.add),
                 [ps, ab], [mod])

            def derive(G, GA, gsrc, sc_off, g_off, mod=mod):
                P.op("dve", lambda e: e.scalar_tensor_tensor(
                    G[:], mod[:, sc_off:sc_off + 8], 1.0, gsrc[:], ALU.add, ALU.mult), [mod, gsrc], [G])
                P.op("dve", lambda e: e.tensor_scalar(GA[:], mod[:, g_off:g_off + 8], 1.0, None, ALU.add),
                     [mod], [GA])
            derive(self.G1[l], self.GA1[l], mgs[l], 8, 16)
            derive(self.G2[l], self.GA2[l], fgs[l], 32, 40)
            self.SH1[l] = (mod, 0)
            self.SH2[l] = (mod, 24)
        P.end_scope()

    def slab_toks(self, t0, n):
        return self.xT_tok[t0 // TF:(t0 + n) // TF]

    def load_xt(self, t0, n):
        P = self.P
        xt = self.xt_ring.next()
        src = self.xT_d[:, :, t0:t0 + n].rearrange("c p t -> p c t")
        P.dma("sp", lambda e: e.dma_start(out=xt[:, :, 0:n], in_=src), self.slab_toks(t0, n), [xt], xt)
        return xt

    def store_xt(self, xt, t0, n):
        P = self.P
        dst = self.xT_d[:, :, t0:t0 + n].rearrange("c p t -> p c t")
        P.dma("sp", lambda e: e.dma_start(out=dst, in_=xt[:, :, 0:n]), [xt], self.slab_toks(t0, n), xt)

    def rstd_of(self, xt, n):
        P = self.P
        sq = self.sq_ring.next()
        P.op("act", lambda e: e.activation(sq[:, 0:NCH, 0:n], xt[:, :, 0:n], AF.Square), [xt], [sq])
        ps = self.psum.next()
        for c in range(NCH):
            P.op("pe", lambda e, c=c: e.matmul(ps[:, 0:n], self.ones_b[:], sq[:, c, 0:n],
                                               start=(c == 0), stop=(c == NCH - 1)), [sq, self.ones_b], [ps])
        tmp = self.fw.next()
        P.op("act", lambda e: e.activation(tmp[:, 0:n], ps[:, 0:n], AF.Sqrt, scale=1.0 / D, bias=EPS), [ps], [tmp])
        rstd = self.rstd_ring.next()
        P.op("dve", lambda e: e.reciprocal(rstd[:, 0:n], tmp[:, 0:n]), [tmp], [rstd])
        return rstd

    def norm_mod(self, xt, n, G, SH, ht=None, c0=0):
        P = self.P
        rstd = self.rstd_of(xt, n)
        if ht is None:
            ht = self.ht_ring.next()
        shb, sho = SH
        for c in range(NCH):
            tmp = self.fw.next()
            P.op("dve" if c % 2 == 0 else "pool", lambda e, c=c, tmp=tmp: e.tensor_tensor(
                tmp[:, 0:n], xt[:, c, 0:n], rstd[:, 0:n], ALU.mult), [xt, rstd], [tmp])
            P.op("act", lambda e, c=c, tmp=tmp: e.activation(
                ht[:, c, c0:c0 + n], tmp[:, 0:n], AF.Identity, scale=G[:, c:c + 1], bias=shb[:, sho + c:sho + c + 1]),
                [tmp, G, shb], [ht])
        return ht

    def phase_in(self, x):
        P, T = self.P, self.T
        P.begin_scope()
        self.work(TT)
        xin_ring = P.ring("xin", [128, 4, D], F32, 2)
        for ti in range(T // TT):
            t0 = ti * TT
            xin = xin_ring.next()
            src = x[t0:t0 + TT, :].rearrange("(s p) d -> p s d", p=128)
            P.dma("sp", lambda e, xin=xin, src=src: e.dma_start(out=xin[:], in_=src), [], [xin], xin)
            xt = self.xt_ring.next()
            for c in range(NCH):
                ps = self.psum.next()
                for s in range(4):
                    P.op("pe", lambda e, ps=ps, s=s, c=c, xin=xin: e.transpose(
                        ps[:, s * 128:(s + 1) * 128], xin[:, s, c * 128:(c + 1) * 128], self.ident[:]),
                        [xin, self.ident], [ps])
                if c % 2 == 0:
                    P.op("act", lambda e, ps=ps, c=c, xt=xt: e.copy(xt[:, c, :], ps[:]), [ps], [xt])
                else:
                    P.op("dve", lambda e, ps=ps, c=c, xt=xt: e.tensor_copy(xt[:, c, :], ps[:]), [ps], [xt])
            self.store_xt(xt, t0, TT)
        P.end_scope()

    def phase_out(self, y):
        P, T = self.P, self.T
        P.begin_scope()
        self.work(TT)
        yo_ring = P.ring("yo", [128, 4, D], F32, 2)
        yts = P.ring("yT", [128, NCH, TT], F32, 1)
        out_toks = []
        for ti in range(T // TT):
            t0 = ti * TT
            xt = self.load_xt(t0, TT)
            rstd = self.rstd_of(xt, TT)
            yT = yts.next()
            for c in range(NCH):
                P.op("dve", lambda e, c=c, xt=xt, yT=yT, rstd=rstd: e.scalar_tensor_tensor(
                    yT[:, c, :], xt[:, c, :], self.final_g[:, c:c + 1], rstd[:], ALU.mult, ALU.mult),
                    [xt, rstd, self.final_g], [yT])
            yo = yo_ring.next()
            for s in range(4):
                for h in range(2):
                    ps = self.psum.next()
                    for cc in range(4):
                        c = h * 4 + cc
                        P.op("pe", lambda e, ps=ps, s=s, c=c, cc=cc, yT=yT: e.transpose(
                            ps[:, cc * 128:(cc + 1) * 128], yT[:, c, s * 128:(s + 1) * 128], self.ident[:]),
                            [yT, self.ident], [ps])
                    if (s + h) % 2 == 0:
                        P.op("act", lambda e, ps=ps, s=s, h=h, yo=yo: e.copy(yo[:, s, h * 512:(h + 1) * 512], ps[:]),
                             [ps], [yo])
                    else:
                        P.op("dve", lambda e, ps=ps, s=s, h=h, yo=yo: e.tensor_copy(yo[:, s, h * 512:(h + 1) * 512], ps[:]),
                             [ps], [yo])
            dst = y[t0:t0 + TT, :].rearrange("(s p) d -> p s d", p=128)
            tk = Tok(f"yout{ti}")
            out_toks.append(tk)
            P.dma("sp", lambda e, yo=yo, dst=dst: e.dma_start(out=dst, in_=yo[:]), [yo], [tk], yo)
        P.op("sp", None, reads=out_toks + self.dbg_toks, writes=[])
        P.end_scope()

    def phase_ffn(self, l):
        P, T = self.P, self.T
        ff_w_up, ff_conv_w, ff_conv_b, ff_w_down = (self.L(k, l) for k in ("ff_w_up", "ff_conv_w", "ff_conv_b", "ff_w_down"))
        n = TF
        cw = [self.load_cols(f"cw{l}_{j}", ff_conv_w[j, :, :], 44) for j in range(3)]
        cb = self.load_cols(f"cb{l}", ff_conv_b[:, :], 44)
        P.begin_scope()
        self.work(n)
        wup = P.sb("wup", [128, NCH, 2 * FF], BF16)
        wdn = P.sb("wdn", [128, NFC, D], BF16)
        hts = P.ring("hth", [128, NCH, n + 2], BF16, 2)
        ybufs = P.ring("ybuf", [128, n], F32, 8)
        sgs = P.ring("sgb", [128, n], F32, 3)
        actT = P.sb("actT", [128, NFC, n], BF16)
        for h in range(4):
            src = ff_w_up[:, h * 1408:(h + 1) * 1408].rearrange("(k p) n -> p k n", p=128)
            P.dma("pool", lambda e, src=src, h=h: e.dma_start(out=wup[:, :, h * 1408:(h + 1) * 1408], in_=src),
                  [], [wup], wup)
        for h in range(2):
            src = ff_w_down[h * 1408:(h + 1) * 1408, :].rearrange("(j p) n -> p j n", p=128)
            P.dma("pool", lambda e, src=src, h=h: e.dma_start(out=wdn[:, h * 11:(h + 1) * 11, :], in_=src),
                  [], [wdn], wdn)
        G2, SH2, GA2 = self.G2[l], self.SH2[l], self.GA2[l]
        def prep(ti, prev):
            xt = self.load_xt(ti * n, n)
            ht = hts.next()
            if prev is None:
                P.op("dve", lambda e, ht=ht: e.memset(ht[:, :, 0:2], 0.0), [], [ht])
            else:
                P.op("dve", lambda e, ht=ht, prev=prev: e.tensor_copy(ht[:, :, 0:2], prev[:, :, n:n + 2]), [prev], [ht])
            self.norm_mod(xt, n, G2, SH2, ht=ht, c0=2)
            return xt, ht

        nxt_prep = prep(0, None)
        for ti in range(T // n):
            t0 = ti * n
            xt, ht = nxt_prep
            pend_g = None

            def gate(yg, yv, j):
                sg = sgs.next()
                P.op("act", lambda e: e.activation(sg[:], yg[:], AF.Silu), [yg], [sg])
                P.op("pool", lambda e: e.tensor_tensor(actT[:, j, :], sg[:], yv[:], ALU.mult), [sg, yv], [actT])
            for j in range(NFC):
                ys = []
                for half in range(2):
                    ch = half * NFC + j
                    ps = self.psum.next()
                    for k in range(NCH):
                        P.op("pe", lambda e, ps=ps, k=k, ch=ch, ht=ht: e.matmul(
                            ps[:, 0:n + 2], wup[:, k, ch * 128:(ch + 1) * 128], ht[:, k, :],
                            start=(k == 0), stop=(k == NCH - 1)), [wup, ht], [ps])
                    yb = ybufs.next()
                    P.op("act", lambda e, yb=yb, ps=ps, ch=ch: e.activation(
                        yb[:], ps[:, 2:n + 2], AF.Identity, scale=cw[2][:, ch:ch + 1], bias=cb[:, ch:ch + 1]),
                        [ps, cw[2], cb], [yb])
                    P.op("dve", lambda e, yb=yb, ps=ps, ch=ch: e.scalar_tensor_tensor(
                        yb[:], ps[:, 1:n + 1], cw[1][:, ch:ch + 1], yb[:], ALU.mult, ALU.add), [ps, yb, cw[1]], [yb])
                    P.op("dve", lambda e, yb=yb, ps=ps, ch=ch: e.scalar_tensor_tensor(
                        yb[:], ps[:, 0:n], cw[0][:, ch:ch + 1], yb[:], ALU.mult, ALU.add), [ps, yb, cw[0]], [yb])
                    ys.append(yb)
                if pend_g is not None:
                    gate(*pend_g)
                pend_g = (ys[0], ys[1], j)
            gate(*pend_g)
            pend_g = None
            if ti + 1 < T // n:
                nxt_prep = prep(ti + 1, ht)
            for c in range(NCH):
                ps = self.psum.next()
                for j in range(NFC):
                    P.op("pe", lambda e, ps=ps, j=j, c=c: e.matmul(
                        ps[:, 0:n], wdn[:, j, c * 128:(c + 1) * 128], actT[:, j, :],
                        start=(j == 0), stop=(j == NFC - 1)), [wdn, actT], [ps])
                P.op("dve", lambda e, ps=ps, c=c, xt=xt: e.scalar_tensor_tensor(
                    xt[:, c, 0:n], ps[:, 0:n], GA2[:, c:c + 1], xt[:, c, 0:n], ALU.mult, ALU.add),
                    [ps, xt, GA2], [xt])
            self.store_xt(xt, t0, n)
        P.end_scope()

    def pipeline(self, items, stages):
        ns = len(stages)
        for step in range(len(items) + ns - 1):
            for s in reversed(range(ns)):
                i = step - s
                if 0 <= i < len(items):
                    stages[s](items[i])

    def phase_gm(self, l, j):
        P, T = self.P, self.T
        gm_w_in, gm_ln_g, gm_ln_b, gm_w_s, gm_b_s, gm_w_out = (self.L(k, j) for k in ("gm_w_in", "gm_ln_g", "gm_ln_b", "gm_w_s", "gm_b_s", "gm_w_out"))
        k_sel = self.din("k_sel", [8, 1024])
        n = TT
        lng = self.load_cols(f"lng{l}", gm_ln_g[:, :], 16)
        lnb = self.load_cols(f"lnb{l}", gm_ln_b[:, :], 16)
        P.begin_scope()
        self.work(n, nxt=1, sq=False, nfw=2, nrstd=1)
        self.gw = P.ring("gw", [128, n], F32, 1)
        xs_r = P.ring("gxs", [128, n], F32, 4)
        t_r = P.ring("gt", [128, n], F32, 5)
        win = P.sb("win", [128, NCH, 4096], BF16)
        wout = P.sb("wout", [128, 16, D], BF16)
        wsT = P.sb("wsT", [128, 8, 128], BF16)
        CT = P.sb("CT", [128, 16, 128], F32)
        uT = P.sb("uT", [128, 16, n], BF16)
        self.sq_ring = Ring([uT])
        vg = P.sb("vg", [128, GM_INNER], F32)
        vhat = P.sb("vhat", [128, 4, GM_INNER], BF16)
        stats = P.sb("stats", [128, 4, 6], F32)
        mv = P.sb("mv", [128, 2], F32)
        for h in range(4):
            src = gm_w_in[:, h * 1024:(h + 1) * 1024].rearrange("(k p) n -> p k n", p=128)
            P.dma("pool", lambda e, src=src, h=h: e.dma_start(out=win[:, :, h * 1024:(h + 1) * 1024], in_=src),
                  [], [win], win)
        src = gm_w_out[:, :].rearrange("(f p) n -> p f n", p=128)
        P.dma("pool", lambda e, src=src: e.dma_start(out=wout[:], in_=src), [], [wout], wout)
        wsl_r = P.ring("wsl", [128, 128], F32, 2)
        bsl = P.sb("bsl", [8, 128], F32)
        sel_r = P.ring("sel", [8, 128], F32, 2)
        P.dma("sp", lambda e: e.dma_start(out=bsl[:], in_=gm_b_s[:, :]), [], [bsl], bsl)
        bsb = P.sb("bsb", [128, 128], F32)
        for g in range(8):
            wsl = wsl_r.next()
            sel = sel_r.next()
            P.dma("sp", lambda e, wsl=wsl, g=g: e.dma_start(out=wsl[:], in_=gm_w_s[g, :, :]), [], [wsl], wsl)
            P.dma("sp", lambda e, sel=sel, g=g: e.dma_start(out=sel[:], in_=k_sel[:, g * 128:(g + 1) * 128]), [], [sel], sel)
            ps = self.psum.next()
            P.op("pe", lambda e, ps=ps, wsl=wsl: e.transpose(ps[:, 0:128], wsl[:], self.ident[:]),
                 [wsl, self.ident], [ps])
            P.op("dve", lambda e, ps=ps, g=g: e.tensor_tensor(wsT[:, g, :], ps[:, 0:128], self.tri_f[:], ALU.mult),
                 [ps, self.tri_f], [wsT])
            ps2 = self.psum.next()
            P.op("pe", lambda e, ps2=ps2, g=g: e.matmul(ps2[:, 0:128], self.ones_b[:], wsT[:, g, :], start=True, stop=True),
                 [wsT, self.ones_b], [ps2])
            P.op("pe", lambda e, ps2=ps2, sel=sel: e.matmul(ps2[:, 128:256], sel[:, :], bsl[:, :],
                                                            start=True, stop=True), [sel, bsl], [ps2])
            P.op("act", lambda e, ps2=ps2: e.copy(bsb[:], ps2[:, 128:256]), [ps2], [bsb])
            for q in range(2):
                fc = 2 * g + q
                P.op("dve", lambda e, ps2=ps2, fc=fc: e.scalar_tensor_tensor(
                    CT[:, fc, :], ps2[:, 0:128], lnb[:, fc:fc + 1], bsb[:], ALU.mult, ALU.add),
                    [ps2, lnb, bsb], [CT])
        G1, SH1, GA1 = self.G1[l], self.SH1[l], self.GA1[l]
        for ti in range(T // n):
            t0 = ti * n
            xt = self.load_xt(t0, n)
            ht = self.norm_mod(xt, n, G1, SH1)
            items = [("u", fc, 0) for fc in range(16)] + [("v", s_, nb) for s_ in range(4) for nb in range(4)]
            st = {}

            def s_mm(it, ht=ht):
                kind, a_, b_ = it
                ps = self.psum.next()
                st[it] = {"ps": ps}
                for k in range(NCH):
                    if kind == "u":
                        P.op("pe", lambda e, ps=ps, k=k, fc=a_: e.matmul(
                            ps[:, :], win[:, k, fc * 128:(fc + 1) * 128], ht[:, k, :],
                            start=(k == 0), stop=(k == NCH - 1)), [win, ht], [ps])
                    else:
                        P.op("pe", lambda e, ps=ps, k=k, s_=a_, nb=b_: e.matmul(
                            ps[:, :], ht[:, k, s_ * 128:(s_ + 1) * 128], win[:, k, 2048 + nb * 512:2048 + (nb + 1) * 512],
                            start=(k == 0), stop=(k == NCH - 1)), [win, ht], [ps])

            def s_copy(it):
                d = st[it]
                d["xs"] = xs_r.next()
                d["t"] = t_r.next()
                P.op("act", lambda e, d=d: e.activation(d["xs"][:], d["ps"][:, :], AF.Identity), [d["ps"]], [d["xs"]])
                P.op("act", lambda e, d=d: e.activation(d["t"][:], d["ps"][:, :], AF.Square, scale=0.21145921592590375),
                     [d["ps"]], [d["t"]])

            def s_poly(it):
                d = st[it]
                P.op("dve", lambda e, d=d: e.scalar_tensor_tensor(
                    d["t"][:], d["t"][:], 1.0, d["xs"][:], ALU.add, ALU.mult), [d["t"], d["xs"]], [d["t"]])

            def s_sig(it):
                d = st[it]
                P.op("act", lambda e, d=d: e.activation(d["t"][:], d["t"][:], AF.Sigmoid, scale=1.5957691216057308),
                     [d["t"]], [d["t"]])

            def s_out(it):
                kind, a_, b_ = it
                d = st[it]
                if kind == "u":
                    P.op("dve", lambda e, d=d, fc=a_: e.tensor_tensor(uT[:, fc, :], d["t"][:], d["xs"][:], ALU.mult),
                         [d["t"], d["xs"]], [uT])
                else:
                    P.op("dve", lambda e, d=d, nb=b_: e.tensor_tensor(
                        vg[:, nb * 512:(nb + 1) * 512], d["t"][:], d["xs"][:], ALU.mult), [d["t"], d["xs"]], [vg])

            def s_ln(it):
                kind, s_, nb = it
                if kind != "v":
                    return
                P.op("dve", lambda e, nb=nb: e.bn_stats(stats[:, nb, :], vg[:, nb * 512:(nb + 1) * 512]), [vg], [stats])
                if nb == 3:
                    P.op("dve", lambda e: e.bn_aggr(mv[:], stats[:].rearrange("p a b -> p (a b)")), [stats], [mv])
                    P.op("act", lambda e: e.activation(mv[:, 1:2], mv[:, 1:2], AF.Sqrt, bias=EPS), [mv], [mv])
                    P.op("dve", lambda e: e.reciprocal(mv[:, 1:2], mv[:, 1:2]), [mv], [mv])
                    P.op("dve", lambda e: e.scalar_tensor_tensor(mv[:, 0:1], mv[:, 0:1], -1.0, mv[:, 1:2], ALU.mult, ALU.mult),
                         [mv], [mv])
                    P.op("act", lambda e, s_=s_: e.activation(vhat[:, s_, :], vg[:], AF.Identity, scale=mv[:, 1:2], bias=mv[:, 0:1]),
                         [vg, mv], [vhat])

            self.pipeline(items, [s_mm, s_copy, s_poly, s_sig, s_out, s_ln])
            for fc in range(16):
                ps = self.psum.next()
                for s in range(4):
                    P.op("pe", lambda e, ps=ps, s=s, fc=fc: e.matmul(
                        ps[:, s * 128:(s + 1) * 128], vhat[:, s, fc * 128:(fc + 1) * 128], wsT[:, fc // 2, :],
                        start=True, stop=True), [vhat, wsT], [ps])
                tmp = self.gw.next()
                P.op("dve", lambda e, ps=ps, fc=fc, tmp=tmp: e.scalar_tensor_tensor(
                    tmp[:].rearrange("p (s t) -> p s t", s=4), ps[:, :].rearrange("p (s t) -> p s t", s=4),
                    lng[:, fc:fc + 1], CT[:, fc:fc + 1, :].to_broadcast([128, 4, 128]),
                    ALU.mult, ALU.add), [ps, lng, CT], [tmp])
                P.op("dve", lambda e, fc=fc, tmp=tmp: e.tensor_tensor(uT[:, fc, :], uT[:, fc, :], tmp[:], ALU.mult),
                     [uT, tmp], [uT])
            for c in range(NCH):
                ps = self.psum.next()
                for fc in range(16):
                    P.op("pe", lambda e, ps=ps, fc=fc, c=c: e.matmul(
                        ps[:, :], wout[:, fc, c * 128:(c + 1) * 128], uT[:, fc, :],
                        start=(fc == 0), stop=(fc == 15)), [wout, uT], [ps])
                P.op("dve", lambda e, ps=ps, c=c, xt=xt: e.scalar_tensor_tensor(
                    xt[:, c, :], ps[:, :], GA1[:, c:c + 1], xt[:, c, :], ALU.mult, ALU.add),
                    [ps, xt, GA1], [xt])
            self.store_xt(xt, t0, n)
        P.end_scope()

    def rope_tables(self, cos2, sin2):
        P, T = self.P, self.T
        pos = self.din("positions", [1, T], I32)
        k_invf = self.din("k_invf", [2, 128])
        posi = P.sb("posi", [1, T], I32)
        posf = P.sb("posf", [1, T], F32)
        invf = P.sb("invf", [1, 128], F32)
        sgn = self.load_cols("ropesgn", k_invf[:, :], 2)
        P.dma("sp", lambda e: e.dma_start(out=posi[:], in_=pos[:, :]), [], [posi], posi)
        P.dma("sp", lambda e: e.dma_start(out=invf[:], in_=k_invf[0:1, :]), [], [invf], invf)
        P.op("dve", lambda e: e.tensor_copy(posf[:], posi[:]), [posi], [posf])
        ki = P.sb("ropek", [128, 512], I32)
        kf = P.sb("ropekf", [128, 512], F32)
        r = P.sb("roper", [128, 512], F32)
        m = P.sb("ropem", [128, 512], F32)
        TWO_PI = 6.283185307179586
        for b in range(T // 512):
            ps = self.psum.next()
            P.op("pe", lambda e, ps=ps, b=b: e.matmul(ps[:, :], invf[:, :], posf[:, b * 512:(b + 1) * 512],
                                                      start=True, stop=True), [invf, posf], [ps])
            for which, dst in ((0, sin2), (1, cos2)):
                P.op("dve", lambda e, ps=ps, which=which: e.tensor_scalar(
                    r[:], ps[:, :], 1.0 / TWO_PI, 0.25 * which, ALU.mult, ALU.add), [ps], [r])
                P.op("dve", lambda e: e.tensor_copy(ki[:], r[:]), [r], [ki])
                P.op("dve", lambda e: e.tensor_copy(kf[:], ki[:]), [ki], [kf])
                P.op("dve", lambda e: e.tensor_tensor(r[:], r[:], kf[:], ALU.subtract), [r, kf], [r])
                P.op("dve", lambda e: e.tensor_scalar(m[:], r[:], 0.5, None, ALU.is_gt), [r], [m])
                P.op("dve", lambda e: e.tensor_tensor(r[:], r[:], m[:], ALU.subtract), [r, m], [r])
                P.op("dve", lambda e: e.tensor_scalar(m[:], r[:], -0.5, None, ALU.is_lt), [r], [m])
                P.op("dve", lambda e: e.tensor_tensor(r[:], r[:], m[:], ALU.add), [r, m], [r])
                if which == 0:
                    P.op("act", lambda e, b=b: e.activation(m[:], r[:], AF.Sin, scale=6.28318), [r], [m])
                    P.op("dve", lambda e, b=b, dst=dst: e.tensor_scalar(
                        dst[:, b * 512:(b + 1) * 512], m[:], sgn[:, 1:2], None, ALU.mult), [m, sgn], [dst])
                else:
                    P.op("act", lambda e, b=b, dst=dst: e.activation(
                        dst[:, b * 512:(b + 1) * 512], r[:], AF.Sin, scale=6.28318), [r], [dst])

    def phase_mla(self, l):
        P, T = self.P, self.T
        nc = self.nc
        n = TT
        NTL = T // n
        w_a, qg_d, kvg_d, w_qb, w_kvb, w_o = (self.L(k, 0) for k in (
            "mla_w_a", "mla_q_norm_g", "mla_kv_norm_g", "mla_w_qb", "mla_w_kvb", "mla_w_o"))
        qg = self.load_cols("mlaqg", qg_d[:, :], 2)
        kvg = self.load_cols("mlakvg", kvg_d[:, :], 1)
        qn_d = nc.dram_tensor("qn_d", [8, 128, T], BF16, kind="Internal").ap()
        qr_d = nc.dram_tensor("qr_d", [4, 128, T], BF16, kind="Internal").ap()
        kn_d = nc.dram_tensor("kn_d", [8, 128, T], BF16, kind="Internal").ap()
        kr_d = nc.dram_tensor("kr_d", [128, T], BF16, kind="Internal").ap()
        v_d = nc.dram_tensor("v_d", [T, 1024], BF16, kind="Internal").ap()
        oT_d = nc.dram_tensor("oT_d", [8, 128, T], BF16, kind="Internal").ap()
        tk = {nm: [Tok(f"{nm}{i}") for i in range(NTL)] for nm in ("qn", "qr", "kn", "kr", "v")}
        otk = [Tok(f"oT{h}") for h in range(8)]
        G1, SH1, GA1 = self.G1[l], self.SH1[l], self.GA1[l]

        P.begin_scope()
        self.work(n)
        cos2 = P.sb("cos2", [128, T], F32)
        sin2 = P.sb("sin2", [128, T], F32)
        self.rope_tables(cos2, sin2)
        wa = P.sb("wa", [128, NCH, 384], BF16)
        wkr = P.sb("wkr", [128, NCH, 256], BF16)
        wqn = P.sb("wqn", [128, 2, 8, 128], BF16)
        wqrA = P.sb("wqrA", [128, 2, 8, 64], BF16)
        wqrB = P.sb("wqrB", [128, 2, 8, 64], BF16)
        wkn = P.sb("wkn", [128, 8, 128], BF16)
        wv = P.sb("wv", [128, 8, 128], BF16)
        wa_v = w_a.rearrange("(k p) n -> p k n", p=128)
        P.dma("pool", lambda e: e.dma_start(out=wa[:], in_=wa_v[:, :, 0:384]), [], [wa], wa)

        def wkr_load(e):
            r = []
            for dup in range(2):
                r.append(e.dma_start(out=wkr[:, :, dup * 64:dup * 64 + 64], in_=wa_v[:, :, 384:448]))
                r.append(e.dma_start(out=wkr[:, :, 128 + dup * 64:128 + dup * 64 + 32], in_=wa_v[:, :, 416:448]))
                r.append(e.dma_start(out=wkr[:, :, 160 + dup * 64:160 + dup * 64 + 32], in_=wa_v[:, :, 384:416]))
            return r
        P.dma("pool", wkr_load, [], [wkr], wkr)
        wq_v = w_qb.rearrange("(k p) (h e) -> p k h e", p=128, e=192)
        P.dma("pool", lambda e: [e.dma_start(out=wqn[:, kc, :, :], in_=wq_v[:, kc, :, 0:128]) for kc in range(2)],
              [], [wqn], wqn)
        P.dma("pool", lambda e: [e.dma_start(out=wqrA[:, kc, :, :], in_=wq_v[:, kc, :, 128:192]) for kc in range(2)],
              [], [wqrA], wqrA)
        P.dma("pool", lambda e: [e.dma_start(out=wqrB[:, kc, :, 0:32], in_=wq_v[:, kc, :, 160:192]) for kc in range(2)] +
                                [e.dma_start(out=wqrB[:, kc, :, 32:64], in_=wq_v[:, kc, :, 128:160]) for kc in range(2)],
              [], [wqrB], wqrB)
        wkv_v = w_kvb.rearrange("p (h two e) -> p two h e", two=2, e=128)
        P.dma("pool", lambda e: e.dma_start(out=wkn[:], in_=wkv_v[:, 0, :, :]), [], [wkn], wkn)
        P.dma("pool", lambda e: e.dma_start(out=wv[:], in_=wkv_v[:, 1, :, :]), [], [wv], wv)
        cqn = P.sb("cqn", [128, 2, n], BF16)
        ckvn = P.sb("ckvn", [128, n], BF16)
        sqs = P.ring("sqs", [128, n], BF16, 2)
        qn_t = P.sb("qn_t", [128, 8, n], BF16)
        qr_t = P.sb("qr_t", [128, 4, n], BF16)
        kn_t = P.sb("kn_t", [128, 8, n], BF16)
        kr_t = P.sb("kr_t", [128, n], BF16)
        v_t = P.sb("v_t", [128, 4, 1024], BF16)
        rt = P.ring("rt", [128, n], F32, 3)

        def small_rstd(pss, dim):
            ps_s = self.psum.next()
            for i, pz in enumerate(pss):
                sq = sqs.next()
                P.op("act", lambda e, sq=sq, pz=pz: e.activation(sq[:], pz[:, :], AF.Square), [pz], [sq])
                P.op("pe", lambda e, sq=sq, i=i: e.matmul(ps_s[:, :], self.ones_b[:], sq[:],
                                                          start=(i == 0), stop=(i == len(pss) - 1)),
                     [sq, self.ones_b], [ps_s])
            tmp = self.fw.next()
            P.op("act", lambda e: e.activation(tmp[:], ps_s[:, :], AF.Sqrt, scale=1.0 / dim, bias=EPS), [ps_s], [tmp])
            rs = self.rstd_ring.next()
            P.op("dve", lambda e: e.reciprocal(rs[:], tmp[:]), [tmp], [rs])
            return rs

        def rotate(psA, psB, dst_fn, writes, t0):
            ta, tb = rt.next(), rt.next()
            P.op("dve", lambda e: e.tensor_tensor(ta[:], psA[:, :], cos2[:, t0:t0 + n], ALU.mult), [psA, cos2], [ta])
            P.op("dve", lambda e: e.tensor_tensor(tb[:], psB[:, :], sin2[:, t0:t0 + n], ALU.mult), [psB, sin2], [tb])
            P.op("pool", lambda e: e.tensor_tensor(dst_fn(), ta[:], tb[:], ALU.add), [ta, tb], writes)

        for ti in range(NTL):
            t0 = ti * n
            xt = self.load_xt(t0, n)
            ht = self.norm_mod(xt, n, G1, SH1)
            pcq = []
            for c in range(3):
                ps = self.psum.next()
                for k in range(NCH):
                    P.op("pe", lambda e, ps=ps, k=k, c=c, ht=ht: e.matmul(
                        ps[:, :], wa[:, k, c * 128:(c + 1) * 128], ht[:, k, :],
                        start=(k == 0), stop=(k == NCH - 1)), [wa, ht], [ps])
                pcq.append(ps)
            rs_q = small_rstd(pcq[0:2], 256)
            for c in range(2):
                P.op("dve", lambda e, c=c, rs_q=rs_q, ps=pcq[c]: e.scalar_tensor_tensor(
                    cqn[:, c, :], ps[:, :], qg[:, c:c + 1], rs_q[:], ALU.mult, ALU.mult), [pcq[c], qg, rs_q], [cqn])
            rs_kv = small_rstd(pcq[2:3], 128)
            P.op("dve", lambda e, rs_kv=rs_kv, ps=pcq[2]: e.scalar_tensor_tensor(
                ckvn[:], ps[:, :], kvg[:, 0:1], rs_kv[:], ALU.mult, ALU.mult), [pcq[2], kvg, rs_kv], [ckvn])
            pab = []
            for ab in range(2):
                ps = self.psum.next()
                for k in range(NCH):
                    P.op("pe", lambda e, ps=ps, k=k, ab=ab, ht=ht: e.matmul(
                        ps[:, :], wkr[:, k, ab * 128:(ab + 1) * 128], ht[:, k, :],
                        start=(k == 0), stop=(k == NCH - 1)), [wkr, ht], [ps])
                pab.append(ps)
            rotate(pab[0], pab[1], lambda: kr_t[:], [kr_t], t0)
            P.dma("sp", lambda e, t0=t0: e.dma_start(out=kr_d[:, t0:t0 + n], in_=kr_t[:]), [kr_t], [tk["kr"][ti]], kr_t)
            for h in range(8):
                ps = self.psum.next()
                for kc in range(2):
                    P.op("pe", lambda e, ps=ps, kc=kc, h=h: e.matmul(
                        ps[:, :], wqn[:, kc, h, :], cqn[:, kc, :], start=(kc == 0), stop=(kc == 1)), [wqn, cqn], [ps])
                if h % 2 == 0:
                    P.op("act", lambda e, ps=ps, h=h: e.copy(qn_t[:, h, :], ps[:, :]), [ps], [qn_t])
                else:
                    P.op("dve", lambda e, ps=ps, h=h: e.tensor_copy(qn_t[:, h, :], ps[:, :]), [ps], [qn_t])
            P.dma("sp", lambda e, t0=t0: e.dma_start(out=qn_d[:, :, t0:t0 + n].rearrange("h p t -> p h t"), in_=qn_t[:]),
                  [qn_t], [tk["qn"][ti]], qn_t)
            for pr in range(4):
                pab = []
                for W in (wqrA, wqrB):
                    ps = self.psum.next()
                    for kc in range(2):
                        P.op("pe", lambda e, ps=ps, kc=kc, pr=pr, W=W: e.matmul(
                            ps[:, :], W[:, kc, 2 * pr:2 * pr + 2, :].rearrange("p h e -> p (h e)"), cqn[:, kc, :],
                            start=(kc == 0), stop=(kc == 1)), [W, cqn], [ps])
                    pab.append(ps)
                rotate(pab[0], pab[1], lambda pr=pr: qr_t[:, pr, :], [qr_t], t0)
            P.dma("sp", lambda e, t0=t0: e.dma_start(out=qr_d[:, :, t0:t0 + n].rearrange("h p t -> p h t"), in_=qr_t[:]),
                  [qr_t], [tk["qr"][ti]], qr_t)
            for h in range(8):
                ps = self.psum.next()
                P.op("pe", lambda e, ps=ps, h=h: e.matmul(ps[:, :], wkn[:, h, :], ckvn[:], start=True, stop=True),
                     [wkn, ckvn], [ps])
                if h % 2 == 0:
                    P.op("act", lambda e, ps=ps, h=h: e.copy(kn_t[:, h, :], ps[:, :]), [ps], [kn_t])
                else:
                    P.op("dve", lambda e, ps=ps, h=h: e.tensor_copy(kn_t[:, h, :], ps[:, :]), [ps], [kn_t])
            P.dma("sp", lambda e, t0=t0: e.dma_start(out=kn_d[:, :, t0:t0 + n].rearrange("h p t -> p h t"), in_=kn_t[:]),
                  [kn_t], [tk["kn"][ti]], kn_t)
            for s in range(4):
                for hh in range(2):
                    ps = self.psum.next()
                    P.op("pe", lambda e, ps=ps, s=s, hh=hh: e.matmul(
                        ps[:, :], ckvn[:, s * 128:(s + 1) * 128],
                        wv[:, 4 * hh:4 * hh + 4, :].rearrange("p h e -> p (h e)"), start=True, stop=True),
                        [wv, ckvn], [ps])
                    if hh == 0:
                        P.op("act", lambda e, ps=ps, s=s: e.copy(v_t[:, s, 0:512], ps[:, :]), [ps], [v_t])
                    else:
                        P.op("dve", lambda e, ps=ps, s=s: e.tensor_copy(v_t[:, s, 512:1024], ps[:, :]), [ps], [v_t])
            P.dma("sp", lambda e, t0=t0: e.dma_start(out=v_d[t0:t0 + n, :].rearrange("(s p) f -> p s f", p=128), in_=v_t[:]),
                  [v_t], [tk["v"][ti]], v_t)
        P.end_scope()

        P.begin_scope()
        NKB = T // 128
        qn_r = P.ring("qn", [128, T], BF16, 2)
        kn_r = P.ring("kn", [128, T], BF16, 2)
        qr_r = P.ring("qr", [128, T], BF16, 2)
        v_r = P.ring("v", [128, NKB, 128], BF16, 2)
        ob_r = P.ring("ob", [128, T], BF16, 2)
        kr2 = P.sb("kr2", [128, T], BF16)
        tri_b = P.sb("tri_b", [128, 128], BF16)
        P.op("dve", lambda e: e.tensor_copy(tri_b[:], self.tri_f[:]), [self.tri_f], [tri_b])
        pt_r = P.ring("pt", [128, 512], BF16, 4)
        rc_r = P.ring("rc", [128, 512], F32, 2)
        acc_r = Ring([(self.psum.bufs[0], self.psum.bufs[1]), (self.psum.bufs[2], self.psum.bufs[3])])
        s_r = Ring(self.psum.bufs[4:7])
        P.dma("sp", lambda e: e.dma_start(out=kr2[:], in_=kr_d[:, :]), tk["kr"], [kr2], kr2)
        SCALE = 192.0 ** -0.5
        qr = None
        for h in range(8):
            qn, kn, v, ob = qn_r.next(), kn_r.next(), v_r.next(), ob_r.next()
            P.dma("sp", lambda e, qn=qn, h=h: e.dma_start(out=qn[:], in_=qn_d[h, :, :]), tk["qn"], [qn], qn)
            P.dma("sp", lambda e, kn=kn, h=h: e.dma_start(out=kn[:], in_=kn_d[h, :, :]), tk["kn"], [kn], kn)
            P.dma("sp", lambda e, v=v, h=h: [e.dma_start(
                out=v[:, q * (NKB // 4):(q + 1) * (NKB // 4), :],
                in_=v_d[q * (T // 4):(q + 1) * (T // 4), h * 128:(h + 1) * 128].rearrange("(n p) e -> p n e", p=128))
                for q in range(4)], tk["v"], [v], v)
            if h % 2 == 0:
                qr = qr_r.next()
                P.dma("sp", lambda e, qr=qr, h=h: e.dma_start(out=qr[:], in_=qr_d[h // 2, :, :]), tk["qr"], [qr], qr)
            r0 = 64 * (h % 2)
            for qi in range(T // 512):
                O_ps, R_ps = acc_r.next()
                nkb = 4 * (qi + 1)
                q0 = qi * 512
                pend = None

                def flush(pend):
                    kb, c0, pt = pend
                    P.op("pe", lambda e, O_ps=O_ps, kb=kb, c0=c0, pt=pt, v=v, nkb=nkb: e.matmul(
                        O_ps[:, c0:512], v[:, kb, :], pt[:, c0:512], start=(kb == 0), stop=(kb == nkb - 1)),
                        [v, pt], [O_ps])
                    P.op("pe", lambda e, R_ps=R_ps, kb=kb, c0=c0, pt=pt, nkb=nkb: e.matmul(
                        R_ps[:, c0:512], self.ones_b[:], pt[:, c0:512], start=(kb == 0), stop=(kb == nkb - 1)),
                        [self.ones_b, pt], [R_ps])

                for kb in range(nkb):
                    j = kb - 4 * qi
                    c0 = max(0, j) * 128
                    S_ps = s_r.next()
                    P.op("pe", lambda e, S_ps=S_ps, kb=kb, c0=c0, q0=q0, kn=kn, qn=qn: e.matmul(
                        S_ps[:, c0:512], kn[:, kb * 128:(kb + 1) * 128], qn[:, q0 + c0:q0 + 512],
                        start=True, stop=False), [kn, qn], [S_ps])
                    P.op("pe", lambda e, S_ps=S_ps, kb=kb, c0=c0, q0=q0, qr=qr, r0=r0: e.matmul(
                        S_ps[:, c0:512], kr2[r0:r0 + 64, kb * 128:(kb + 1) * 128], qr[r0:r0 + 64, q0 + c0:q0 + 512],
                        start=False, stop=True), [kr2, qr], [S_ps])
                    if pend is not None:
                        flush(pend)
                    pt = pt_r.next()
                    P.op("act", lambda e, S_ps=S_ps, pt=pt, c0=c0: e.activation(
                        pt[:, c0:512], S_ps[:, c0:512], AF.Exp, scale=SCALE), [S_ps], [pt])
                    if j >= 0:
                        P.op("pool", lambda e, pt=pt, c0=c0: e.tensor_tensor(
                            pt[:, c0:c0 + 128], pt[:, c0:c0 + 128], tri_b[:], ALU.mult), [pt, tri_b], [pt])
                    pend = (kb, c0, pt)
                flush(pend)
                rc = rc_r.next()
                P.op("dve", lambda e, rc=rc, R_ps=R_ps: e.reciprocal(rc[:], R_ps[:, :]), [R_ps], [rc])
                P.op("dve", lambda e, rc=rc, O_ps=O_ps, ob=ob, q0=q0: e.tensor_tensor(
                    ob[:, q0:q0 + 512], O_ps[:, :], rc[:], ALU.mult), [O_ps, rc], [ob])
            P.dma("sp", lambda e, ob=ob, h=h: e.dma_start(out=oT_d[h, :, :], in_=ob[:]), [ob], [otk[h]], ob)
        P.end_scope()

        P.begin_scope()
        self.work(n)
        wo = P.sb("wo", [128, 8, D], BF16)
        P.dma("pool", lambda e: e.dma_start(out=wo[:], in_=w_o.rearrange("(h p) n -> p h n", p=128)), [], [wo], wo)
        ot_r = P.ring("ot", [128, 8, n], BF16, 2)
        for ti in range(NTL):
            t0 = ti * n
            xt = self.load_xt(t0, n)
            ot = ot_r.next()
            P.dma("sp", lambda e, ot=ot, t0=t0: e.dma_start(out=ot[:], in_=oT_d[:, :, t0:t0 + n].rearrange("h p t -> p h t")),
                  otk, [ot], ot)
            for c in range(NCH):
                ps = self.psum.next()
                for h in range(8):
                    P.op("pe", lambda e, ps=ps, h=h, c=c, ot=ot: e.matmul(
                        ps[:, :], wo[:, h, c * 128:(c + 1) * 128], ot[:, h, :], start=(h == 0), stop=(h == 7)),
                        [wo, ot], [ps])
                P.op("dve", lambda e, ps=ps, c=c, xt=xt: e.scalar_tensor_tensor(
                    xt[:, c, :], ps[:, :], GA1[:, c:c + 1], xt[:, c, :], ALU.mult, ALU.add), [ps, xt, GA1], [xt])
            self.store_xt(xt, t0, n)
        P.end_scope()

    def phase_hg(self, l):
        P, T = self.P, self.T
        n = 256
        NS = n // 128
        NC = n // 64
        w_in, ng_d, w_o = (self.L(k, 0) for k in ("hg_w_in", "hg_norm_g", "hg_w_o"))
        hg_lb = self.din("hg_lb", [DEPTH, NCH, 128])
        k_mask2 = self.din("k_mask2", [128, 128])
        ng = self.load_cols("hgng", ng_d[:, :], 1)
        lbc = [self.load_cols(f"hglb{d}", hg_lb[d, :, :], NCH) for d in range(DEPTH)]
        lb = P.sb("hg_lbv", [128, NCH], F32)
        oml = P.sb("hg_oml", [128, NCH], F32)
        esum = P.sb("hg_esum", [128, NCH], F32)
        enum_ = P.sb("hg_enum", [128, NCH], F32)
        for d in range(DEPTH):
            P.op("act", lambda e, d=d: e.activation(lbc[d][:], lbc[d][:], AF.Exp), [lbc[d]], [lbc[d]])
        P.op("dve", lambda e: e.tensor_tensor(esum[:], lbc[0][:], lbc[1][:], ALU.add), [lbc[0], lbc[1]], [esum])
        P.op("dve", lambda e: e.tensor_tensor(esum[:], esum[:], lbc[2][:], ALU.add), [esum, lbc[2]], [esum])
        P.op("dve", lambda e: e.tensor_tensor(esum[:], esum[:], lbc[3][:], ALU.add), [esum, lbc[3]], [esum])
        P.op("dve", lambda e: e.memset(enum_[:], 0.0), [], [enum_])
        for d in range(1, l + 1):
            P.op("dve", lambda e, d=d: e.tensor_tensor(enum_[:], enum_[:], lbc[d][:], ALU.add), [enum_, lbc[d]], [enum_])
        P.op("dve", lambda e: e.reciprocal(esum[:], esum[:]), [esum], [esum])
        P.op("dve", lambda e: e.tensor_tensor(lb[:], enum_[:], esum[:], ALU.mult), [enum_, esum], [lb])
        P.op("dve", lambda e: e.tensor_scalar(oml[:], lb[:], -1.0, 1.0, ALU.mult, ALU.add), [lb], [oml])
        G1, SH1, GA1 = self.G1[l], self.SH1[l], self.GA1[l]

        P.begin_scope()
        self.work(n, nxt=2)
        win = P.sb("hwin", [128, NCH, 4096], BF16)
        wo = P.sb("hwo", [128, 8, D], BF16)
        for h in range(4):
            src = w_in[:, h * 1024:(h + 1) * 1024].rearrange("(k p) n -> p k n", p=128)
            P.dma("pool", lambda e, src=src, h=h: e.dma_start(out=win[:, :, h * 1024:(h + 1) * 1024], in_=src),
                  [], [win], win)
        P.dma("pool", lambda e: e.dma_start(out=wo[:], in_=w_o.rearrange("(h p) n -> p h n", p=128)), [], [wo], wo)
        mask2 = P.sb("mask2", [128, 128], F32)
        P.dma("sp", lambda e: e.dma_start(out=mask2[:], in_=k_mask2[:, :]), [], [mask2], mask2)
        ident_b = P.sb("ident_b", [128, 128], BF16)
        P.op("dve", lambda e: e.tensor_copy(ident_b[:], self.ident[:]), [self.ident], [ident_b])
        rmask = P.sb("rmask", [128, n], F32)
        P.op("dve", lambda e: e.memset(rmask[:], 1.0), [], [rmask])
        for c in range(NC):
            P.op("dve", lambda e, c=c: e.memset(rmask[:, 64 * c:64 * c + 1], 0.0), [], [rmask])
        S = P.sb("hgS", [128, 8, 128], F32)
        S_bf = [P.sb(f"hgSb{g}", [128, 4, 128], BF16) for g in range(2)]
        P.op("dve", lambda e: e.memset(S[:], 0.0), [], [S])
        for g in range(2):
            P.op("pool", lambda e, g=g: e.memset(S_bf[g][:], 0.0), [], [S_bf[g]])
        ebs = P.ring("hgebA", [128, NC, 8], F32, 2)
        Qt = [P.sb(f"hgQ{h}", [128, n], BF16) for h in range(8)]
        Kt = [P.sb(f"hgK{h}", [128, n], BF16) for h in range(8)]
        Ktok = [P.sb(f"hgKt{h}", [128, NS, 128], BF16) for h in range(8)]
        sgate = [P.sb(f"hgsg{h}", [128, n], BF16) for h in range(8)]
        V_t = P.sb("hgV", [128, NS, 1024], BF16)
        o_f = P.sb("hgo", [128, 8, n], F32)
        ogT = P.sb("hgog", [128, 8, n], BF16)
        tA_r = P.ring("hgtA", [128, n], F32, 6)
        tB_r = P.ring("hgtB", [128, n], F32, 8)
        tC_r = P.ring("hgtC", [128, n], F32, 6)
        tD_r = P.ring("hgtD", [128, n], F32, 5)
        tr = P.ring("hgt", [128, n], F32, 3)
        at_r = P.ring("hgAT", [128, 128], BF16, 8)
        st_r = P.ring("hgSt", [128, 512], F32, 3)
        sqh = P.ring("hgsq", [128, n], BF16, 2)

        hts = P.ring("hght", [128, NCH, n], BF16, 2)

        def prep(ti):
            xt = self.load_xt(ti * n, n)
            return xt, self.norm_mod(xt, n, G1, SH1, ht=hts.next())

        nxt_prep = prep(0)
        for ti in range(T // n):
            t0 = ti * n
            xt, ht = nxt_prep
            ebA = ebs.next()
            for s in range(NS):
                for hh in range(2):
                    ps = self.psum.next()
                    for k in range(NCH):
                        P.op("pe", lambda e, ps=ps, k=k, s=s, hh=hh, ht=ht: e.matmul(
                            ps[:, :], ht[:, k, s * 128:(s + 1) * 128], win[:, k, 2048 + hh * 512:2048 + (hh + 1) * 512],
                            start=(k == 0), stop=(k == NCH - 1)), [win, ht], [ps])
                    if hh == 0:
                        P.op("act", lambda e, ps=ps, s=s: e.copy(V_t[:, s, 0:512], ps[:, :]), [ps], [V_t])
                    else:
                        P.op("dve", lambda e, ps=ps, s=s: e.tensor_copy(V_t[:, s, 512:1024], ps[:, :]), [ps], [V_t])
            st = {}

            def proj(off, h, ht=ht):
                ps = self.psum.next()
                for k in range(NCH):
                    P.op("pe", lambda e, ps=ps, k=k: e.matmul(
                        ps[:, 0:n], win[:, k, off + h * 128:off + (h + 1) * 128], ht[:, k, :],
                        start=(k == 0), stop=(k == NCH - 1)), [win, ht], [ps])
                return ps

            def h0(h):
                st[h] = {"fl": proj(1024, h), "g": proj(3072, h)}

            def h1(h):
                d = st[h]
                d["A"], d["B"] = tA_r.next(), tB_r.next()
                P.op("act", lambda e: e.activation(d["A"][:], d["fl"][:, 0:n], AF.Exp, scale=-1.0), [d["fl"]], [d["A"]])
                P.op("act", lambda e: e.activation(d["B"][:], d["g"][:, 0:n], AF.Exp, scale=-1.0), [d["g"]], [d["B"]])
                P.op("act", lambda e: e.copy(sgate[h][:], d["g"][:, 0:n]), [d["g"]], [sgate[h]])

            def h1b(h):
                d = st[h]
                P.op("act", lambda e: e.activation(d["A"][:], d["A"][:], AF.Ln, bias=1.0), [d["A"]], [d["A"]])
                P.op("act", lambda e: e.activation(d["B"][:], d["B"][:], AF.Ln, bias=1.0), [d["B"]], [d["B"]])

            def h1c(h):
                d = st[h]
                P.op("act", lambda e: e.activation(d["A"][:], d["A"][:], AF.Exp, scale=-1.0), [d["A"]], [d["A"]])
                P.op("act", lambda e: e.activation(d["B"][:], d["B"][:], AF.Exp, scale=-1.0), [d["B"]], [d["B"]])

            def h2(h):
                d = st[h]
                A, B = d["A"], d["B"]
                P.op("dve", lambda e: e.tensor_scalar(A[:], A[:], oml[:, h:h + 1], lb[:, h:h + 1], ALU.mult, ALU.add),
                     [A, oml, lb], [A])
                P.op("dve", lambda e: e.tensor_tensor(sgate[h][:], sgate[h][:], B[:], ALU.mult), [sgate[h], B], [sgate[h]])

            def h3(h):
                d = st[h]
                d["C"] = tC_r.next()
                P.op("act", lambda e: e.activation(d["C"][:], d["A"][:], AF.Ln), [d["A"]], [d["C"]])
                P.op("pool", lambda e: e.tensor_scalar(d["B"][:], d["A"][:], -1.0, 1.0, ALU.mult, ALU.add), [d["A"]], [d["B"]])

            def h4(h):
                d = st[h]
                d["D"] = tD_r.next()
                P.op("dve", lambda e: e.tensor_tensor_scan(d["D"][:], rmask[:], d["C"][:], 0.0, ALU.mult, ALU.add),
                     [rmask, d["C"]], [d["D"]])

            def h5(h):
                d = st[h]
                P.op("act", lambda e: e.activation(d["C"][:], d["D"][:], AF.Exp), [d["D"]], [d["C"]])
                P.op("act", lambda e: e.activation(d["D"][:], d["D"][:], AF.Exp, scale=-1.0), [d["D"]], [d["D"]])

            def h6(h):
                st[h]["q"] = proj(0, h)

            def h7(h):
                d = st[h]
                P.op("dve", lambda e: e.tensor_tensor(Qt[h][:], d["q"][:, 0:n], d["C"][:], ALU.mult), [d["q"], d["C"]], [Qt[h]])
                P.op("dve", lambda e: e.tensor_copy(ebA[:, :, h], d["C"][:].rearrange("p (c t) -> p c t", t=64)[:, :, 63]),
                     [d["C"]], [ebA])
                P.op("pool", lambda e: e.tensor_tensor(Kt[h][:], d["B"][:], d["D"][:], ALU.mult), [d["B"], d["D"]], [Kt[h]])

            def h8(h):
                pb = self.psb
                hb = (h % 2) * 512
                for s in range(NS):
                    P.op("pe", lambda e, s=s: e.transpose(
                        pb[:, hb + s * 128:hb + (s + 1) * 128], Kt[h][:, s * 128:(s + 1) * 128], ident_b[:]),
                        [Kt[h], ident_b], [pb])
                P.op("act", lambda e: e.copy(Ktok[h][:].rearrange("p s d -> p (s d)"), pb[:, hb:hb + NS * 128]),
                     [pb], [Ktok[h]])

            self.pipeline(list(range(8)), [h0, h1, h1b, h1c, h2, h3, h4, h5, h6, h7, h8])
            if ti + 1 < T // n:
                nxt_prep = prep(ti + 1)

            for s in range(NS):
                cols = slice(s * 128, (s + 1) * 128)
                bA = [self.psum.next(), self.psum.next()]
                for h in range(8):
                    P.op("pe", lambda e, h=h, cols=cols, b=bA[h // 4]: e.matmul(
                        b[:, (h % 4) * 128:(h % 4 + 1) * 128], Kt[h][:, cols], Qt[h][:, cols], start=True, stop=True),
                        [Kt[h], Qt[h]], [bA[h // 4]])
                ats = []
                for h in range(8):
                    at = at_r.next()
                    ats.append(at)
                    P.op("dve", lambda e, h=h, at=at, b=bA[h // 4]: e.tensor_tensor(
                        at[:], b[:, (h % 4) * 128:(h % 4 + 1) * 128], mask2[:], ALU.mult), [bA[h // 4], mask2], [at])
                bO = [self.psum.next(), self.psum.next()]
                for h in range(8):
                    P.op("pe", lambda e, h=h, at=ats[h], s=s, b=bO[h // 4]: e.matmul(
                        b[:, (h % 4) * 128:(h % 4 + 1) * 128], V_t[:, s, h * 128:(h + 1) * 128], at[:],
                        start=(h % 4 == 0), stop=False, skip_group_check=True), [V_t, ats[h]], [bO[h // 4]])
                for half in range(2):
                    po = 64 * half
                    cc = slice(s * 128 + po, s * 128 + po + 64)
                    c = 2 * s + half
                    bS = [self.psum.next(), self.psum.next()]
                    for h in range(8):
                        P.op("pe", lambda e, h=h, cc=cc, po=po, half=half, b=bO[h // 4]: e.matmul(
                            b[:, (h % 4) * 128 + po:(h % 4) * 128 + po + 64], S_bf[h // 4][:, h % 4, :], Qt[h][:, cc],
                            start=False, stop=(half == 1), skip_group_check=True), [S_bf[h // 4], Qt[h]], [bO[h // 4]])
                        P.op("pe", lambda e, h=h, s=s, po=po, b=bS[h // 4]: e.matmul(
                            b[:, (h % 4) * 128:(h % 4 + 1) * 128], Ktok[h][po:po + 64, s, :],
                            V_t[po:po + 64, s, h * 128:(h + 1) * 128], start=True, stop=True),
                            [Ktok[h], V_t], [bS[h // 4]])
                    for g in range(2):
                        stt = st_r.next()
                        P.op("dve", lambda e, g=g, stt=stt, b=bS[g]: e.tensor_tensor(
                            stt[:], b[:, :], S[:, 4 * g:4 * g + 4, :].rearrange("p h d -> p (h d)"), ALU.add),
                            [bS[g], S], [stt])
                        P.op("dve", lambda e, g=g, stt=stt, c=c: e.tensor_tensor(
                            S[:, 4 * g:4 * g + 4, :], stt[:].rearrange("p (h d) -> p h d", h=4),
                            ebA[:, c, 4 * g:4 * g + 4].rearrange("p (h o) -> p h o", o=1).to_broadcast([128, 4, 128]),
                            ALU.mult), [stt, ebA], [S])
                        P.op("act", lambda e, g=g: e.copy(S_bf[g][:], S[:, 4 * g:4 * g + 4, :]), [S], [S_bf[g]])
                for h in range(8):
                    P.op("act", lambda e, h=h, cols=cols, b=bO[h // 4]: e.copy(
                        o_f[:, h, cols], b[:, (h % 4) * 128:(h % 4 + 1) * 128]), [bO[h // 4]], [o_f])
            for h in range(8):
                sq = sqh.next()
                P.op("pool", lambda e, sq=sq, h=h: e.tensor_tensor(sq[:], o_f[:, h, :], o_f[:, h, :], ALU.mult), [o_f], [sq])
                ps = self.psum.next()
                P.op("pe", lambda e, ps=ps, sq=sq: e.matmul(ps[:, 0:n], self.ones_b[:], sq[:], start=True, stop=True),
                     [sq, self.ones_b], [ps])
                t1 = tr.next()
                P.op("dve", lambda e, t1=t1, ps=ps: e.tensor_scalar(t1[:], ps[:, 0:n], 1.0 / 128, EPS, ALU.mult, ALU.add),
                     [ps], [t1])
                P.op("act", lambda e, t1=t1: e.activation(t1[:], t1[:], AF.Ln), [t1], [t1])
                P.op("act", lambda e, t1=t1: e.activation(t1[:], t1[:], AF.Exp, scale=-0.5), [t1], [t1])
                P.op("dve", lambda e, t1=t1, h=h: e.scalar_tensor_tensor(
                    t1[:], o_f[:, h, :], ng[:, 0:1], t1[:], ALU.mult, ALU.mult), [o_f, ng, t1], [t1])
                P.op("pool", lambda e, t1=t1, h=h: e.tensor_tensor(ogT[:, h, :], t1[:], sgate[h][:], ALU.mult),
                     [t1, sgate[h]], [ogT])
            for c in range(NCH):
                ps = self.psum.next()
                for h in range(8):
                    P.op("pe", lambda e, ps=ps, h=h, c=c: e.matmul(
                        ps[:, 0:n], wo[:, h, c * 128:(c + 1) * 128], ogT[:, h, :], start=(h == 0), stop=(h == 7)),
                        [wo, ogT], [ps])
                P.op("dve", lambda e, ps=ps, c=c, xt=xt: e.scalar_tensor_tensor(
                    xt[:, c, 0:n], ps[:, 0:n], GA1[:, c:c + 1], xt[:, c, 0:n], ALU.mult, ALU.add), [ps, xt, GA1], [xt])
            self.store_xt(xt, t0, n)
        P.end_scope()


FULL_PHASES = [("gm", 0), ("ffn", 0), ("mla", 1), ("ffn", 1), ("hg", 2), ("ffn", 2), ("gm", 3), ("ffn", 3)]


def make_consts():
    ident = np.eye(128, dtype=np.float32)
    tri = np.triu(np.ones((128, 128), dtype=np.float32))
    j = np.arange(128) % 32
    invf = np.zeros((2, 128), dtype=np.float32)
    invf[0] = (10000.0 ** (-(2.0 * j) / 64.0)).astype(np.float32)
    invf[1] = np.where((np.arange(128) // 32) % 2 == 0, -1.0, 1.0)
    sel = np.zeros((8, 1024), dtype=np.float32)
    for g in range(8):
        sel[g, g * 128:(g + 1) * 128] = 1.0
    mask2 = np.zeros((128, 128), dtype=np.float32)
    mask2[0:64, 0:64] = tri[0:64, 0:64]
    mask2[64:128, 64:128] = tri[0:64, 0:64]
    return {"k_ident": ident, "k_tri": tri, "k_invf": invf, "k_sel": sel, "k_mask2": mask2}


LAYER_SHAPES = {
    "ada_w": [D, 6 * D], "ada_b": [48, 128], "mix_norm_g": [NCH, 128], "ffn_norm_g": [NCH, 128],
    "gm_w_in": [D, 4096], "gm_ln_g": [16, 128], "gm_ln_b": [16, 128], "gm_w_s": [8, 128, 128],
    "gm_b_s": [8, 128], "gm_w_out": [GM_INNER, D],
    "mla_w_a": [D, 448], "mla_q_norm_g": [2, 128], "mla_kv_norm_g": [1, 128], "mla_w_qb": [256, 1536],
    "mla_w_kvb": [128, 2048], "mla_w_o": [1024, 1024],
    "hg_w_in": [D, 4096], "hg_norm_g": [1, 128], "hg_w_o": [1024, 1024],
    "ff_w_up": [D, 2 * FF], "ff_conv_w": [3, 44, 128], "ff_conv_b": [44, 128], "ff_w_down": [FF, D],
}


def core_inputs(inp, names, b, T, consts):
    f = np.ascontiguousarray
    m = {}
    for nm in names:
        if nm in consts:
            m[nm] = consts[nm]
        elif "__" in nm:
            base, idx = nm.split("__")
            m[nm] = f(np.asarray(inp[base][int(idx)], dtype=np.float32).reshape(LAYER_SHAPES[base]))
        elif nm == "x":
            m[nm] = f(inp["x"][b, :T])
        elif nm == "c":
            m[nm] = f(inp["c"][b].reshape(NCH, 128))
        elif nm == "positions":
            m[nm] = f(inp["positions"][b, :T].reshape(1, T).astype(np.int32))
        elif nm == "final_g":
            m[nm] = f(inp["final_g"].reshape(NCH, 128))
        elif nm == "hg_lb":
            m[nm] = f(inp["hg_lb"].reshape(DEPTH, NCH, 128))
        else:
            raise KeyError(nm)
    return m


def run(inp, T, phases, trace=False, debug=False):
    bld = Builder(T, phases)
    bld.debug = debug
    nc = bld.build()
    names = list(bld.decl.keys())
    B = inp["x"].shape[0]
    consts = make_consts()
    per_b = [core_inputs(inp, names, b, T, consts) for b in range(B)]
    in_maps = [per_b[c % B] for c in range(N_CORES)]
    res = run_bass_kernel_spmd(nc, in_maps, core_ids=list(range(N_CORES)), trace=trace)
    out = np.stack([res.results[b]["y"] for b in range(B)], axis=0)
    return out, res


def kernel(**inputs):
    inp = {k: np.asarray(v) for k, v in inputs.items()}
    out, _ = run(inp, inp["x"].shape[1], FULL_PHASES)
    return out.astype(np.float32)
```

```python
import contextlib
import numpy as np
import concourse.bass as bass
import concourse.mybir as mybir
from concourse.bass_utils import run_bass_kernel_spmd

F32 = mybir.dt.float32
BF16 = mybir.dt.bfloat16
I32 = mybir.dt.int32
ALU = mybir.AluOpType
AF = mybir.ActivationFunctionType

D = 1024
NCH = 8
DEPTH = 4
FF = 2816
NFC = 22
EPS = 1e-6
GM_INNER = 2048
TT = 512
TF = 256
N_CORES = 8


class Tok:
    __slots__ = ("name", "last_w", "readers", "sem", "cnt")

    def __init__(self, name):
        self.name = name
        self.last_w = None
        self.readers = {}
        self.sem = None
        self.cnt = 0


class Op:
    __slots__ = ("engine", "fn", "reads", "writes", "dma", "chain", "deps", "signal", "ev")

    def __init__(self, engine, fn, reads, writes, dma, chain):
        self.engine = engine
        self.fn = fn
        self.reads = reads
        self.writes = writes
        self.dma = dma
        self.chain = chain
        self.deps = []
        self.signal = False
        self.ev = None


def _tok(x):
    return x.tok if isinstance(x, Buf) else x


class Ring:
    def __init__(self, bufs):
        self.bufs = bufs
        self.i = 0

    def next(self):
        b = self.bufs[self.i % len(self.bufs)]
        self.i += 1
        return b


class Buf:
    def __init__(self, name, shape, dt, psum):
        self.name, self.shape, self.dt, self.psum = name, list(shape), dt, psum
        self.t = None
        self.tok = Tok(name)

    def __getitem__(self, k):
        return self.t[k]


class Prog:
    def __init__(self, nc):
        self.nc = nc
        self.ops = []
        self.eng = {"pe": nc.tensor, "act": nc.scalar, "dve": nc.vector,
                    "pool": nc.gpsimd, "sp": nc.sync}
        self.in_scope = False

    def _mk(self, name, shape, dt, psum):
        self.uid = getattr(self, "uid", 0) + 1
        b = Buf(f"{name}_{self.uid}", shape, dt, psum)
        self.ops.append(("alloc", b, self.in_scope))
        return b

    def sb(self, name, shape, dt):
        return self._mk(name, shape, dt, False)

    def ps(self, name, shape, dt=F32):
        return self._mk(name, shape, dt, True)

    def ring(self, name, shape, dt, n, psum=False):
        return Ring([self._mk(f"{name}{i}", shape, dt, psum) for i in range(n)])

    def begin_scope(self):
        assert not self.in_scope
        self.in_scope = True
        self.ops.append(("begin",))

    def end_scope(self):
        assert self.in_scope
        self.in_scope = False
        self.ops.append(("end",))

    def op(self, engine, fn, reads=(), writes=(), dma=False, chain=None):
        o = Op(engine, fn, [_tok(r) for r in reads], [_tok(w) for w in writes], dma,
               _tok(chain) if chain is not None else None)
        self.ops.append(o)
        return o

    def dma(self, engine, fn, reads, writes, chain):
        return self.op(engine, fn, reads, writes, dma=True, chain=chain)

    def finalize(self):
        order = {}
        pending = {}
        scope_toks = []
        for idx, op in enumerate(self.ops):
            if isinstance(op, tuple):
                if op[0] == "alloc":
                    op[1].tok.readers = dict(pending)
                    if op[2]:
                        scope_toks.append(op[1].tok)
                elif op[0] == "begin":
                    scope_toks = []
                elif op[0] == "end":
                    for t in scope_toks:
                        cands = list(t.readers.items())
                        if t.last_w is not None:
                            lw = t.last_w
                            cands.append(((("dma", id(lw.chain)) if lw.dma else lw.engine), lw))
                        for k, o in cands:
                            if k not in pending or order[id(pending[k])] < order[id(o)]:
                                pending[k] = o
                    scope_toks = []
                continue
            order[id(op)] = idx
            deps = []
            for t in op.reads:
                if t.last_w is not None:
                    deps.append((t.last_w, True))
            for t in op.writes:
                if t.last_w is not None:
                    deps.append((t.last_w, False))
                deps.extend((r, False) for r in t.readers.values())
            need = []
            seen = set()
            for d, raw in deps:
                if d is op:
                    continue
                if (not d.dma) and (not op.dma) and d.engine == op.engine:
                    if not raw or op.engine == "pe":
                        continue
                if id(d) in seen:
                    continue
                seen.add(id(d))
                need.append(d)
                d.signal = True
            op.deps = need
            key = ("dma", id(op.chain)) if op.dma else op.engine
            for t in op.reads:
                t.readers[key] = op
            for t in op.writes:
                t.last_w = op
                t.readers = {}
        es_global = contextlib.ExitStack()
        es_scope = None

        def newsem(name):
            return es_global.enter_context(self.nc.semaphore(name))

        engsem = {e: newsem(f"sem_{e}") for e in ("pe", "act", "dve", "pool")}
        engcnt = {e: 0 for e in engsem}
        waited = {e: {} for e in self.eng}
        for op in self.ops:
            if isinstance(op, tuple):
                if op[0] == "alloc":
                    b = op[1]
                    st = es_scope if op[2] else es_global
                    f = self.nc.psum_tensor if b.psum else self.nc.sbuf_tensor
                    b.t = st.enter_context(f(b.name, b.shape, b.dt))
                elif op[0] == "begin":
                    es_scope = contextlib.ExitStack()
                elif op[0] == "end":
                    es_scope.close()
                    es_scope = None
                continue
            E = self.eng[op.engine]
            w = waited[op.engine]
            for d in op.deps:
                sem, val = d.ev
                k = id(sem)
                if w.get(k, 0) >= val:
                    continue
                E.wait_ge(sem, val)
                w[k] = val
            insts = op.fn(E) if op.fn is not None else None
            if op.dma:
                ch = op.chain
                if ch.sem is None:
                    ch.sem = newsem(f"dsem_{ch.name}")
                if not isinstance(insts, (list, tuple)):
                    insts = [insts]
                for ins in insts:
                    ins.then_inc(ch.sem, 16)
                    ch.cnt += 1
                op.ev = (ch.sem, 16 * ch.cnt)
            elif op.signal:
                assert insts is not None, "signalling op must emit an instruction"
                if isinstance(insts, (list, tuple)):
                    insts = insts[-1]
                engcnt[op.engine] += 1
                insts.then_inc(engsem[op.engine], 1)
                op.ev = (engsem[op.engine], engcnt[op.engine])
        es_global.close()


class Builder:
    def __init__(self, T, phases):
        self.T = T
        self.phases = phases
        nc = bass.Bass("TRN2", target_bir_lowering=False)
        self.nc = nc
        self.P = Prog(nc)
        self.decl = {}
        self.dbg_toks = []

    def din(self, name, shape, dt=F32):
        if name not in self.decl:
            self.decl[name] = self.nc.dram_tensor(name, list(shape), dt, kind="ExternalInput").ap()
        return self.decl[name]

    def dump(self, name, buf, shape, dt=F32, sl=None):
        if not getattr(self, "debug", False):
            return
        ap = self.nc.dram_tensor(name, list(shape), dt, kind="ExternalOutput").ap()
        tk = Tok("dbg" + name)
        self.dbg_toks.append(tk)
        self.P.dma("sp", lambda e: e.dma_start(out=ap, in_=(sl(buf) if sl else buf[:])), [buf], [tk], buf)

    def L(self, name, idx):
        return self.din(f"{name}__{idx}", LAYER_SHAPES[name])

    def build(self):
        nc, P, T = self.nc, self.P, self.T
        x = self.din("x", [T, D])
        cvec = self.din("c", [NCH, 128])
        final_g = self.din("final_g", [NCH, 128])
        k_ident = self.din("k_ident", [128, 128])
        k_tri = self.din("k_tri", [128, 128])
        y = nc.dram_tensor("y", [T, D], F32, kind="ExternalOutput").ap()
        self.xT_d = nc.dram_tensor("xT_scratch", [NCH, 128, T], F32, kind="Internal").ap()
        self.xT_tok = [Tok(f"xTd{i}") for i in range(T // TF)]

        ident = P.sb("ident", [128, 128], F32)
        ones_b = P.sb("ones_b", [128, 128], BF16)
        tri_f = P.sb("tri_f", [128, 128], F32)
        self.ident, self.ones_b, self.tri_f = ident, ones_b, tri_f
        P.dma("sp", lambda e: e.dma_start(out=ident[:], in_=k_ident[:, :]), [], [ident], ident)
        P.dma("sp", lambda e: e.dma_start(out=tri_f[:], in_=k_tri[:, :]), [], [tri_f], tri_f)
        P.op("dve", lambda e: e.memset(ones_b[:], 1.0), [], [ones_b])
        self.psum = P.ring("ps", [128, 512], F32, 7, psum=True)
        self.psb = P.ps("psb", [128, 1024], BF16)
        self.colscr = P.ring("colscr", [128, 128], F32, 2)

        self.phase_in(x)
        self.prologue(cvec, final_g)
        for kind, l in self.phases:
            if kind == "ffn":
                self.phase_ffn(l)
            elif kind == "gm":
                self.phase_gm(l, l // 3)
            elif kind == "mla":
                self.phase_mla(l)
            elif kind == "hg":
                self.phase_hg(l)
        self.phase_out(y)
        P.finalize()
        return nc

    def work(self, n, nxt=2, sq=True, nfw=3, nrstd=2):
        P = self.P
        self.n = n
        self.xt_ring = P.ring("xt", [128, NCH, n], F32, nxt)
        self.ht_ring = P.ring("ht", [128, NCH, n], BF16, 1)
        if sq:
            self.sq_ring = P.ring("sq", [128, NCH, n], BF16, 1)
        self.fw = P.ring("fw", [128, n], F32, nfw)
        self.rstd_ring = P.ring("rstd", [128, n], F32, nrstd)

    def load_cols(self, name, src_rows_ap, n):
        P = self.P
        dst = P.sb(name, [128, n], F32)
        scr = self.colscr.next()
        P.dma("sp", lambda e: e.dma_start(out=scr[0:n, :], in_=src_rows_ap), [], [scr], scr)
        ps = self.psum.next()
        P.op("pe", lambda e: e.transpose(ps[:, 0:n], scr[0:n, :], self.ident[0:n, 0:n]),
             [scr, self.ident], [ps])
        P.op("dve", lambda e: e.tensor_copy(dst[:], ps[:, 0:n]), [ps], [dst])
        return dst

    def prologue(self, cvec, final_g):
        P = self.P
        cT = self.load_cols("cT", cvec[:, :], NCH)
        cact = P.sb("cact", [128, NCH], F32)
        cact_b = P.sb("cact_b", [128, NCH], BF16)
        P.op("act", lambda e: e.activation(cact[:], cT[:], AF.Silu), [cT], [cact])
        P.op("dve", lambda e: e.tensor_copy(cact_b[:], cact[:]), [cact], [cact_b])
        self.final_g = self.load_cols("fing", final_g[:, :], NCH)
        layers = sorted(set(l for _, l in self.phases))
        self.G1, self.SH1, self.GA1, self.G2, self.SH2, self.GA2 = {}, {}, {}, {}, {}, {}
        mods, mgs, fgs, abs_ = {}, {}, {}, {}
        for l in layers:
            abs_[l] = self.load_cols(f"adab{l}", self.L("ada_b", l)[:, :], 48)
            mgs[l] = self.load_cols(f"mixg{l}", self.L("mix_norm_g", l)[:, :], NCH)
            fgs[l] = self.load_cols(f"ffng{l}", self.L("ffn_norm_g", l)[:, :], NCH)
            mods[l] = P.sb(f"mod{l}", [128, 48], F32)
            for nm, dd in (("Ga", self.G1), ("GAa", self.GA1), ("Gb", self.G2), ("GAb", self.GA2)):
                dd[l] = P.sb(f"{nm}{l}", [128, NCH], F32)
        P.begin_scope()
        wring = P.ring("adaw", [128, NCH, 1536], BF16, 2)
        for l in layers:
            mod, ab = mods[l], abs_[l]
            ps = self.psum.next()
            for q in range(4):
                wb = wring.next()
                src = self.L("ada_w", l)[:, q * 1536:(q + 1) * 1536].rearrange("(k p) n -> p k n", p=128)
                P.dma("pool", lambda e, wb=wb, src=src: e.dma_start(out=wb[:], in_=src), [], [wb], wb)
                for cc in range(12):
                    col = q * 12 + cc
                    for k in range(NCH):
                        P.op("pe", lambda e, wb=wb, cc=cc, k=k, col=col, ps=ps: e.matmul(
                            ps[:, col:col + 1], wb[:, k, cc * 128:(cc + 1) * 128], cact_b[:, k:k + 1],
                            start=(k == 0), stop=(k == NCH - 1)), [wb, cact_b], [ps])
            P.op("dve", lambda e, mod=mod, ps=ps, ab=ab: e.tensor_tensor(mod[:], ps[:, 0:48], ab[:], ALU.add),
                 [ps, ab], [mod])

            def derive(G, GA, gsrc, sc_off, g_off, mod=mod):
                P.op("dve", lambda e: e.scalar_tensor_tensor(
                    G[:], mod[:, sc_off:sc_off + 8], 1.0, gsrc[:], ALU.add, ALU.mult), [mod, gsrc], [G])
                P.op("dve", lambda e: e.tensor_scalar(GA[:], mod[:, g_off:g_off + 8], 1.0, None, ALU.add),
                     [mod], [GA])
            derive(self.G1[l], self.GA1[l], mgs[l], 8, 16)
            derive(self.G2[l], self.GA2[l], fgs[l], 32, 40)
            self.SH1[l] = (mod, 0)
            self.SH2[l] = (mod, 24)
        P.end_scope()

    def slab_toks(self, t0, n):
        return self.xT_tok[t0 // TF:(t0 + n) // TF]

    def load_xt(self, t0, n):
        P = self.P
        xt = self.xt_ring.next()
        src = self.xT_d[:, :, t0:t0 + n].rearrange("c p t -> p c t")
        P.dma("sp", lambda e: e.dma_start(out=xt[:, :, 0:n], in_=src), self.slab_toks(t0, n), [xt], xt)
        return xt

    def store_xt(self, xt, t0, n):
        P = self.P
        dst = self.xT_d[:, :, t0:t0 + n].rearrange("c p t -> p c t")
        P.dma("sp", lambda e: e.dma_start(out=dst, in_=xt[:, :, 0:n]), [xt], self.slab_toks(t0, n), xt)

    def rstd_of(self, xt, n):
        P = self.P
        sq = self.sq_ring.next()
        P.op("act", lambda e: e.activation(sq[:, 0:NCH, 0:n], xt[:, :, 0:n], AF.Square), [xt], [sq])
        ps = self.psum.next()
        for c in range(NCH):
            P.op("pe", lambda e, c=c: e.matmul(ps[:, 0:n], self.ones_b[:], sq[:, c, 0:n],
                                               start=(c == 0), stop=(c == NCH - 1)), [sq, self.ones_b], [ps])
        tmp = self.fw.next()
        P.op("act", lambda e: e.activation(tmp[:, 0:n], ps[:, 0:n], AF.Sqrt, scale=1.0 / D, bias=EPS), [ps], [tmp])
        rstd = self.rstd_ring.next()
        P.op("dve", lambda e: e.reciprocal(rstd[:, 0:n], tmp[:, 0:n]), [tmp], [rstd])
        return rstd

    def norm_mod(self, xt, n, G, SH, ht=None, c0=0):
        P = self.P
        rstd = self.rstd_of(xt, n)
        if ht is None:
            ht = self.ht_ring.next()
        shb, sho = SH
        for c in range(NCH):
            tmp = self.fw.next()
            P.op("dve" if c % 2 == 0 else "pool", lambda e, c=c, tmp=tmp: e.tensor_tensor(
                tmp[:, 0:n], xt[:, c, 0:n], rstd[:, 0:n], ALU.mult), [xt, rstd], [tmp])
            P.op("act", lambda e, c=c, tmp=tmp: e.activation(
                ht[:, c, c0:c0 + n], tmp[:, 0:n], AF.Identity, scale=G[:, c:c + 1], bias=shb[:, sho + c:sho + c + 1]),
                [tmp, G, shb], [ht])
        return ht

    def phase_in(self, x):
        P, T = self.P, self.T
        P.begin_scope()
        self.work(TT)
        xin_ring = P.ring("xin", [128, 4, D], F32, 2)
        for ti in range(T // TT):
            t0 = ti * TT
            xin = xin_ring.next()
            src = x[t0:t0 + TT, :].rearrange("(s p) d -> p s d", p=128)
            P.dma("sp", lambda e, xin=xin, src=src: e.dma_start(out=xin[:], in_=src), [], [xin], xin)
            xt = self.xt_ring.next()
            for c in range(NCH):
                ps = self.psum.next()
                for s in range(4):
                    P.op("pe", lambda e, ps=ps, s=s, c=c, xin=xin: e.transpose(
                        ps[:, s * 128:(s + 1) * 128], xin[:, s, c * 128:(c + 1) * 128], self.ident[:]),
                        [xin, self.ident], [ps])
                if c % 2 == 0:
                    P.op("act", lambda e, ps=ps, c=c, xt=xt: e.copy(xt[:, c, :], ps[:]), [ps], [xt])
                else:
                    P.op("dve", lambda e, ps=ps, c=c, xt=xt: e.tensor_copy(xt[:, c, :], ps[:]), [ps], [xt])
            self.store_xt(xt, t0, TT)
        P.end_scope()

    def phase_out(self, y):
        P, T = self.P, self.T
        P.begin_scope()
        self.work(TT)
        yo_ring = P.ring("yo", [128, 4, D], F32, 2)
        yts = P.ring("yT", [128, NCH, TT], F32, 1)
        out_toks = []
        for ti in range(T // TT):
            t0 = ti * TT
            xt = self.load_xt(t0, TT)
            rstd = self.rstd_of(xt, TT)
            yT = yts.next()
            for c in range(NCH):
                P.op("dve", lambda e, c=c, xt=xt, yT=yT, rstd=rstd: e.scalar_tensor_tensor(
                    yT[:, c, :], xt[:, c, :], self.final_g[:, c:c + 1], rstd[:], ALU.mult, ALU.mult),
                    [xt, rstd, self.final_g], [yT])
            yo = yo_ring.next()
            for s in range(4):
                for h in range(2):
                    ps = self.psum.next()
                    for cc in range(4):
                        c = h * 4 + cc
                        P.op("pe", lambda e, ps=ps, s=s, c=c, cc=cc, yT=yT: e.transpose(
                            ps[:, cc * 128:(cc + 1) * 128], yT[:, c, s * 128:(s + 1) * 128], self.ident[:]),
                            [yT, self.ident], [ps])
                    if (s + h) % 2 == 0:
                        P.op("act", lambda e, ps=ps, s=s, h=h, yo=yo: e.copy(yo[:, s, h * 512:(h + 1) * 512], ps[:]),
                             [ps], [yo])
                    else:
                        P.op("dve", lambda e, ps=ps, s=s, h=h, yo=yo: e.tensor_copy(yo[:, s, h * 512:(h + 1) * 512], ps[:]),
                             [ps], [yo])
            dst = y[t0:t0 + TT, :].rearrange("(s p) d -> p s d", p=128)
            tk = Tok(f"yout{ti}")
            out_toks.append(tk)
            P.dma("sp", lambda e, yo=yo, dst=dst: e.dma_start(out=dst, in_=yo[:]), [yo], [tk], yo)
        P.op("sp", None, reads=out_toks + self.dbg_toks, writes=[])
        P.end_scope()

    def phase_ffn(self, l):
        P, T = self.P, self.T
        ff_w_up, ff_conv_w, ff_conv_b, ff_w_down = (self.L(k, l) for k in ("ff_w_up", "ff_conv_w", "ff_conv_b", "ff_w_down"))
        n = TF
        cw = [self.load_cols(f"cw{l}_{j}", ff_conv_w[j, :, :], 44) for j in range(3)]
        cb = self.load_cols(f"cb{l}", ff_conv_b[:, :], 44)
        P.begin_scope()
        self.work(n)
        wup = P.sb("wup", [128, NCH, 2 * FF], BF16)
        wdn = P.sb("wdn", [128, NFC, D], BF16)
        hts = P.ring("hth", [128, NCH, n + 2], BF16, 2)
        ybufs = P.ring("ybuf", [128, n], F32, 8)
        sgs = P.ring("sgb", [128, n], F32, 3)
        actT = P.sb("actT", [128, NFC, n], BF16)
        for h in range(4):
            src = ff_w_up[:, h * 1408:(h + 1) * 1408].rearrange("(k p) n -> p k n", p=128)
            P.dma("pool", lambda e, src=src, h=h: e.dma_start(out=wup[:, :, h * 1408:(h + 1) * 1408], in_=src),
                  [], [wup], wup)
        for h in range(2):
            src = ff_w_down[h * 1408:(h + 1) * 1408, :].rearrange("(j p) n -> p j n", p=128)
            P.dma("pool", lambda e, src=src, h=h: e.dma_start(out=wdn[:, h * 11:(h + 1) * 11, :], in_=src),
                  [], [wdn], wdn)
        G2, SH2, GA2 = self.G2[l], self.SH2[l], self.GA2[l]
        def prep(ti, prev):
            xt = self.load_xt(ti * n, n)
            ht = hts.next()
            if prev is None:
                P.op("dve", lambda e, ht=ht: e.memset(ht[:, :, 0:2], 0.0), [], [ht])
            else:
                P.op("dve", lambda e, ht=ht, prev=prev: e.tensor_copy(ht[:, :, 0:2], prev[:, :, n:n + 2]), [prev], [ht])
            self.norm_mod(xt, n, G2, SH2, ht=ht, c0=2)
            return xt, ht

        nxt_prep = prep(0, None)
        for ti in range(T // n):
            t0 = ti * n
            xt, ht = nxt_prep
            pend_g = None

            def gate(yg, yv, j):
                sg = sgs.next()
                P.op("act", lambda e: e.activation(sg[:], yg[:], AF.Silu), [yg], [sg])
                P.op("pool", lambda e: e.tensor_tensor(actT[:, j, :], sg[:], yv[:], ALU.mult), [sg, yv], [actT])
            for j in range(NFC):
                ys = []
                for half in range(2):
                    ch = half * NFC + j
                    ps = self.psum.next()
                    for k in range(NCH):
                        P.op("pe", lambda e, ps=ps, k=k, ch=ch, ht=ht: e.matmul(
                            ps[:, 0:n + 2], wup[:, k, ch * 128:(ch + 1) * 128], ht[:, k, :],
                            start=(k == 0), stop=(k == NCH - 1)), [wup, ht], [ps])
                    yb = ybufs.next()
                    P.op("act", lambda e, yb=yb, ps=ps, ch=ch: e.activation(
                        yb[:], ps[:, 2:n + 2], AF.Identity, scale=cw[2][:, ch:ch + 1], bias=cb[:, ch:ch + 1]),
                        [ps, cw[2], cb], [yb])
                    P.op("dve", lambda e, yb=yb, ps=ps, ch=ch: e.scalar_tensor_tensor(
                        yb[:], ps[:, 1:n + 1], cw[1][:, ch:ch + 1], yb[:], ALU.mult, ALU.add), [ps, yb, cw[1]], [yb])
                    P.op("dve", lambda e, yb=yb, ps=ps, ch=ch: e.scalar_tensor_tensor(
                        yb[:], ps[:, 0:n], cw[0][:, ch:ch + 1], yb[:], ALU.mult, ALU.add), [ps, yb, cw[0]], [yb])
                    ys.append(yb)
                if pend_g is not None:
                    gate(*pend_g)
                pend_g = (ys[0], ys[1], j)
            gate(*pend_g)
            pend_g = None
            if ti + 1 < T // n:
                nxt_prep = prep(ti + 1, ht)
            for c in range(NCH):
                ps = self.psum.next()
                for j in range(NFC):
                    P.op("pe", lambda e, ps=ps, j=j, c=c: e.matmul(
                        ps[:, 0:n], wdn[:, j, c * 128:(c + 1) * 128], actT[:, j, :],
                        start=(j == 0), stop=(j == NFC - 1)), [wdn, actT], [ps])
                P.op("dve", lambda e, ps=ps, c=c, xt=xt: e.scalar_tensor_tensor(
                    xt[:, c, 0:n], ps[:, 0:n], GA2[:, c:c + 1], xt[:, c, 0:n], ALU.mult, ALU.add),
                    [ps, xt, GA2], [xt])
            self.store_xt(xt, t0, n)
        P.end_scope()

    def pipeline(self, items, stages):
        ns = len(stages)
        for step in range(len(items) + ns - 1):
            for s in reversed(range(ns)):
                i = step - s
                if 0 <= i < len(items):
                    stages[s](items[i])

    def phase_gm(self, l, j):
        P, T = self.P, self.T
        gm_w_in, gm_ln_g, gm_ln_b, gm_w_s, gm_b_s, gm_w_out = (self.L(k, j) for k in ("gm_w_in", "gm_ln_g", "gm_ln_b", "gm_w_s", "gm_b_s", "gm_w_out"))
        k_sel = self.din("k_sel", [8, 1024])
        n = TT
        lng = self.load_cols(f"lng{l}", gm_ln_g[:, :], 16)
        lnb = self.load_cols(f"lnb{l}", gm_ln_b[:, :], 16)
        P.begin_scope()
        self.work(n, nxt=1, sq=False, nfw=2, nrstd=1)
        xs_r = P.ring("gxs", [128, n], F32, 4)
        t_r = P.ring("gt", [128, n], F32, 6)
        win = P.sb("win", [128, NCH, 4096], BF16)
        wout = P.sb("wout", [128, 16, D], BF16)
        wsT = P.sb("wsT", [128, 8, 128], BF16)
        CT = P.sb("CT", [128, 16, 128], F32)
        uT = P.sb("uT", [128, 16, n], BF16)
        self.sq_ring = Ring([uT])
        vg = P.sb("vg", [128, GM_INNER], F32)
        vhat = P.sb("vhat", [128, 4, GM_INNER], BF16)
        stats = P.sb("stats", [128, 4, 6], F32)
        mv = P.sb("mv", [128, 2], F32)
        for h in range(4):
            src = gm_w_in[:, h * 1024:(h + 1) * 1024].rearrange("(k p) n -> p k n", p=128)
            P.dma("pool", lambda e, src=src, h=h: e.dma_start(out=win[:, :, h * 1024:(h + 1) * 1024], in_=src),
                  [], [win], win)
        src = gm_w_out[:, :].rearrange("(f p) n -> p f n", p=128)
        P.dma("pool", lambda e, src=src: e.dma_start(out=wout[:], in_=src), [], [wout], wout)
        wsl_r = P.ring("wsl", [128, 128], F32, 2)
        bsl = P.sb("bsl", [8, 128], F32)
        sel_r = P.ring("sel", [8, 128], F32, 2)
        P.dma("sp", lambda e: e.dma_start(out=bsl[:], in_=gm_b_s[:, :]), [], [bsl], bsl)
        bsb = P.sb("bsb", [128, 128], F32)
        for g in range(8):
            wsl = wsl_r.next()
            sel = sel_r.next()
            P.dma("sp", lambda e, wsl=wsl, g=g: e.dma_start(out=wsl[:], in_=gm_w_s[g, :, :]), [], [wsl], wsl)
            P.dma("sp", lambda e, sel=sel, g=g: e.dma_start(out=sel[:], in_=k_sel[:, g * 128:(g + 1) * 128]), [], [sel], sel)
            ps = self.psum.next()
            P.op("pe", lambda e, ps=ps, wsl=wsl: e.transpose(ps[:, 0:128], wsl[:], self.ident[:]),
                 [wsl, self.ident], [ps])
            P.op("dve", lambda e, ps=ps, g=g: e.tensor_tensor(wsT[:, g, :], ps[:, 0:128], self.tri_f[:], ALU.mult),
                 [ps, self.tri_f], [wsT])
            ps2 = self.psum.next()
            P.op("pe", lambda e, ps2=ps2, g=g: e.matmul(ps2[:, 0:128], self.ones_b[:], wsT[:, g, :], start=True, stop=True),
                 [wsT, self.ones_b], [ps2])
            P.op("pe", lambda e, ps2=ps2, sel=sel: e.matmul(ps2[:, 128:256], sel[:, :], bsl[:, :],
                                                            start=True, stop=True), [sel, bsl], [ps2])
            P.op("act", lambda e, ps2=ps2: e.copy(bsb[:], ps2[:, 128:256]), [ps2], [bsb])
            for q in range(2):
                fc = 2 * g + q
                P.op("dve", lambda e, ps2=ps2, fc=fc: e.scalar_tensor_tensor(
                    CT[:, fc, :], ps2[:, 0:128], lnb[:, fc:fc + 1], bsb[:], ALU.mult, ALU.add),
                    [ps2, lnb, bsb], [CT])
        G1, SH1, GA1 = self.G1[l], self.SH1[l], self.GA1[l]
        for ti in range(T // n):
            t0 = ti * n
            xt = self.load_xt(t0, n)
            ht = self.norm_mod(xt, n, G1, SH1)
            items = [("v", s_, nb) for s_ in range(4) for nb in range(4)] + [("u", fc, 0) for fc in range(16)]
            st = {}

            def s_mm(it, ht=ht):
                kind, a_, b_ = it
                ps = self.psum.next()
                st[it] = {"ps": ps}
                for k in range(NCH):
                    if kind == "u":
                        P.op("pe", lambda e, ps=ps, k=k, fc=a_: e.matmul(
                            ps[:, :], win[:, k, fc * 128:(fc + 1) * 128], ht[:, k, :],
                            start=(k == 0), stop=(k == NCH - 1)), [win, ht], [ps])
                    else:
                        P.op("pe", lambda e, ps=ps, k=k, s_=a_, nb=b_: e.matmul(
                            ps[:, :], ht[:, k, s_ * 128:(s_ + 1) * 128], win[:, k, 2048 + nb * 512:2048 + (nb + 1) * 512],
                            start=(k == 0), stop=(k == NCH - 1)), [win, ht], [ps])

            def s_copy(it):
                d = st[it]
                d["xs"] = xs_r.next()
                d["t"] = t_r.next()
                P.op("act", lambda e, d=d: e.activation(d["xs"][:], d["ps"][:, :], AF.Identity), [d["ps"]], [d["xs"]])
                P.op("act", lambda e, d=d: e.activation(d["t"][:], d["ps"][:, :], AF.Square, scale=0.21145921592590375),
                     [d["ps"]], [d["t"]])

            def s_poly(it):
                d = st[it]
                P.op("dve", lambda e, d=d: e.scalar_tensor_tensor(
                    d["t"][:], d["t"][:], 1.0, d["xs"][:], ALU.add, ALU.mult), [d["t"], d["xs"]], [d["t"]])

            def s_sig(it):
                d = st[it]
                P.op("act", lambda e, d=d: e.activation(d["t"][:], d["t"][:], AF.Sigmoid, scale=1.5957691216057308),
                     [d["t"]], [d["t"]])

            def s_out(it):
                kind, a_, b_ = it
                d = st[it]
                if kind == "u":
                    P.op("dve", lambda e, d=d, fc=a_: e.tensor_tensor(uT[:, fc, :], d["t"][:], d["xs"][:], ALU.mult),
                         [d["t"], d["xs"]], [uT])
                else:
                    P.op("dve", lambda e, d=d, nb=b_: e.tensor_tensor(
                        vg[:, nb * 512:(nb + 1) * 512], d["t"][:], d["xs"][:], ALU.mult), [d["t"], d["xs"]], [vg])

            def s_sp_mm(it):
                kind, fc, _ = it
                d = st[it]
                d["sp"] = self.psum.next()
                for s2 in range(4):
                    P.op("pe", lambda e, d=d, s2=s2, fc=fc: e.matmul(
                        d["sp"][:, s2 * 128:(s2 + 1) * 128], vhat[:, s2, fc * 128:(fc + 1) * 128], wsT[:, fc // 2, :],
                        start=True, stop=True), [vhat, wsT], [d["sp"]])

            def s_sp_evac(it):
                kind, fc, _ = it
                if kind != "u":
                    return
                d = st[it]
                d["tmp"] = t_r.next()
                P.op("dve", lambda e, d=d, fc=fc: e.scalar_tensor_tensor(
                    d["tmp"][:].rearrange("p (s t) -> p s t", s=4), d["sp"][:, :].rearrange("p (s t) -> p s t", s=4),
                    lng[:, fc:fc + 1], CT[:, fc:fc + 1, :].to_broadcast([128, 4, 128]),
                    ALU.mult, ALU.add), [d["sp"], lng, CT], [d["tmp"]])

            def s_ymul(it):
                kind, fc, _ = it
                if kind != "u":
                    return
                d = st[it]
                P.op("dve", lambda e, d=d, fc=fc: e.tensor_tensor(uT[:, fc, :], uT[:, fc, :], d["tmp"][:], ALU.mult),
                     [uT, d["tmp"]], [uT])

            def s_ln(it):
                kind, s_, nb = it
                if kind != "v":
                    s_sp_mm(it)
                    return
                P.op("dve", lambda e, nb=nb: e.bn_stats(stats[:, nb, :], vg[:, nb * 512:(nb + 1) * 512]), [vg], [stats])
                if nb == 3:
                    P.op("dve", lambda e: e.bn_aggr(mv[:], stats[:].rearrange("p a b -> p (a b)")), [stats], [mv])
                    P.op("act", lambda e: e.activation(mv[:, 1:2], mv[:, 1:2], AF.Sqrt, bias=EPS), [mv], [mv])
                    P.op("dve", lambda e: e.reciprocal(mv[:, 1:2], mv[:, 1:2]), [mv], [mv])
                    P.op("dve", lambda e: e.scalar_tensor_tensor(mv[:, 0:1], mv[:, 0:1], -1.0, mv[:, 1:2], ALU.mult, ALU.mult),
                         [mv], [mv])
                    P.op("act", lambda e, s_=s_: e.activation(vhat[:, s_, :], vg[:], AF.Identity, scale=mv[:, 1:2], bias=mv[:, 0:1]),
                         [vg, mv], [vhat])

            self.pipeline(items, [s_mm, s_copy, s_poly, s_sig, s_out, s_ln, s_sp_evac, s_ymul])
            for c in range(NCH):
                ps = self.psum.next()
                for fc in range(16):
                    P.op("pe", lambda e, ps=ps, fc=fc, c=c: e.matmul(
                        ps[:, :], wout[:, fc, c * 128:(c + 1) * 128], uT[:, fc, :],
                        start=(fc == 0), stop=(fc == 15)), [wout, uT], [ps])
                P.op("dve", lambda e, ps=ps, c=c, xt=xt: e.scalar_tensor_tensor(
                    xt[:, c, :], ps[:, :], GA1[:, c:c + 1], xt[:, c, :], ALU.mult, ALU.add),
                    [ps, xt, GA1], [xt])
            self.store_xt(xt, t0, n)
        P.end_scope()

    def rope_tables(self, cos2, sin2):
        P, T = self.P, self.T
        pos = self.din("positions", [1, T], I32)
        k_invf = self.din("k_invf", [2, 128])
        posi = P.sb("posi", [1, T], I32)
        posf = P.sb("posf", [1, T], F32)
        invf = P.sb("invf", [1, 128], F32)
        sgn = self.load_cols("ropesgn", k_invf[:, :], 2)
        P.dma("sp", lambda e: e.dma_start(out=posi[:], in_=pos[:, :]), [], [posi], posi)
        P.dma("sp", lambda e: e.dma_start(out=invf[:], in_=k_invf[0:1, :]), [], [invf], invf)
        P.op("dve", lambda e: e.tensor_copy(posf[:], posi[:]), [posi], [posf])
        ki = P.sb("ropek", [128, 512], I32)
        kf = P.sb("ropekf", [128, 512], F32)
        r = P.sb("roper", [128, 512], F32)
        m = P.sb("ropem", [128, 512], F32)
        TWO_PI = 6.283185307179586
        for b in range(T // 512):
            ps = self.psum.next()
            P.op("pe", lambda e, ps=ps, b=b: e.matmul(ps[:, :], invf[:, :], posf[:, b * 512:(b + 1) * 512],
                                                      start=True, stop=True), [invf, posf], [ps])
            for which, dst in ((0, sin2), (1, cos2)):
                P.op("dve", lambda e, ps=ps, which=which: e.tensor_scalar(
                    r[:], ps[:, :], 1.0 / TWO_PI, 0.25 * which, ALU.mult, ALU.add), [ps], [r])
                P.op("dve", lambda e: e.tensor_copy(ki[:], r[:]), [r], [ki])
                P.op("dve", lambda e: e.tensor_copy(kf[:], ki[:]), [ki], [kf])
                P.op("dve", lambda e: e.tensor_tensor(r[:], r[:], kf[:], ALU.subtract), [r, kf], [r])
                P.op("dve", lambda e: e.tensor_scalar(m[:], r[:], 0.5, None, ALU.is_gt), [r], [m])
                P.op("dve", lambda e: e.tensor_tensor(r[:], r[:], m[:], ALU.subtract), [r, m], [r])
                P.op("dve", lambda e: e.tensor_scalar(m[:], r[:], -0.5, None, ALU.is_lt), [r], [m])
                P.op("dve", lambda e: e.tensor_tensor(r[:], r[:], m[:], ALU.add), [r, m], [r])
                if which == 0:
                    P.op("act", lambda e, b=b: e.activation(m[:], r[:], AF.Sin, scale=6.28318), [r], [m])
                    P.op("dve", lambda e, b=b, dst=dst: e.tensor_scalar(
                        dst[:, b * 512:(b + 1) * 512], m[:], sgn[:, 1:2], None, ALU.mult), [m, sgn], [dst])
                else:
                    P.op("act", lambda e, b=b, dst=dst: e.activation(
                        dst[:, b * 512:(b + 1) * 512], r[:], AF.Sin, scale=6.28318), [r], [dst])

    def phase_mla(self, l):
        P, T = self.P, self.T
        nc = self.nc
        n = TT
        NTL = T // n
        w_a, qg_d, kvg_d, w_qb, w_kvb, w_o = (self.L(k, 0) for k in (
            "mla_w_a", "mla_q_norm_g", "mla_kv_norm_g", "mla_w_qb", "mla_w_kvb", "mla_w_o"))
        qg = self.load_cols("mlaqg", qg_d[:, :], 2)
        kvg = self.load_cols("mlakvg", kvg_d[:, :], 1)
        qn_d = nc.dram_tensor("qn_d", [8, 128, T], BF16, kind="Internal").ap()
        qr_d = nc.dram_tensor("qr_d", [4, 128, T], BF16, kind="Internal").ap()
        kn_d = nc.dram_tensor("kn_d", [8, 128, T], BF16, kind="Internal").ap()
        kr_d = nc.dram_tensor("kr_d", [128, T], BF16, kind="Internal").ap()
        v_d = nc.dram_tensor("v_d", [T, 1024], BF16, kind="Internal").ap()
        oT_d = nc.dram_tensor("oT_d", [8, 128, T], BF16, kind="Internal").ap()
        tk = {nm: [Tok(f"{nm}{i}") for i in range(NTL)] for nm in ("qn", "qr", "kn", "kr", "v")}
        otk = [Tok(f"oT{h}") for h in range(8)]
        G1, SH1, GA1 = self.G1[l], self.SH1[l], self.GA1[l]

        P.begin_scope()
        self.work(n)
        cos2 = P.sb("cos2", [128, T], F32)
        sin2 = P.sb("sin2", [128, T], F32)
        self.rope_tables(cos2, sin2)
        wa = P.sb("wa", [128, NCH, 384], BF16)
        wkr = P.sb("wkr", [128, NCH, 256], BF16)
        wqn = P.sb("wqn", [128, 2, 8, 128], BF16)
        wqrA = P.sb("wqrA", [128, 2, 8, 64], BF16)
        wqrB = P.sb("wqrB", [128, 2, 8, 64], BF16)
        wkn = P.sb("wkn", [128, 8, 128], BF16)
        wv = P.sb("wv", [128, 8, 128], BF16)
        wa_v = w_a.rearrange("(k p) n -> p k n", p=128)
        P.dma("pool", lambda e: e.dma_start(out=wa[:], in_=wa_v[:, :, 0:384]), [], [wa], wa)

        def wkr_load(e):
            r = []
            for dup in range(2):
                r.append(e.dma_start(out=wkr[:, :, dup * 64:dup * 64 + 64], in_=wa_v[:, :, 384:448]))
                r.append(e.dma_start(out=wkr[:, :, 128 + dup * 64:128 + dup * 64 + 32], in_=wa_v[:, :, 416:448]))
                r.append(e.dma_start(out=wkr[:, :, 160 + dup * 64:160 + dup * 64 + 32], in_=wa_v[:, :, 384:416]))
            return r
        P.dma("pool", wkr_load, [], [wkr], wkr)
        wq_v = w_qb.rearrange("(k p) (h e) -> p k h e", p=128, e=192)
        P.dma("pool", lambda e: [e.dma_start(out=wqn[:, kc, :, :], in_=wq_v[:, kc, :, 0:128]) for kc in range(2)],
              [], [wqn], wqn)
        P.dma("pool", lambda e: [e.dma_start(out=wqrA[:, kc, :, :], in_=wq_v[:, kc, :, 128:192]) for kc in range(2)],
              [], [wqrA], wqrA)
        P.dma("pool", lambda e: [e.dma_start(out=wqrB[:, kc, :, 0:32], in_=wq_v[:, kc, :, 160:192]) for kc in range(2)] +
                                [e.dma_start(out=wqrB[:, kc, :, 32:64], in_=wq_v[:, kc, :, 128:160]) for kc in range(2)],
              [], [wqrB], wqrB)
        wkv_v = w_kvb.rearrange("p (h two e) -> p two h e", two=2, e=128)
        P.dma("pool", lambda e: e.dma_start(out=wkn[:], in_=wkv_v[:, 0, :, :]), [], [wkn], wkn)
        P.dma("pool", lambda e: e.dma_start(out=wv[:], in_=wkv_v[:, 1, :, :]), [], [wv], wv)
        cqn = P.sb("cqn", [128, 2, n], BF16)
        ckvn = P.sb("ckvn", [128, n], BF16)
        sqs = P.ring("sqs", [128, n], BF16, 2)
        qn_t = P.sb("qn_t", [128, 8, n], BF16)
        qr_t = P.sb("qr_t", [128, 4, n], BF16)
        kn_t = P.sb("kn_t", [128, 8, n], BF16)
        kr_t = P.sb("kr_t", [128, n], BF16)
        v_t = P.sb("v_t", [128, 4, 1024], BF16)
        rt = P.ring("rt", [128, n], F32, 3)

        def small_rstd(pss, dim):
            ps_s = self.psum.next()
            for i, pz in enumerate(pss):
                sq = sqs.next()
                P.op("act", lambda e, sq=sq, pz=pz: e.activation(sq[:], pz[:, :], AF.Square), [pz], [sq])
                P.op("pe", lambda e, sq=sq, i=i: e.matmul(ps_s[:, :], self.ones_b[:], sq[:],
                                                          start=(i == 0), stop=(i == len(pss) - 1)),
                     [sq, self.ones_b], [ps_s])
            tmp = self.fw.next()
            P.op("act", lambda e: e.activation(tmp[:], ps_s[:, :], AF.Sqrt, scale=1.0 / dim, bias=EPS), [ps_s], [tmp])
            rs = self.rstd_ring.next()
            P.op("dve", lambda e: e.reciprocal(rs[:], tmp[:]), [tmp], [rs])
            return rs

        def rotate(psA, psB, dst_fn, writes, t0):
            ta, tb = rt.next(), rt.next()
            P.op("dve", lambda e: e.tensor_tensor(ta[:], psA[:, :], cos2[:, t0:t0 + n], ALU.mult), [psA, cos2], [ta])
            P.op("dve", lambda e: e.tensor_tensor(tb[:], psB[:, :], sin2[:, t0:t0 + n], ALU.mult), [psB, sin2], [tb])
            P.op("pool", lambda e: e.tensor_tensor(dst_fn(), ta[:], tb[:], ALU.add), [ta, tb], writes)

        for ti in range(NTL):
            t0 = ti * n
            xt = self.load_xt(t0, n)
            ht = self.norm_mod(xt, n, G1, SH1)
            pcq = []
            for c in range(3):
                ps = self.psum.next()
                for k in range(NCH):
                    P.op("pe", lambda e, ps=ps, k=k, c=c, ht=ht: e.matmul(
                        ps[:, :], wa[:, k, c * 128:(c + 1) * 128], ht[:, k, :],
                        start=(k == 0), stop=(k == NCH - 1)), [wa, ht], [ps])
                pcq.append(ps)
            rs_q = small_rstd(pcq[0:2], 256)
            for c in range(2):
                P.op("dve", lambda e, c=c, rs_q=rs_q, ps=pcq[c]: e.scalar_tensor_tensor(
                    cqn[:, c, :], ps[:, :], qg[:, c:c + 1], rs_q[:], ALU.mult, ALU.mult), [pcq[c], qg, rs_q], [cqn])
            rs_kv = small_rstd(pcq[2:3], 128)
            P.op("dve", lambda e, rs_kv=rs_kv, ps=pcq[2]: e.scalar_tensor_tensor(
                ckvn[:], ps[:, :], kvg[:, 0:1], rs_kv[:], ALU.mult, ALU.mult), [pcq[2], kvg, rs_kv], [ckvn])
            pab = []
            for ab in range(2):
                ps = self.psum.next()
                for k in range(NCH):
                    P.op("pe", lambda e, ps=ps, k=k, ab=ab, ht=ht: e.matmul(
                        ps[:, :], wkr[:, k, ab * 128:(ab + 1) * 128], ht[:, k, :],
                        start=(k == 0), stop=(k == NCH - 1)), [wkr, ht], [ps])
                pab.append(ps)
            rotate(pab[0], pab[1], lambda: kr_t[:], [kr_t], t0)
            P.dma("sp", lambda e, t0=t0: e.dma_start(out=kr_d[:, t0:t0 + n], in_=kr_t[:]), [kr_t], [tk["kr"][ti]], kr_t)
            for h in range(8):
                ps = self.psum.next()
                for kc in range(2):
                    P.op("pe", lambda e, ps=ps, kc=kc, h=h: e.matmul(
                        ps[:, :], wqn[:, kc, h, :], cqn[:, kc, :], start=(kc == 0), stop=(kc == 1)), [wqn, cqn], [ps])
                if h % 2 == 0:
                    P.op("act", lambda e, ps=ps, h=h: e.copy(qn_t[:, h, :], ps[:, :]), [ps], [qn_t])
                else:
                    P.op("dve", lambda e, ps=ps, h=h: e.tensor_copy(qn_t[:, h, :], ps[:, :]), [ps], [qn_t])
            P.dma("sp", lambda e, t0=t0: e.dma_start(out=qn_d[:, :, t0:t0 + n].rearrange("h p t -> p h t"), in_=qn_t[:]),
                  [qn_t], [tk["qn"][ti]], qn_t)
            for pr in range(4):
                pab = []
                for W in (wqrA, wqrB):
                    ps = self.psum.next()
                    for kc in range(2):
                        P.op("pe", lambda e, ps=ps, kc=kc, pr=pr, W=W: e.matmul(
                            ps[:, :], W[:, kc, 2 * pr:2 * pr + 2, :].rearrange("p h e -> p (h e)"), cqn[:, kc, :],
                            start=(kc == 0), stop=(kc == 1)), [W, cqn], [ps])
                    pab.append(ps)
                rotate(pab[0], pab[1], lambda pr=pr: qr_t[:, pr, :], [qr_t], t0)
            P.dma("sp", lambda e, t0=t0: e.dma_start(out=qr_d[:, :, t0:t0 + n].rearrange("h p t -> p h t"), in_=qr_t[:]),
                  [qr_t], [tk["qr"][ti]], qr_t)
            for h in range(8):
                ps = self.psum.next()
                P.op("pe", lambda e, ps=ps, h=h: e.matmul(ps[:, :], wkn[:, h, :], ckvn[:], start=True, stop=True),
                     [wkn, ckvn], [ps])
                if h % 2 == 0:
                    P.op("act", lambda e, ps=ps, h=h: e.copy(kn_t[:, h, :], ps[:, :]), [ps], [kn_t])
                else:
                    P.op("dve", lambda e, ps=ps, h=h: e.tensor_copy(kn_t[:, h, :], ps[:, :]), [ps], [kn_t])
            P.dma("sp", lambda e, t0=t0: e.dma_start(out=kn_d[:, :, t0:t0 + n].rearrange("h p t -> p h t"), in_=kn_t[:]),
                  [kn_t], [tk["kn"][ti]], kn_t)
            for s in range(4):
                for hh in range(2):
                    ps = self.psum.next()
                    P.op("pe", lambda e, ps=ps, s=s, hh=hh: e.matmul(
                        ps[:, :], ckvn[:, s * 128:(s + 1) * 128],
                        wv[:, 4 * hh:4 * hh + 4, :].rearrange("p h e -> p (h e)"), start=True, stop=True),
                        [wv, ckvn], [ps])
                    if hh == 0:
                        P.op("act", lambda e, ps=ps, s=s: e.copy(v_t[:, s, 0:512], ps[:, :]), [ps], [v_t])
                    else:
                        P.op("dve", lambda e, ps=ps, s=s: e.tensor_copy(v_t[:, s, 512:1024], ps[:, :]), [ps], [v_t])
            P.dma("sp", lambda e, t0=t0: e.dma_start(out=v_d[t0:t0 + n, :].rearrange("(s p) f -> p s f", p=128), in_=v_t[:]),
                  [v_t], [tk["v"][ti]], v_t)
        P.end_scope()

        P.begin_scope()
        NKB = T // 128
        qn_r = P.ring("qn", [128, T], BF16, 2)
        kn_r = P.ring("kn", [128, T], BF16, 2)
        qr_r = P.ring("qr", [128, T], BF16, 2)
        v_r = P.ring("v", [128, NKB, 128], BF16, 2)
        ob_r = P.ring("ob", [128, T], BF16, 2)
        kr2 = P.sb("kr2", [128, T], BF16)
        tri_b = P.sb("tri_b", [128, 128], BF16)
        P.op("dve", lambda e: e.tensor_copy(tri_b[:], self.tri_f[:]), [self.tri_f], [tri_b])
        pt_r = P.ring("pt", [128, 512], BF16, 4)
        rc_r = P.ring("rc", [128, 512], F32, 2)
        acc_r = Ring([(self.psum.bufs[0], self.psum.bufs[1]), (self.psum.bufs[2], self.psum.bufs[3])])
        s_r = Ring(self.psum.bufs[4:7])
        P.dma("sp", lambda e: e.dma_start(out=kr2[:], in_=kr_d[:, :]), tk["kr"], [kr2], kr2)
        SCALE = 192.0 ** -0.5
        qr = None
        for h in range(8):
            qn, kn, v, ob = qn_r.next(), kn_r.next(), v_r.next(), ob_r.next()
            P.dma("sp", lambda e, qn=qn, h=h: e.dma_start(out=qn[:], in_=qn_d[h, :, :]), tk["qn"], [qn], qn)
            P.dma("sp", lambda e, kn=kn, h=h: e.dma_start(out=kn[:], in_=kn_d[h, :, :]), tk["kn"], [kn], kn)
            P.dma("sp", lambda e, v=v, h=h: [e.dma_start(
                out=v[:, q * (NKB // 4):(q + 1) * (NKB // 4), :],
                in_=v_d[q * (T // 4):(q + 1) * (T // 4), h * 128:(h + 1) * 128].rearrange("(n p) e -> p n e", p=128))
                for q in range(4)], tk["v"], [v], v)
            if h % 2 == 0:
                qr = qr_r.next()
                P.dma("sp", lambda e, qr=qr, h=h: e.dma_start(out=qr[:], in_=qr_d[h // 2, :, :]), tk["qr"], [qr], qr)
            r0 = 64 * (h % 2)
            for qi in range(T // 512):
                O_ps, R_ps = acc_r.next()
                nkb = 4 * (qi + 1)
                q0 = qi * 512
                pend = None

                def flush(pend):
                    kb, c0, pt = pend
                    P.op("pe", lambda e, O_ps=O_ps, kb=kb, c0=c0, pt=pt, v=v, nkb=nkb: e.matmul(
                        O_ps[:, c0:512], v[:, kb, :], pt[:, c0:512], start=(kb == 0), stop=(kb == nkb - 1)),
                        [v, pt], [O_ps])
                    P.op("pe", lambda e, R_ps=R_ps, kb=kb, c0=c0, pt=pt, nkb=nkb: e.matmul(
                        R_ps[:, c0:512], self.ones_b[:], pt[:, c0:512], start=(kb == 0), stop=(kb == nkb - 1)),
                        [self.ones_b, pt], [R_ps])

                for kb in range(nkb):
                    j = kb - 4 * qi
                    c0 = max(0, j) * 128
                    S_ps = s_r.next()
                    P.op("pe", lambda e, S_ps=S_ps, kb=kb, c0=c0, q0=q0, kn=kn, qn=qn: e.matmul(
                        S_ps[:, c0:512], kn[:, kb * 128:(kb + 1) * 128], qn[:, q0 + c0:q0 + 512],
                        start=True, stop=False), [kn, qn], [S_ps])
                    P.op("pe", lambda e, S_ps=S_ps, kb=kb, c0=c0, q0=q0, qr=qr, r0=r0: e.matmul(
                        S_ps[:, c0:512], kr2[r0:r0 + 64, kb * 128:(kb + 1) * 128], qr[r0:r0 + 64, q0 + c0:q0 + 512],
                        start=False, stop=True), [kr2, qr], [S_ps])
                    if pend is not None:
                        flush(pend)
                    pt = pt_r.next()
                    P.op("act", lambda e, S_ps=S_ps, pt=pt, c0=c0: e.activation(
                        pt[:, c0:512], S_ps[:, c0:512], AF.Exp, scale=SCALE), [S_ps], [pt])
                    if j >= 0:
                        P.op("pool", lambda e, pt=pt, c0=c0: e.tensor_tensor(
                            pt[:, c0:c0 + 128], pt[:, c0:c0 + 128], tri_b[:], ALU.mult), [pt, tri_b], [pt])
                    pend = (kb, c0, pt)
                flush(pend)
                rc = rc_r.next()
                P.op("dve", lambda e, rc=rc, R_ps=R_ps: e.reciprocal(rc[:], R_ps[:, :]), [R_ps], [rc])
                P.op("dve", lambda e, rc=rc, O_ps=O_ps, ob=ob, q0=q0: e.tensor_tensor(
                    ob[:, q0:q0 + 512], O_ps[:, :], rc[:], ALU.mult), [O_ps, rc], [ob])
            P.dma("sp", lambda e, ob=ob, h=h: e.dma_start(out=oT_d[h, :, :], in_=ob[:]), [ob], [otk[h]], ob)
        P.end_scope()

        P.begin_scope()
        self.work(n)
        wo = P.sb("wo", [128, 8, D], BF16)
        P.dma("pool", lambda e: e.dma_start(out=wo[:], in_=w_o.rearrange("(h p) n -> p h n", p=128)), [], [wo], wo)
        ot_r = P.ring("ot", [128, 8, n], BF16, 2)
        for ti in range(NTL):
            t0 = ti * n
            xt = self.load_xt(t0, n)
            ot = ot_r.next()
            P.dma("sp", lambda e, ot=ot, t0=t0: e.dma_start(out=ot[:], in_=oT_d[:, :, t0:t0 + n].rearrange("h p t -> p h t")),
                  otk, [ot], ot)
            for c in range(NCH):
                ps = self.psum.next()
                for h in range(8):
                    P.op("pe", lambda e, ps=ps, h=h, c=c, ot=ot: e.matmul(
                        ps[:, :], wo[:, h, c * 128:(c + 1) * 128], ot[:, h, :], start=(h == 0), stop=(h == 7)),
                        [wo, ot], [ps])
                P.op("dve", lambda e, ps=ps, c=c, xt=xt: e.scalar_tensor_tensor(
                    xt[:, c, :], ps[:, :], GA1[:, c:c + 1], xt[:, c, :], ALU.mult, ALU.add), [ps, xt, GA1], [xt])
            self.store_xt(xt, t0, n)
        P.end_scope()

    def phase_hg(self, l):
        P, T = self.P, self.T
        n = 256
        NS = n // 128
        NC = n // 64
        w_in, ng_d, w_o = (self.L(k, 0) for k in ("hg_w_in", "hg_norm_g", "hg_w_o"))
        hg_lb = self.din("hg_lb", [DEPTH, NCH, 128])
        k_mask2 = self.din("k_mask2", [128, 128])
        ng = self.load_cols("hgng", ng_d[:, :], 1)
        lbc = [self.load_cols(f"hglb{d}", hg_lb[d, :, :], NCH) for d in range(DEPTH)]
        lb = P.sb("hg_lbv", [128, NCH], F32)
        oml = P.sb("hg_oml", [128, NCH], F32)
        esum = P.sb("hg_esum", [128, NCH], F32)
        enum_ = P.sb("hg_enum", [128, NCH], F32)
        for d in range(DEPTH):
            P.op("act", lambda e, d=d: e.activation(lbc[d][:], lbc[d][:], AF.Exp), [lbc[d]], [lbc[d]])
        P.op("dve", lambda e: e.tensor_tensor(esum[:], lbc[0][:], lbc[1][:], ALU.add), [lbc[0], lbc[1]], [esum])
        P.op("dve", lambda e: e.tensor_tensor(esum[:], esum[:], lbc[2][:], ALU.add), [esum, lbc[2]], [esum])
        P.op("dve", lambda e: e.tensor_tensor(esum[:], esum[:], lbc[3][:], ALU.add), [esum, lbc[3]], [esum])
        P.op("dve", lambda e: e.memset(enum_[:], 0.0), [], [enum_])
        for d in range(1, l + 1):
            P.op("dve", lambda e, d=d: e.tensor_tensor(enum_[:], enum_[:], lbc[d][:], ALU.add), [enum_, lbc[d]], [enum_])
        P.op("dve", lambda e: e.reciprocal(esum[:], esum[:]), [esum], [esum])
        P.op("dve", lambda e: e.tensor_tensor(lb[:], enum_[:], esum[:], ALU.mult), [enum_, esum], [lb])
        P.op("dve", lambda e: e.tensor_scalar(oml[:], lb[:], -1.0, 1.0, ALU.mult, ALU.add), [lb], [oml])
        G1, SH1, GA1 = self.G1[l], self.SH1[l], self.GA1[l]

        P.begin_scope()
        self.work(n, nxt=2)
        win = P.sb("hwin", [128, NCH, 4096], BF16)
        wo = P.sb("hwo", [128, 8, D], BF16)
        for h in range(4):
            src = w_in[:, h * 1024:(h + 1) * 1024].rearrange("(k p) n -> p k n", p=128)
            P.dma("pool", lambda e, src=src, h=h: e.dma_start(out=win[:, :, h * 1024:(h + 1) * 1024], in_=src),
                  [], [win], win)
        P.dma("pool", lambda e: e.dma_start(out=wo[:], in_=w_o.rearrange("(h p) n -> p h n", p=128)), [], [wo], wo)
        mask2 = P.sb("mask2", [128, 128], F32)
        P.dma("sp", lambda e: e.dma_start(out=mask2[:], in_=k_mask2[:, :]), [], [mask2], mask2)
        ident_b = P.sb("ident_b", [128, 128], BF16)
        P.op("dve", lambda e: e.tensor_copy(ident_b[:], self.ident[:]), [self.ident], [ident_b])
        rmask = P.sb("rmask", [128, n], F32)
        P.op("dve", lambda e: e.memset(rmask[:], 1.0), [], [rmask])
        for c in range(NC):
            P.op("dve", lambda e, c=c: e.memset(rmask[:, 64 * c:64 * c + 1], 0.0), [], [rmask])
        S = P.sb("hgS", [128, 8, 128], F32)
        S_bf = [P.sb(f"hgSb{g}", [128, 4, 128], BF16) for g in range(2)]
        P.op("dve", lambda e: e.memset(S[:], 0.0), [], [S])
        for g in range(2):
            P.op("pool", lambda e, g=g: e.memset(S_bf[g][:], 0.0), [], [S_bf[g]])
        ebs = P.ring("hgebA", [128, NC, 8], F32, 2)
        Qt = [P.sb(f"hgQ{h}", [128, n], BF16) for h in range(8)]
        Kt = [P.sb(f"hgK{h}", [128, n], BF16) for h in range(8)]
        Ktok = [P.sb(f"hgKt{h}", [128, NS, 128], BF16) for h in range(8)]
        sgate = [P.sb(f"hgsg{h}", [128, n], BF16) for h in range(8)]
        V_t = P.sb("hgV", [128, NS, 1024], BF16)
        o_f = P.sb("hgo", [128, 8, n], F32)
        ogT = P.sb("hgog", [128, 8, n], BF16)
        tA_r = P.ring("hgtA", [128, n], F32, 6)
        tB_r = P.ring("hgtB", [128, n], F32, 8)
        tC_r = P.ring("hgtC", [128, n], F32, 6)
        tD_r = P.ring("hgtD", [128, n], F32, 5)
        tr = P.ring("hgt", [128, n], F32, 3)
        at_r = P.ring("hgAT", [128, 128], BF16, 8)
        st_r = P.ring("hgSt", [128, 512], F32, 3)
        sqh = P.ring("hgsq", [128, n], BF16, 2)

        hts = P.ring("hght", [128, NCH, n], BF16, 2)

        def prep(ti):
            xt = self.load_xt(ti * n, n)
            return xt, self.norm_mod(xt, n, G1, SH1, ht=hts.next())

        nxt_prep = prep(0)
        for ti in range(T // n):
            t0 = ti * n
            xt, ht = nxt_prep
            ebA = ebs.next()
            for s in range(NS):
                for hh in range(2):
                    ps = self.psum.next()
                    for k in range(NCH):
                        P.op("pe", lambda e, ps=ps, k=k, s=s, hh=hh, ht=ht: e.matmul(
                            ps[:, :], ht[:, k, s * 128:(s + 1) * 128], win[:, k, 2048 + hh * 512:2048 + (hh + 1) * 512],
                            start=(k == 0), stop=(k == NCH - 1)), [win, ht], [ps])
                    if hh == 0:
                        P.op("act", lambda e, ps=ps, s=s: e.copy(V_t[:, s, 0:512], ps[:, :]), [ps], [V_t])
                    else:
                        P.op("dve", lambda e, ps=ps, s=s: e.tensor_copy(V_t[:, s, 512:1024], ps[:, :]), [ps], [V_t])
            st = {}

            def proj(off, h, ht=ht):
                ps = self.psum.next()
                for k in range(NCH):
                    P.op("pe", lambda e, ps=ps, k=k: e.matmul(
                        ps[:, 0:n], win[:, k, off + h * 128:off + (h + 1) * 128], ht[:, k, :],
                        start=(k == 0), stop=(k == NCH - 1)), [win, ht], [ps])
                return ps

            def h0(h):
                st[h] = {"fl": proj(1024, h), "g": proj(3072, h)}

            def h1(h):
                d = st[h]
                d["A"], d["B"] = tA_r.next(), tB_r.next()
                P.op("act", lambda e: e.activation(d["A"][:], d["fl"][:, 0:n], AF.Exp, scale=-1.0), [d["fl"]], [d["A"]])
                P.op("act", lambda e: e.activation(d["B"][:], d["g"][:, 0:n], AF.Exp, scale=-1.0), [d["g"]], [d["B"]])
                P.op("act", lambda e: e.copy(sgate[h][:], d["g"][:, 0:n]), [d["g"]], [sgate[h]])

            def h1b(h):
                d = st[h]
                P.op("act", lambda e: e.activation(d["A"][:], d["A"][:], AF.Ln, bias=1.0), [d["A"]], [d["A"]])
                P.op("act", lambda e: e.activation(d["B"][:], d["B"][:], AF.Ln, bias=1.0), [d["B"]], [d["B"]])

            def h1c(h):
                d = st[h]
                P.op("act", lambda e: e.activation(d["A"][:], d["A"][:], AF.Exp, scale=-1.0), [d["A"]], [d["A"]])
                P.op("act", lambda e: e.activation(d["B"][:], d["B"][:], AF.Exp, scale=-1.0), [d["B"]], [d["B"]])

            def h2(h):
                d = st[h]
                A, B = d["A"], d["B"]
                P.op("dve", lambda e: e.tensor_scalar(A[:], A[:], oml[:, h:h + 1], lb[:, h:h + 1], ALU.mult, ALU.add),
                     [A, oml, lb], [A])
                P.op("dve", lambda e: e.tensor_tensor(sgate[h][:], sgate[h][:], B[:], ALU.mult), [sgate[h], B], [sgate[h]])

            def h3(h):
                d = st[h]
                d["C"] = tC_r.next()
                P.op("act", lambda e: e.activation(d["C"][:], d["A"][:], AF.Ln), [d["A"]], [d["C"]])
                P.op("pool", lambda e: e.tensor_scalar(d["B"][:], d["A"][:], -1.0, 1.0, ALU.mult, ALU.add), [d["A"]], [d["B"]])

            def h4(h):
                d = st[h]
                d["D"] = tD_r.next()
                P.op("dve", lambda e: e.tensor_tensor_scan(d["D"][:], rmask[:], d["C"][:], 0.0, ALU.mult, ALU.add),
                     [rmask, d["C"]], [d["D"]])

            def h5(h):
                d = st[h]
                P.op("act", lambda e: e.activation(d["C"][:], d["D"][:], AF.Exp), [d["D"]], [d["C"]])
                P.op("act", lambda e: e.activation(d["D"][:], d["D"][:], AF.Exp, scale=-1.0), [d["D"]], [d["D"]])

            def h6(h):
                st[h]["q"] = proj(0, h)

            def h7(h):
                d = st[h]
                P.op("dve", lambda e: e.tensor_tensor(Qt[h][:], d["q"][:, 0:n], d["C"][:], ALU.mult), [d["q"], d["C"]], [Qt[h]])
                P.op("dve", lambda e: e.tensor_copy(ebA[:, :, h], d["C"][:].rearrange("p (c t) -> p c t", t=64)[:, :, 63]),
                     [d["C"]], [ebA])
                P.op("pool", lambda e: e.tensor_tensor(Kt[h][:], d["B"][:], d["D"][:], ALU.mult), [d["B"], d["D"]], [Kt[h]])

            def h8(h):
                pb = self.psb
                hb = (h % 2) * 512
                for s in range(NS):
                    P.op("pe", lambda e, s=s: e.transpose(
                        pb[:, hb + s * 128:hb + (s + 1) * 128], Kt[h][:, s * 128:(s + 1) * 128], ident_b[:]),
                        [Kt[h], ident_b], [pb])
                P.op("act", lambda e: e.copy(Ktok[h][:].rearrange("p s d -> p (s d)"), pb[:, hb:hb + NS * 128]),
                     [pb], [Ktok[h]])

            self.pipeline(list(range(8)), [h0, h1, h1b, h1c, h2, h3, h4, h5, h6, h7, h8])
            if ti + 1 < T // n:
                nxt_prep = prep(ti + 1)

            for s in range(NS):
                cols = slice(s * 128, (s + 1) * 128)
                bA = [self.psum.next(), self.psum.next()]
                for h in range(8):
                    P.op("pe", lambda e, h=h, cols=cols, b=bA[h // 4]: e.matmul(
                        b[:, (h % 4) * 128:(h % 4 + 1) * 128], Kt[h][:, cols], Qt[h][:, cols], start=True, stop=True),
                        [Kt[h], Qt[h]], [bA[h // 4]])
                ats = []
                for h in range(8):
                    at = at_r.next()
                    ats.append(at)
                    P.op("dve", lambda e, h=h, at=at, b=bA[h // 4]: e.tensor_tensor(
                        at[:], b[:, (h % 4) * 128:(h % 4 + 1) * 128], mask2[:], ALU.mult), [bA[h // 4], mask2], [at])
                bO = [self.psum.next(), self.psum.next()]
                for h in range(8):
                    P.op("pe", lambda e, h=h, at=ats[h], s=s, b=bO[h // 4]: e.matmul(
                        b[:, (h % 4) * 128:(h % 4 + 1) * 128], V_t[:, s, h * 128:(h + 1) * 128], at[:],
                        start=(h % 4 == 0), stop=False, skip_group_check=True), [V_t, ats[h]], [bO[h // 4]])
                for half in range(2):
                    po = 64 * half
                    cc = slice(s * 128 + po, s * 128 + po + 64)
                    c = 2 * s + half
                    bS = [self.psum.next(), self.psum.next()]
                    for h in range(8):
                        P.op("pe", lambda e, h=h, cc=cc, po=po, half=half, b=bO[h // 4]: e.matmul(
                            b[:, (h % 4) * 128 + po:(h % 4) * 128 + po + 64], S_bf[h // 4][:, h % 4, :], Qt[h][:, cc],
                            start=False, stop=(half == 1), skip_group_check=True), [S_bf[h // 4], Qt[h]], [bO[h // 4]])
                        P.op("pe", lambda e, h=h, s=s, po=po, b=bS[h // 4]: e.matmul(
                            b[:, (h % 4) * 128:(h % 4 + 1) * 128], Ktok[h][po:po + 64, s, :],
                            V_t[po:po + 64, s, h * 128:(h + 1) * 128], start=True, stop=True),
                            [Ktok[h], V_t], [bS[h // 4]])
                    for g in range(2):
                        stt = st_r.next()
                        P.op("dve", lambda e, g=g, stt=stt, b=bS[g]: e.tensor_tensor(
                            stt[:], b[:, :], S[:, 4 * g:4 * g + 4, :].rearrange("p h d -> p (h d)"), ALU.add),
                            [bS[g], S], [stt])
                        P.op("dve", lambda e, g=g, stt=stt, c=c: e.tensor_tensor(
                            S[:, 4 * g:4 * g + 4, :], stt[:].rearrange("p (h d) -> p h d", h=4),
                            ebA[:, c, 4 * g:4 * g + 4].rearrange("p (h o) -> p h o", o=1).to_broadcast([128, 4, 128]),
                            ALU.mult), [stt, ebA], [S])
                        P.op("act", lambda e, g=g: e.copy(S_bf[g][:], S[:, 4 * g:4 * g + 4, :]), [S], [S_bf[g]])
                for h in range(8):
                    P.op("act", lambda e, h=h, cols=cols, b=bO[h // 4]: e.copy(
                        o_f[:, h, cols], b[:, (h % 4) * 128:(h % 4 + 1) * 128]), [bO[h // 4]], [o_f])
            for h in range(8):
                sq = sqh.next()
                P.op("pool", lambda e, sq=sq, h=h: e.tensor_tensor(sq[:], o_f[:, h, :], o_f[:, h, :], ALU.mult), [o_f], [sq])
                ps = self.psum.next()
                P.op("pe", lambda e, ps=ps, sq=sq: e.matmul(ps[:, 0:n], self.ones_b[:], sq[:], start=True, stop=True),
                     [sq, self.ones_b], [ps])
                t1 = tr.next()
                P.op("dve", lambda e, t1=t1, ps=ps: e.tensor_scalar(t1[:], ps[:, 0:n], 1.0 / 128, EPS, ALU.mult, ALU.add),
                     [ps], [t1])
                P.op("act", lambda e, t1=t1: e.activation(t1[:], t1[:], AF.Ln), [t1], [t1])
                P.op("act", lambda e, t1=t1: e.activation(t1[:], t1[:], AF.Exp, scale=-0.5), [t1], [t1])
                P.op("dve", lambda e, t1=t1, h=h: e.scalar_tensor_tensor(
                    t1[:], o_f[:, h, :], ng[:, 0:1], t1[:], ALU.mult, ALU.mult), [o_f, ng, t1], [t1])
                P.op("pool", lambda e, t1=t1, h=h: e.tensor_tensor(ogT[:, h, :], t1[:], sgate[h][:], ALU.mult),
                     [t1, sgate[h]], [ogT])
            for c in range(NCH):
                ps = self.psum.next()
                for h in range(8):
                    P.op("pe", lambda e, ps=ps, h=h, c=c: e.matmul(
                        ps[:, 0:n], wo[:, h, c * 128:(c + 1) * 128], ogT[:, h, :], start=(h == 0), stop=(h == 7)),
                        [wo, ogT], [ps])
                P.op("dve", lambda e, ps=ps, c=c, xt=xt: e.scalar_tensor_tensor(
                    xt[:, c, 0:n], ps[:, 0:n], GA1[:, c:c + 1], xt[:, c, 0:n], ALU.mult, ALU.add), [ps, xt, GA1], [xt])
            self.store_xt(xt, t0, n)
        P.end_scope()


FULL_PHASES = [("gm", 0), ("ffn", 0), ("mla", 1), ("ffn", 1), ("hg", 2), ("ffn", 2), ("gm", 3), ("ffn", 3)]


def make_consts():
    ident = np.eye(128, dtype=np.float32)
    tri = np.triu(np.ones((128, 128), dtype=np.float32))
    j = np.arange(128) % 32
    invf = np.zeros((2, 128), dtype=np.float32)
    invf[0] = (10000.0 ** (-(2.0 * j) / 64.0)).astype(np.float32)
    invf[1] = np.where((np.arange(128) // 32) % 2 == 0, -1.0, 1.0)
    sel = np.zeros((8, 1024), dtype=np.float32)
    for g in range(8):
        sel[g, g * 128:(g + 1) * 128] = 1.0
    mask2 = np.zeros((128, 128), dtype=np.float32)
    mask2[0:64, 0:64] = tri[0:64, 0:64]
    mask2[64:128, 64:128] = tri[0:64, 0:64]
    return {"k_ident": ident, "k_tri": tri, "k_invf": invf, "k_sel": sel, "k_mask2": mask2}


LAYER_SHAPES = {
    "ada_w": [D, 6 * D], "ada_b": [48, 128], "mix_norm_g": [NCH, 128], "ffn_norm_g": [NCH, 128],
    "gm_w_in": [D, 4096], "gm_ln_g": [16, 128], "gm_ln_b": [16, 128], "gm_w_s": [8, 128, 128],
    "gm_b_s": [8, 128], "gm_w_out": [GM_INNER, D],
    "mla_w_a": [D, 448], "mla_q_norm_g": [2, 128], "mla_kv_norm_g": [1, 128], "mla_w_qb": [256, 1536],
    "mla_w_kvb": [128, 2048], "mla_w_o": [1024, 1024],
    "hg_w_in": [D, 4096], "hg_norm_g": [1, 128], "hg_w_o": [1024, 1024],
    "ff_w_up": [D, 2 * FF], "ff_conv_w": [3, 44, 128], "ff_conv_b": [44, 128], "ff_w_down": [FF, D],
}


def core_inputs(inp, names, b, T, consts):
    f = np.ascontiguousarray
    m = {}
    for nm in names:
        if nm in consts:
            m[nm] = consts[nm]
        elif "__" in nm:
            base, idx = nm.split("__")
            m[nm] = f(np.asarray(inp[base][int(idx)], dtype=np.float32).reshape(LAYER_SHAPES[base]))
        elif nm == "x":
            m[nm] = f(inp["x"][b, :T])
        elif nm == "c":
            m[nm] = f(inp["c"][b].reshape(NCH, 128))
        elif nm == "positions":
            m[nm] = f(inp["positions"][b, :T].reshape(1, T).astype(np.int32))
        elif nm == "final_g":
            m[nm] = f(inp["final_g"].reshape(NCH, 128))
        elif nm == "hg_lb":
            m[nm] = f(inp["hg_lb"].reshape(DEPTH, NCH, 128))
        else:
            raise KeyError(nm)
    return m


def run(inp, T, phases, trace=False, debug=False):
    bld = Builder(T, phases)
    bld.debug = debug
    nc = bld.build()
    names = list(bld.decl.keys())
    B = inp["x"].shape[0]
    consts = make_consts()
    per_b = [core_inputs(inp, names, b, T, consts) for b in range(B)]
    in_maps = [per_b[c % B] for c in range(N_CORES)]
    res = run_bass_kernel_spmd(nc, in_maps, core_ids=list(range(N_CORES)), trace=trace)
    out = np.stack([res.results[b]["y"] for b in range(B)], axis=0)
    return out, res


def kernel(**inputs):
    inp = {k: np.asarray(v) for k, v in inputs.items()}
    out, _ = run(inp, inp["x"].shape[1], FULL_PHASES)
    return out.astype(np.float32)
```

```python
import contextlib
import numpy as np
import concourse.bass as bass
import concourse.mybir as mybir
from concourse.bass_utils import run_bass_kernel_spmd

F32 = mybir.dt.float32
BF16 = mybir.dt.bfloat16
I32 = mybir.dt.int32
ALU = mybir.AluOpType
AF = mybir.ActivationFunctionType

D = 1024
NCH = 8
DEPTH = 4
FF = 2816
NFC = 22
EPS = 1e-6
GM_INNER = 2048
TT = 512
TF = 256
N_CORES = 8


class Tok:
    __slots__ = ("name", "last_w", "readers", "sem", "cnt")

    def __init__(self, name):
        self.name = name
        self.last_w = None
        self.readers = {}
        self.sem = None
        self.cnt = 0


class Op:
    __slots__ = ("engine", "fn", "reads", "writes", "dma", "chain", "deps", "signal", "ev")

    def __init__(self, engine, fn, reads, writes, dma, chain):
        self.engine = engine
        self.fn = fn
        self.reads = reads
        self.writes = writes
        self.dma = dma
        self.chain = chain
        self.deps = []
        self.signal = False
        self.ev = None


def _tok(x):
    return x.tok if isinstance(x, Buf) else x


class Ring:
    def __init__(self, bufs):
        self.bufs = bufs
        self.i = 0

    def next(self):
        b = self.bufs[self.i % len(self.bufs)]
        self.i += 1
        return b


class Buf:
    def __init__(self, name, shape, dt, psum):
        self.name, self.shape, self.dt, self.psum = name, list(shape), dt, psum
        self.t = None
        self.tok = Tok(name)

    def __getitem__(self, k):
        return self.t[k]


class Prog:
    def __init__(self, nc):
        self.nc = nc
        self.ops = []
        self.eng = {"pe": nc.tensor, "act": nc.scalar, "dve": nc.vector,
                    "pool": nc.gpsimd, "sp": nc.sync}
        self.in_scope = False

    def _mk(self, name, shape, dt, psum):
        self.uid = getattr(self, "uid", 0) + 1
        b = Buf(f"{name}_{self.uid}", shape, dt, psum)
        self.ops.append(("alloc", b, self.in_scope))
        return b

    def sb(self, name, shape, dt):
        return self._mk(name, shape, dt, False)

    def ps(self, name, shape, dt=F32):
        return self._mk(name, shape, dt, True)

    def ring(self, name, shape, dt, n, psum=False):
        return Ring([self._mk(f"{name}{i}", shape, dt, psum) for i in range(n)])

    def begin_scope(self):
        assert not self.in_scope
        self.in_scope = True
        self.ops.append(("begin",))

    def end_scope(self):
        assert self.in_scope
        self.in_scope = False
        self.ops.append(("end",))

    def op(self, engine, fn, reads=(), writes=(), dma=False, chain=None):
        o = Op(engine, fn, [_tok(r) for r in reads], [_tok(w) for w in writes], dma,
               _tok(chain) if chain is not None else None)
        self.ops.append(o)
        return o

    def dma(self, engine, fn, reads, writes, chain):
        return self.op(engine, fn, reads, writes, dma=True, chain=chain)

    def finalize(self):
        order = {}
        pending = {}
        scope_toks = []
        for idx, op in enumerate(self.ops):
            if isinstance(op, tuple):
                if op[0] == "alloc":
                    op[1].tok.readers = dict(pending)
                    if op[2]:
                        scope_toks.append(op[1].tok)
                elif op[0] == "begin":
                    scope_toks = []
                elif op[0] == "end":
                    for t in scope_toks:
                        cands = list(t.readers.items())
                        if t.last_w is not None:
                            lw = t.last_w
                            cands.append(((("dma", id(lw.chain)) if lw.dma else lw.engine), lw))
                        for k, o in cands:
                            if k not in pending or order[id(pending[k])] < order[id(o)]:
                                pending[k] = o
                    scope_toks = []
                continue
            order[id(op)] = idx
            deps = []
            for t in op.reads:
                if t.last_w is not None:
                    deps.append((t.last_w, True))
            for t in op.writes:
                if t.last_w is not None:
                    deps.append((t.last_w, False))
                deps.extend((r, False) for r in t.readers.values())
            need = []
            seen = set()
            for d, raw in deps:
                if d is op:
                    continue
                if (not d.dma) and (not op.dma) and d.engine == op.engine:
                    if not raw or op.engine == "pe":
                        continue
                if id(d) in seen:
                    continue
                seen.add(id(d))
                need.append(d)
                d.signal = True
            op.deps = need
            key = ("dma", id(op.chain)) if op.dma else op.engine
            for t in op.reads:
                t.readers[key] = op
            for t in op.writes:
                t.last_w = op
                t.readers = {}
        es_global = contextlib.ExitStack()
        es_scope = None

        def newsem(name):
            return es_global.enter_context(self.nc.semaphore(name))

        engsem = {e: newsem(f"sem_{e}") for e in ("pe", "act", "dve", "pool")}
        engcnt = {e: 0 for e in engsem}
        waited = {e: {} for e in self.eng}
        for op in self.ops:
            if isinstance(op, tuple):
                if op[0] == "alloc":
                    b = op[1]
                    st = es_scope if op[2] else es_global
                    f = self.nc.psum_tensor if b.psum else self.nc.sbuf_tensor
                    b.t = st.enter_context(f(b.name, b.shape, b.dt))
                elif op[0] == "begin":
                    es_scope = contextlib.ExitStack()
                elif op[0] == "end":
                    es_scope.close()
                    es_scope = None
                continue
            E = self.eng[op.engine]
            w = waited[op.engine]
            for d in op.deps:
                sem, val = d.ev
                k = id(sem)
                if w.get(k, 0) >= val:
                    continue
                E.wait_ge(sem, val)
                w[k] = val
            insts = op.fn(E) if op.fn is not None else None
            if op.dma:
                ch = op.chain
                if ch.sem is None:
                    ch.sem = newsem(f"dsem_{ch.name}")
                if not isinstance(insts, (list, tuple)):
                    insts = [insts]
                for ins in insts:
                    ins.then_inc(ch.sem, 16)
                    ch.cnt += 1
                op.ev = (ch.sem, 16 * ch.cnt)
            elif op.signal:
                assert insts is not None, "signalling op must emit an instruction"
                if isinstance(insts, (list, tuple)):
                    insts = insts[-1]
                engcnt[op.engine] += 1
                insts.then_inc(engsem[op.engine], 1)
                op.ev = (engsem[op.engine], engcnt[op.engine])
        es_global.close()


class Builder:
    def __init__(self, T, phases):
        self.T = T
        self.phases = phases
        nc = bass.Bass("TRN2", target_bir_lowering=False)
        self.nc = nc
        self.P = Prog(nc)
        self.decl = {}
        self.dbg_toks = []

    def din(self, name, shape, dt=F32):
        if name not in self.decl:
            self.decl[name] = self.nc.dram_tensor(name, list(shape), dt, kind="ExternalInput").ap()
        return self.decl[name]

    def dump(self, name, buf, shape, dt=F32, sl=None):
        if not getattr(self, "debug", False):
            return
        ap = self.nc.dram_tensor(name, list(shape), dt, kind="ExternalOutput").ap()
        tk = Tok("dbg" + name)
        self.dbg_toks.append(tk)
        self.P.dma("sp", lambda e: e.dma_start(out=ap, in_=(sl(buf) if sl else buf[:])), [buf], [tk], buf)

    def L(self, name, idx):
        return self.din(f"{name}__{idx}", LAYER_SHAPES[name])

    def build(self):
        nc, P, T = self.nc, self.P, self.T
        x = self.din("x", [T, D])
        cvec = self.din("c", [NCH, 128])
        final_g = self.din("final_g", [NCH, 128])
        k_ident = self.din("k_ident", [128, 128])
        k_tri = self.din("k_tri", [128, 128])
        y = nc.dram_tensor("y", [T, D], F32, kind="ExternalOutput").ap()
        self.xT_d = nc.dram_tensor("xT_scratch", [NCH, 128, T], F32, kind="Internal").ap()
        self.xT_tok = [Tok(f"xTd{i}") for i in range(T // TF)]

        ident = P.sb("ident", [128, 128], F32)
        ones_b = P.sb("ones_b", [128, 128], BF16)
        tri_f = P.sb("tri_f", [128, 128], F32)
        self.ident, self.ones_b, self.tri_f = ident, ones_b, tri_f
        P.dma("sp", lambda e: e.dma_start(out=ident[:], in_=k_ident[:, :]), [], [ident], ident)
        P.dma("sp", lambda e: e.dma_start(out=tri_f[:], in_=k_tri[:, :]), [], [tri_f], tri_f)
        P.op("dve", lambda e: e.memset(ones_b[:], 1.0), [], [ones_b])
        self.psum = P.ring("ps", [128, 512], F32, 7, psum=True)
        self.psb = P.ps("psb", [128, 1024], BF16)
        self.colscr = P.ring("colscr", [128, 128], F32, 2)

        self.phase_in(x)
        self.prologue(cvec, final_g)
        for kind, l in self.phases:
            if kind == "ffn":
                self.phase_ffn(l)
            elif kind == "gm":
                self.phase_gm(l, l // 3)
            elif kind == "mla":
                self.phase_mla(l)
            elif kind == "hg":
                self.phase_hg(l)
        self.phase_out(y)
        P.finalize()
        return nc

    def work(self, n, nxt=2, sq=True, nfw=3, nrstd=2):
        P = self.P
        self.n = n
        self.xt_ring = P.ring("xt", [128, NCH, n], F32, nxt)
        self.ht_ring = P.ring("ht", [128, NCH, n], BF16, 1)
        if sq:
            self.sq_ring = P.ring("sq", [128, NCH, n], BF16, 1)
        self.fw = P.ring("fw", [128, n], F32, nfw)
        self.rstd_ring = P.ring("rstd", [128, n], F32, nrstd)

    def load_cols(self, name, src_rows_ap, n):
        P = self.P
        dst = P.sb(name, [128, n], F32)
        scr = self.colscr.next()
        P.dma("sp", lambda e: e.dma_start(out=scr[0:n, :], in_=src_rows_ap), [], [scr], scr)
        ps = self.psum.next()
        P.op("pe", lambda e: e.transpose(ps[:, 0:n], scr[0:n, :], self.ident[0:n, 0:n]),
             [scr, self.ident], [ps])
        P.op("dve", lambda e: e.tensor_copy(dst[:], ps[:, 0:n]), [ps], [dst])
        return dst

    def prologue(self, cvec, final_g):
        P = self.P
        cT = self.load_cols("cT", cvec[:, :], NCH)
        cact = P.sb("cact", [128, NCH], F32)
        cact_b = P.sb("cact_b", [128, NCH], BF16)
        P.op("act", lambda e: e.activation(cact[:], cT[:], AF.Silu), [cT], [cact])
        P.op("dve", lambda e: e.tensor_copy(cact_b[:], cact[:]), [cact], [cact_b])
        self.final_g = self.load_cols("fing", final_g[:, :], NCH)
        layers = sorted(set(l for _, l in self.phases))
        self.G1, self.SH1, self.GA1, self.G2, self.SH2, self.GA2 = {}, {}, {}, {}, {}, {}
        mods, mgs, fgs, abs_ = {}, {}, {}, {}
        for l in layers:
            abs_[l] = self.load_cols(f"adab{l}", self.L("ada_b", l)[:, :], 48)
            mgs[l] = self.load_cols(f"mixg{l}", self.L("mix_norm_g", l)[:, :], NCH)
            fgs[l] = self.load_cols(f"ffng{l}", self.L("ffn_norm_g", l)[:, :], NCH)
            mods[l] = P.sb(f"mod{l}", [128, 48], F32)
            for nm, dd in (("Ga", self.G1), ("GAa", self.GA1), ("Gb", self.G2), ("GAb", self.GA2)):
                dd[l] = P.sb(f"{nm}{l}", [128, NCH], F32)
        P.begin_scope()
        wring = P.ring("adaw", [128, NCH, 1536], BF16, 2)
        for l in layers:
            mod, ab = mods[l], abs_[l]
            ps = self.psum.next()
            for q in range(4):
                wb = wring.next()
                src = self.L("ada_w", l)[:, q * 1536:(q + 1) * 1536].rearrange("(k p) n -> p k n", p=128)
                P.dma("pool", lambda e, wb=wb, src=src: e.dma_start(out=wb[:], in_=src), [], [wb], wb)
                for cc in range(12):
                    col = q * 12 + cc
                    for k in range(NCH):
                        P.op("pe", lambda e, wb=wb, cc=cc, k=k, col=col, ps=ps: e.matmul(
                            ps[:, col:col + 1], wb[:, k, cc * 128:(cc + 1) * 128], cact_b[:, k:k + 1],
                            start=(k == 0), stop=(k == NCH - 1)), [wb, cact_b], [ps])
            P.op("dve", lambda e, mod=mod, ps=ps, ab=ab: e.tensor_tensor(mod[:], ps[:, 0:48], ab[:], ALU.add),
                 [ps, ab], [mod])

            def derive(G, GA, gsrc, sc_off, g_off, mod=mod):
                P.op("dve", lambda e: e.scalar_tensor_tensor(
                    G[:], mod[:, sc_off:sc_off + 8], 1.0, gsrc[:], ALU.add, ALU.mult), [mod, gsrc], [G])
                P.op("dve", lambda e: e.tensor_scalar(GA[:], mod[:, g_off:g_off + 8], 1.0, None, ALU.add),
                     [mod], [GA])
            derive(self.G1[l], self.GA1[l], mgs[l], 8, 16)
            derive(self.G2[l], self.GA2[l], fgs[l], 32, 40)
            self.SH1[l] = (mod, 0)
            self.SH2[l] = (mod, 24)
        P.end_scope()

    def slab_toks(self, t0, n):
        return self.xT_tok[t0 // TF:(t0 + n) // TF]

    def load_xt(self, t0, n):
        P = self.P
        xt = self.xt_ring.next()
        src = self.xT_d[:, :, t0:t0 + n].rearrange("c p t -> p c t")
        P.dma("sp", lambda e: e.dma_start(out=xt[:, :, 0:n], in_=src), self.slab_toks(t0, n), [xt], xt)
        return xt

    def store_xt(self, xt, t0, n):
        P = self.P
        dst = self.xT_d[:, :, t0:t0 + n].rearrange("c p t -> p c t")
        P.dma("sp", lambda e: e.dma_start(out=dst, in_=xt[:, :, 0:n]), [xt], self.slab_toks(t0, n), xt)

    def rstd_of(self, xt, n):
        P = self.P
        sq = self.sq_ring.next()
        P.op("act", lambda e: e.activation(sq[:, 0:NCH, 0:n], xt[:, :, 0:n], AF.Square), [xt], [sq])
        ps = self.psum.next()
        for c in range(NCH):
            P.op("pe", lambda e, c=c: e.matmul(ps[:, 0:n], self.ones_b[:], sq[:, c, 0:n],
                                               start=(c == 0), stop=(c == NCH - 1)), [sq, self.ones_b], [ps])
        tmp = self.fw.next()
        P.op("act", lambda e: e.activation(tmp[:, 0:n], ps[:, 0:n], AF.Sqrt, scale=1.0 / D, bias=EPS), [ps], [tmp])
        rstd = self.rstd_ring.next()
        P.op("dve", lambda e: e.reciprocal(rstd[:, 0:n], tmp[:, 0:n]), [tmp], [rstd])
        return rstd

    def norm_mod(self, xt, n, G, SH, ht=None, c0=0):
        P = self.P
        rstd = self.rstd_of(xt, n)
        if ht is None:
            ht = self.ht_ring.next()
        shb, sho = SH
        for c in range(NCH):
            tmp = self.fw.next()
            P.op("dve" if c % 2 == 0 else "pool", lambda e, c=c, tmp=tmp: e.tensor_tensor(
                tmp[:, 0:n], xt[:, c, 0:n], rstd[:, 0:n], ALU.mult), [xt, rstd], [tmp])
            P.op("act", lambda e, c=c, tmp=tmp: e.activation(
                ht[:, c, c0:c0 + n], tmp[:, 0:n], AF.Identity, scale=G[:, c:c + 1], bias=shb[:, sho + c:sho + c + 1]),
                [tmp, G, shb], [ht])
        return ht

    def phase_in(self, x):
        P, T = self.P, self.T
        P.begin_scope()
        self.work(TT)
        xin_ring = P.ring("xin", [128, 4, D], F32, 2)
        for ti in range(T // TT):
            t0 = ti * TT
            xin = xin_ring.next()
            src = x[t0:t0 + TT, :].rearrange("(s p) d -> p s d", p=128)
            P.dma("sp", lambda e, xin=xin, src=src: e.dma_start(out=xin[:], in_=src), [], [xin], xin)
            xt = self.xt_ring.next()
            for c in range(NCH):
                ps = self.psum.next()
                for s in range(4):
                    P.op("pe", lambda e, ps=ps, s=s, c=c, xin=xin: e.transpose(
                        ps[:, s * 128:(s + 1) * 128], xin[:, s, c * 128:(c + 1) * 128], self.ident[:]),
                        [xin, self.ident], [ps])
                if c % 2 == 0:
                    P.op("act", lambda e, ps=ps, c=c, xt=xt: e.copy(xt[:, c, :], ps[:]), [ps], [xt])
                else:
                    P.op("dve", lambda e, ps=ps, c=c, xt=xt: e.tensor_copy(xt[:, c, :], ps[:]), [ps], [xt])
            self.store_xt(xt, t0, TT)
        P.end_scope()

    def phase_out(self, y):
        P, T = self.P, self.T
        P.begin_scope()
        self.work(TT)
        yo_ring = P.ring("yo", [128, 4, D], F32, 2)
        yts = P.ring("yT", [128, NCH, TT], F32, 1)
        out_toks = []
        for ti in range(T // TT):
            t0 = ti * TT
            xt = self.load_xt(t0, TT)
            rstd = self.rstd_of(xt, TT)
            yT = yts.next()
            for c in range(NCH):
                P.op("dve", lambda e, c=c, xt=xt, yT=yT, rstd=rstd: e.scalar_tensor_tensor(
                    yT[:, c, :], xt[:, c, :], self.final_g[:, c:c + 1], rstd[:], ALU.mult, ALU.mult),
                    [xt, rstd, self.final_g], [yT])
            yo = yo_ring.next()
            for s in range(4):
                for h in range(2):
                    ps = self.psum.next()
                    for cc in range(4):
                        c = h * 4 + cc
                        P.op("pe", lambda e, ps=ps, s=s, c=c, cc=cc, yT=yT: e.transpose(
                            ps[:, cc * 128:(cc + 1) * 128], yT[:, c, s * 128:(s + 1) * 128], self.ident[:]),
                            [yT, self.ident], [ps])
                    if (s + h) % 2 == 0:
                        P.op("act", lambda e, ps=ps, s=s, h=h, yo=yo: e.copy(yo[:, s, h * 512:(h + 1) * 512], ps[:]),
                             [ps], [yo])
                    else:
                        P.op("dve", lambda e, ps=ps, s=s, h=h, yo=yo: e.tensor_copy(yo[:, s, h * 512:(h + 1) * 512], ps[:]),
                             [ps], [yo])
            dst = y[t0:t0 + TT, :].rearrange("(s p) d -> p s d", p=128)
            tk = Tok(f"yout{ti}")
            out_toks.append(tk)
            P.dma("sp", lambda e, yo=yo, dst=dst: e.dma_start(out=dst, in_=yo[:]), [yo], [tk], yo)
        P.op("sp", None, reads=out_toks + self.dbg_toks, writes=[])
        P.end_scope()

    def phase_ffn(self, l):
        P, T = self.P, self.T
        ff_w_up, ff_conv_w, ff_conv_b, ff_w_down = (self.L(k, l) for k in ("ff_w_up", "ff_conv_w", "ff_conv_b", "ff_w_down"))
        n = TF
        cw = [self.load_cols(f"cw{l}_{j}", ff_conv_w[j, :, :], 44) for j in range(3)]
        cb = self.load_cols(f"cb{l}", ff_conv_b[:, :], 44)
        P.begin_scope()
        self.work(n)
        wup = P.sb("wup", [128, NCH, 2 * FF], BF16)
        wdn = P.sb("wdn", [128, NFC, D], BF16)
        hts = P.ring("hth", [128, NCH, n + 2], BF16, 2)
        ybufs = P.ring("ybuf", [128, n], F32, 8)
        sgs = P.ring("sgb", [128, n], F32, 3)
        actT = P.sb("actT", [128, NFC, n], BF16)
        for h in range(4):
            src = ff_w_up[:, h * 1408:(h + 1) * 1408].rearrange("(k p) n -> p k n", p=128)
            P.dma("pool", lambda e, src=src, h=h: e.dma_start(out=wup[:, :, h * 1408:(h + 1) * 1408], in_=src),
                  [], [wup], wup)
        for h in range(2):
            src = ff_w_down[h * 1408:(h + 1) * 1408, :].rearrange("(j p) n -> p j n", p=128)
            P.dma("pool", lambda e, src=src, h=h: e.dma_start(out=wdn[:, h * 11:(h + 1) * 11, :], in_=src),
                  [], [wdn], wdn)
        G2, SH2, GA2 = self.G2[l], self.SH2[l], self.GA2[l]
        def prep(ti, prev):
            xt = self.load_xt(ti * n, n)
            ht = hts.next()
            if prev is None:
                P.op("dve", lambda e, ht=ht: e.memset(ht[:, :, 0:2], 0.0), [], [ht])
            else:
                P.op("dve", lambda e, ht=ht, prev=prev: e.tensor_copy(ht[:, :, 0:2], prev[:, :, n:n + 2]), [prev], [ht])
            self.norm_mod(xt, n, G2, SH2, ht=ht, c0=2)
            return xt, ht

        nxt_prep = prep(0, None)
        for ti in range(T // n):
            t0 = ti * n
            xt, ht = nxt_prep
            pend_g = None

            def gate(yg, yv, j):
                sg = sgs.next()
                P.op("act", lambda e: e.activation(sg[:], yg[:], AF.Silu), [yg], [sg])
                P.op("pool", lambda e: e.tensor_tensor(actT[:, j, :], sg[:], yv[:], ALU.mult), [sg, yv], [actT])
            for j in range(NFC):
                ys = []
                for half in range(2):
                    ch = half * NFC + j
                    ps = self.psum.next()
                    for k in range(NCH):
                        P.op("pe", lambda e, ps=ps, k=k, ch=ch, ht=ht: e.matmul(
                            ps[:, 0:n + 2], wup[:, k, ch * 128:(ch + 1) * 128], ht[:, k, :],
                            start=(k == 0), stop=(k == NCH - 1)), [wup, ht], [ps])
                    yb = ybufs.next()
                    P.op("act", lambda e, yb=yb, ps=ps, ch=ch: e.activation(
                        yb[:], ps[:, 2:n + 2], AF.Identity, scale=cw[2][:, ch:ch + 1], bias=cb[:, ch:ch + 1]),
                        [ps, cw[2], cb], [yb])
                    P.op("dve", lambda e, yb=yb, ps=ps, ch=ch: e.scalar_tensor_tensor(
                        yb[:], ps[:, 1:n + 1], cw[1][:, ch:ch + 1], yb[:], ALU.mult, ALU.add), [ps, yb, cw[1]], [yb])
                    P.op("dve", lambda e, yb=yb, ps=ps, ch=ch: e.scalar_tensor_tensor(
                        yb[:], ps[:, 0:n], cw[0][:, ch:ch + 1], yb[:], ALU.mult, ALU.add), [ps, yb, cw[0]], [yb])
                    ys.append(yb)
                if pend_g is not None:
                    gate(*pend_g)
                pend_g = (ys[0], ys[1], j)
            gate(*pend_g)
            pend_g = None
            if ti + 1 < T // n:
                nxt_prep = prep(ti + 1, ht)
            for c in range(NCH):
                ps = self.psum.next()
                for j in range(NFC):
                    P.op("pe", lambda e, ps=ps, j=j, c=c: e.matmul(
                        ps[:, 0:n], wdn[:, j, c * 128:(c + 1) * 128], actT[:, j, :],
                        start=(j == 0), stop=(j == NFC - 1)), [wdn, actT], [ps])
                P.op("dve", lambda e, ps=ps, c=c, xt=xt: e.scalar_tensor_tensor(
                    xt[:, c, 0:n], ps[:, 0:n], GA2[:, c:c + 1], xt[:, c, 0:n], ALU.mult, ALU.add),
                    [ps, xt, GA2], [xt])
            self.store_xt(xt, t0, n)
        P.end_scope()

    def pipeline(self, items, stages):
        ns = len(stages)
        for step in range(len(items) + ns - 1):
            for s in reversed(range(ns)):
                i = step - s
                if 0 <= i < len(items):
                    stages[s](items[i])

    def phase_gm(self, l, j):
        P, T = self.P, self.T
        gm_w_in, gm_ln_g, gm_ln_b, gm_w_s, gm_b_s, gm_w_out = (self.L(k, j) for k in ("gm_w_in", "gm_ln_g", "gm_ln_b", "gm_w_s", "gm_b_s", "gm_w_out"))
        k_sel = self.din("k_sel", [8, 1024])
        n = TT
        lng = self.load_cols(f"lng{l}", gm_ln_g[:, :], 16)
        lnb = self.load_cols(f"lnb{l}", gm_ln_b[:, :], 16)
        P.begin_scope()
        self.work(n, nxt=1, sq=False, nfw=2, nrstd=1)
        xs_r = P.ring("gxs", [128, n], F32, 4)
        t_r = P.ring("gt", [128, n], F32, 6)
        win = P.sb("win", [128, NCH, 4096], BF16)
        wout = P.sb("wout", [128, 16, D], BF16)
        wsT = P.sb("wsT", [128, 8, 128], BF16)
        CT = P.sb("CT", [128, 16, 128], F32)
        uT = P.sb("uT", [128, 16, n], BF16)
        self.sq_ring = Ring([uT])
        vg = P.sb("vg", [128, GM_INNER], F32)
        vhat = P.sb("vhat", [128, 4, GM_INNER], BF16)
        stats = P.sb("stats", [128, 4, 6], F32)
        mv = P.sb("mv", [128, 2], F32)
        for h in range(4):
            src = gm_w_in[:, h * 1024:(h + 1) * 1024].rearrange("(k p) n -> p k n", p=128)
            P.dma("pool", lambda e, src=src, h=h: e.dma_start(out=win[:, :, h * 1024:(h + 1) * 1024], in_=src),
                  [], [win], win)
        src = gm_w_out[:, :].rearrange("(f p) n -> p f n", p=128)
        P.dma("pool", lambda e, src=src: e.dma_start(out=wout[:], in_=src), [], [wout], wout)
        wsl_r = P.ring("wsl", [128, 128], F32, 2)
        bsl = P.sb("bsl", [8, 128], F32)
        sel_r = P.ring("sel", [8, 128], F32, 2)
        P.dma("sp", lambda e: e.dma_start(out=bsl[:], in_=gm_b_s[:, :]), [], [bsl], bsl)
        bsb = P.sb("bsb", [128, 128], F32)
        for g in range(8):
            wsl = wsl_r.next()
            sel = sel_r.next()
            P.dma("sp", lambda e, wsl=wsl, g=g: e.dma_start(out=wsl[:], in_=gm_w_s[g, :, :]), [], [wsl], wsl)
            P.dma("sp", lambda e, sel=sel, g=g: e.dma_start(out=sel[:], in_=k_sel[:, g * 128:(g + 1) * 128]), [], [sel], sel)
            ps = self.psum.next()
            P.op("pe", lambda e, ps=ps, wsl=wsl: e.transpose(ps[:, 0:128], wsl[:], self.ident[:]),
                 [wsl, self.ident], [ps])
            P.op("dve", lambda e, ps=ps, g=g: e.tensor_tensor(wsT[:, g, :], ps[:, 0:128], self.tri_f[:], ALU.mult),
                 [ps, self.tri_f], [wsT])
            ps2 = self.psum.next()
            P.op("pe", lambda e, ps2=ps2, g=g: e.matmul(ps2[:, 0:128], self.ones_b[:], wsT[:, g, :], start=True, stop=True),
                 [wsT, self.ones_b], [ps2])
            P.op("pe", lambda e, ps2=ps2, sel=sel: e.matmul(ps2[:, 128:256], sel[:, :], bsl[:, :],
                                                            start=True, stop=True), [sel, bsl], [ps2])
            P.op("act", lambda e, ps2=ps2: e.copy(bsb[:], ps2[:, 128:256]), [ps2], [bsb])
            for q in range(2):
                fc = 2 * g + q
                P.op("dve", lambda e, ps2=ps2, fc=fc: e.scalar_tensor_tensor(
                    CT[:, fc, :], ps2[:, 0:128], lnb[:, fc:fc + 1], bsb[:], ALU.mult, ALU.add),
                    [ps2, lnb, bsb], [CT])
        G1, SH1, GA1 = self.G1[l], self.SH1[l], self.GA1[l]
        for ti in range(T // n):
            t0 = ti * n
            xt = self.load_xt(t0, n)
            ht = self.norm_mod(xt, n, G1, SH1)
            items = [("v", s_, nb) for s_ in range(4) for nb in range(4)] + [("u", fc, 0) for fc in range(16)]
            st = {}

            def s_mm(it, ht=ht):
                kind, a_, b_ = it
                ps = self.psum.next()
                st[it] = {"ps": ps}
                for k in range(NCH):
                    if kind == "u":
                        P.op("pe", lambda e, ps=ps, k=k, fc=a_: e.matmul(
                            ps[:, :], win[:, k, fc * 128:(fc + 1) * 128], ht[:, k, :],
                            start=(k == 0), stop=(k == NCH - 1)), [win, ht], [ps])
                    else:
                        P.op("pe", lambda e, ps=ps, k=k, s_=a_, nb=b_: e.matmul(
                            ps[:, :], ht[:, k, s_ * 128:(s_ + 1) * 128], win[:, k, 2048 + nb * 512:2048 + (nb + 1) * 512],
                            start=(k == 0), stop=(k == NCH - 1)), [win, ht], [ps])

            def s_copy(it):
                d = st[it]
                d["xs"] = xs_r.next()
                d["t"] = t_r.next()
                P.op("act", lambda e, d=d: e.activation(d["xs"][:], d["ps"][:, :], AF.Identity), [d["ps"]], [d["xs"]])
                P.op("act", lambda e, d=d: e.activation(d["t"][:], d["ps"][:, :], AF.Square, scale=0.21145921592590375),
                     [d["ps"]], [d["t"]])

            def s_poly(it):
                d = st[it]
                P.op("dve", lambda e, d=d: e.scalar_tensor_tensor(
                    d["t"][:], d["t"][:], 1.0, d["xs"][:], ALU.add, ALU.mult), [d["t"], d["xs"]], [d["t"]])

            def s_sig(it):
                d = st[it]
                P.op("act", lambda e, d=d: e.activation(d["t"][:], d["t"][:], AF.Sigmoid, scale=1.5957691216057308),
                     [d["t"]], [d["t"]])

            def s_out(it):
                kind, a_, b_ = it
                d = st[it]
                if kind == "u":
                    P.op("dve", lambda e, d=d, fc=a_: e.tensor_tensor(uT[:, fc, :], d["t"][:], d["xs"][:], ALU.mult),
                         [d["t"], d["xs"]], [uT])
                else:
                    P.op("dve", lambda e, d=d, nb=b_: e.tensor_tensor(
                        vg[:, nb * 512:(nb + 1) * 512], d["t"][:], d["xs"][:], ALU.mult), [d["t"], d["xs"]], [vg])

            def s_sp_mm(it):
                kind, fc, _ = it
                d = st[it]
                d["sp"] = self.psum.next()
                for s2 in range(4):
                    P.op("pe", lambda e, d=d, s2=s2, fc=fc: e.matmul(
                        d["sp"][:, s2 * 128:(s2 + 1) * 128], vhat[:, s2, fc * 128:(fc + 1) * 128], wsT[:, fc // 2, :],
                        start=True, stop=True), [vhat, wsT], [d["sp"]])

            def s_sp_evac(it):
                kind, fc, _ = it
                if kind != "u":
                    return
                d = st[it]
                d["tmp"] = t_r.next()
                P.op("dve", lambda e, d=d, fc=fc: e.scalar_tensor_tensor(
                    d["tmp"][:].rearrange("p (s t) -> p s t", s=4), d["sp"][:, :].rearrange("p (s t) -> p s t", s=4),
                    lng[:, fc:fc + 1], CT[:, fc:fc + 1, :].to_broadcast([128, 4, 128]),
                    ALU.mult, ALU.add), [d["sp"], lng, CT], [d["tmp"]])

            def s_ymul(it):
                kind, fc, _ = it
                if kind != "u":
                    return
                d = st[it]
                P.op("dve", lambda e, d=d, fc=fc: e.tensor_tensor(uT[:, fc, :], uT[:, fc, :], d["tmp"][:], ALU.mult),
                     [uT, d["tmp"]], [uT])

            def s_ln(it):
                kind, s_, nb = it
                if kind != "v":
                    s_sp_mm(it)
                    return
                P.op("dve", lambda e, nb=nb: e.bn_stats(stats[:, nb, :], vg[:, nb * 512:(nb + 1) * 512]), [vg], [stats])
                if nb == 3:
                    P.op("dve", lambda e: e.bn_aggr(mv[:], stats[:].rearrange("p a b -> p (a b)")), [stats], [mv])
                    P.op("act", lambda e: e.activation(mv[:, 1:2], mv[:, 1:2], AF.Sqrt, bias=EPS), [mv], [mv])
                    P.op("dve", lambda e: e.reciprocal(mv[:, 1:2], mv[:, 1:2]), [mv], [mv])
                    P.op("dve", lambda e: e.scalar_tensor_tensor(mv[:, 0:1], mv[:, 0:1], -1.0, mv[:, 1:2], ALU.mult, ALU.mult),
                         [mv], [mv])
                    P.op("act", lambda e, s_=s_: e.activation(vhat[:, s_, :], vg[:], AF.Identity, scale=mv[:, 1:2], bias=mv[:, 0:1]),
                         [vg, mv], [vhat])

            self.pipeline(items, [s_mm, s_copy, s_poly, s_sig, s_out, s_ln, s_sp_evac, s_ymul])
            for c in range(NCH):
                ps = self.psum.next()
                for fc in range(16):
                    P.op("pe", lambda e, ps=ps, fc=fc, c=c: e.matmul(
                        ps[:, :], wout[:, fc, c * 128:(c + 1) * 128], uT[:, fc, :],
                        start=(fc == 0), stop=(fc == 15)), [wout, uT], [ps])
                P.op("dve", lambda e, ps=ps, c=c, xt=xt: e.scalar_tensor_tensor(
                    xt[:, c, :], ps[:, :], GA1[:, c:c + 1], xt[:, c, :], ALU.mult, ALU.add),
                    [ps, xt, GA1], [xt])
            self.store_xt(xt, t0, n)
        P.end_scope()

    def rope_tables(self, cos2, sin2):
        P, T = self.P, self.T
        pos = self.din("positions", [1, T], I32)
        k_invf = self.din("k_invf", [2, 128])
        posi = P.sb("posi", [1, T], I32)
        posf = P.sb("posf", [1, T], F32)
        invf = P.sb("invf", [1, 128], F32)
        sgn = self.load_cols("ropesgn", k_invf[:, :], 2)
        P.dma("sp", lambda e: e.dma_start(out=posi[:], in_=pos[:, :]), [], [posi], posi)
        P.dma("sp", lambda e: e.dma_start(out=invf[:], in_=k_invf[0:1, :]), [], [invf], invf)
        P.op("dve", lambda e: e.tensor_copy(posf[:], posi[:]), [posi], [posf])
        ki = P.sb("ropek", [128, 512], I32)
        kf = P.sb("ropekf", [128, 512], F32)
        r = P.sb("roper", [128, 512], F32)
        m = P.sb("ropem", [128, 512], F32)
        TWO_PI = 6.283185307179586
        for b in range(T // 512):
            ps = self.psum.next()
            P.op("pe", lambda e, ps=ps, b=b: e.matmul(ps[:, :], invf[:, :], posf[:, b * 512:(b + 1) * 512],
                                                      start=True, stop=True), [invf, posf], [ps])
            for which, dst in ((0, sin2), (1, cos2)):
                P.op("dve", lambda e, ps=ps, which=which: e.tensor_scalar(
                    r[:], ps[:, :], 1.0 / TWO_PI, 0.25 * which, ALU.mult, ALU.add), [ps], [r])
                P.op("dve", lambda e: e.tensor_copy(ki[:], r[:]), [r], [ki])
                P.op("dve", lambda e: e.tensor_copy(kf[:], ki[:]), [ki], [kf])
                P.op("dve", lambda e: e.tensor_tensor(r[:], r[:], kf[:], ALU.subtract), [r, kf], [r])
                P.op("dve", lambda e: e.tensor_scalar(m[:], r[:], 0.5, None, ALU.is_gt), [r], [m])
                P.op("dve", lambda e: e.tensor_tensor(r[:], r[:], m[:], ALU.subtract), [r, m], [r])
                P.op("dve", lambda e: e.tensor_scalar(m[:], r[:], -0.5, None, ALU.is_lt), [r], [m])
                P.op("dve", lambda e: e.tensor_tensor(r[:], r[:], m[:], ALU.add), [r, m], [r])
                if which == 0:
                    P.op("act", lambda e, b=b: e.activation(m[:], r[:], AF.Sin, scale=6.28318), [r], [m])
                    P.op("dve", lambda e, b=b, dst=dst: e.tensor_scalar(
                        dst[:, b * 512:(b + 1) * 512], m[:], sgn[:, 1:2], None, ALU.mult), [m, sgn], [dst])
                else:
                    P.op("act", lambda e, b=b, dst=dst: e.activation(
                        dst[:, b * 512:(b + 1) * 512], r[:], AF.Sin, scale=6.28318), [r], [dst])

    def phase_mla(self, l):
        P, T = self.P, self.T
        nc = self.nc
        n = TT
        NTL = T // n
        w_a, qg_d, kvg_d, w_qb, w_kvb, w_o = (self.L(k, 0) for k in (
            "mla_w_a", "mla_q_norm_g", "mla_kv_norm_g", "mla_w_qb", "mla_w_kvb", "mla_w_o"))
        qg = self.load_cols("mlaqg", qg_d[:, :], 2)
        kvg = self.load_cols("mlakvg", kvg_d[:, :], 1)
        qn_d = nc.dram_tensor("qn_d", [8, 128, T], BF16, kind="Internal").ap()
        qr_d = nc.dram_tensor("qr_d", [4, 128, T], BF16, kind="Internal").ap()
        kn_d = nc.dram_tensor("kn_d", [8, 128, T], BF16, kind="Internal").ap()
        kr_d = nc.dram_tensor("kr_d", [128, T], BF16, kind="Internal").ap()
        v_d = nc.dram_tensor("v_d", [T, 1024], BF16, kind="Internal").ap()
        oT_d = nc.dram_tensor("oT_d", [8, 128, T], BF16, kind="Internal").ap()
        tk = {nm: [Tok(f"{nm}{i}") for i in range(NTL)] for nm in ("qn", "qr", "kn", "kr", "v")}
        otk = [Tok(f"oT{h}") for h in range(8)]
        G1, SH1, GA1 = self.G1[l], self.SH1[l], self.GA1[l]

        P.begin_scope()
        self.work(n)
        cos2 = P.sb("cos2", [128, T], F32)
        sin2 = P.sb("sin2", [128, T], F32)
        self.rope_tables(cos2, sin2)
        wa = P.sb("wa", [128, NCH, 384], BF16)
        wkr = P.sb("wkr", [128, NCH, 256], BF16)
        wqn = P.sb("wqn", [128, 2, 8, 128], BF16)
        wqrA = P.sb("wqrA", [128, 2, 8, 64], BF16)
        wqrB = P.sb("wqrB", [128, 2, 8, 64], BF16)
        wkn = P.sb("wkn", [128, 8, 128], BF16)
        wv = P.sb("wv", [128, 8, 128], BF16)
        wa_v = w_a.rearrange("(k p) n -> p k n", p=128)
        P.dma("pool", lambda e: e.dma_start(out=wa[:], in_=wa_v[:, :, 0:384]), [], [wa], wa)

        def wkr_load(e):
            r = []
            for dup in range(2):
                r.append(e.dma_start(out=wkr[:, :, dup * 64:dup * 64 + 64], in_=wa_v[:, :, 384:448]))
                r.append(e.dma_start(out=wkr[:, :, 128 + dup * 64:128 + dup * 64 + 32], in_=wa_v[:, :, 416:448]))
                r.append(e.dma_start(out=wkr[:, :, 160 + dup * 64:160 + dup * 64 + 32], in_=wa_v[:, :, 384:416]))
            return r
        P.dma("pool", wkr_load, [], [wkr], wkr)
        wq_v = w_qb.rearrange("(k p) (h e) -> p k h e", p=128, e=192)
        P.dma("pool", lambda e: [e.dma_start(out=wqn[:, kc, :, :], in_=wq_v[:, kc, :, 0:128]) for kc in range(2)],
              [], [wqn], wqn)
        P.dma("pool", lambda e: [e.dma_start(out=wqrA[:, kc, :, :], in_=wq_v[:, kc, :, 128:192]) for kc in range(2)],
              [], [wqrA], wqrA)
        P.dma("pool", lambda e: [e.dma_start(out=wqrB[:, kc, :, 0:32], in_=wq_v[:, kc, :, 160:192]) for kc in range(2)] +
                                [e.dma_start(out=wqrB[:, kc, :, 32:64], in_=wq_v[:, kc, :, 128:160]) for kc in range(2)],
              [], [wqrB], wqrB)
        wkv_v = w_kvb.rearrange("p (h two e) -> p two h e", two=2, e=128)
        P.dma("pool", lambda e: e.dma_start(out=wkn[:], in_=wkv_v[:, 0, :, :]), [], [wkn], wkn)
        P.dma("pool", lambda e: e.dma_start(out=wv[:], in_=wkv_v[:, 1, :, :]), [], [wv], wv)
        cqn = P.sb("cqn", [128, 2, n], BF16)
        ckvn = P.sb("ckvn", [128, n], BF16)
        sqs = P.ring("sqs", [128, n], BF16, 2)
        qn_t = P.sb("qn_t", [128, 8, n], BF16)
        qr_t = P.sb("qr_t", [128, 4, n], BF16)
        kn_t = P.sb("kn_t", [128, 8, n], BF16)
        kr_t = P.sb("kr_t", [128, n], BF16)
        v_t = P.sb("v_t", [128, 4, 1024], BF16)
        rt = P.ring("rt", [128, n], F32, 3)

        def small_rstd(pss, dim):
            ps_s = self.psum.next()
            for i, pz in enumerate(pss):
                sq = sqs.next()
                P.op("act", lambda e, sq=sq, pz=pz: e.activation(sq[:], pz[:, :], AF.Square), [pz], [sq])
                P.op("pe", lambda e, sq=sq, i=i: e.matmul(ps_s[:, :], self.ones_b[:], sq[:],
                                                          start=(i == 0), stop=(i == len(pss) - 1)),
                     [sq, self.ones_b], [ps_s])
            tmp = self.fw.next()
            P.op("act", lambda e: e.activation(tmp[:], ps_s[:, :], AF.Sqrt, scale=1.0 / dim, bias=EPS), [ps_s], [tmp])
            rs = self.rstd_ring.next()
            P.op("dve", lambda e: e.reciprocal(rs[:], tmp[:]), [tmp], [rs])
            return rs

        def rotate(psA, psB, dst_fn, writes, t0):
            ta, tb = rt.next(), rt.next()
            P.op("dve", lambda e: e.tensor_tensor(ta[:], psA[:, :], cos2[:, t0:t0 + n], ALU.mult), [psA, cos2], [ta])
            P.op("dve", lambda e: e.tensor_tensor(tb[:], psB[:, :], sin2[:, t0:t0 + n], ALU.mult), [psB, sin2], [tb])
            P.op("pool", lambda e: e.tensor_tensor(dst_fn(), ta[:], tb[:], ALU.add), [ta, tb], writes)

        for ti in range(NTL):
            t0 = ti * n
            xt = self.load_xt(t0, n)
            ht = self.norm_mod(xt, n, G1, SH1)
            pcq = []
            for c in range(3):
                ps = self.psum.next()
                for k in range(NCH):
                    P.op("pe", lambda e, ps=ps, k=k, c=c, ht=ht: e.matmul(
                        ps[:, :], wa[:, k, c * 128:(c + 1) * 128], ht[:, k, :],
                        start=(k == 0), stop=(k == NCH - 1)), [wa, ht], [ps])
                pcq.append(ps)
            rs_q = small_rstd(pcq[0:2], 256)
            for c in range(2):
                P.op("dve", lambda e, c=c, rs_q=rs_q, ps=pcq[c]: e.scalar_tensor_tensor(
                    cqn[:, c, :], ps[:, :], qg[:, c:c + 1], rs_q[:], ALU.mult, ALU.mult), [pcq[c], qg, rs_q], [cqn])
            rs_kv = small_rstd(pcq[2:3], 128)
            P.op("dve", lambda e, rs_kv=rs_kv, ps=pcq[2]: e.scalar_tensor_tensor(
                ckvn[:], ps[:, :], kvg[:, 0:1], rs_kv[:], ALU.mult, ALU.mult), [pcq[2], kvg, rs_kv], [ckvn])
            pab = []
            for ab in range(2):
                ps = self.psum.next()
                for k in range(NCH):
                    P.op("pe", lambda e, ps=ps, k=k, ab=ab, ht=ht: e.matmul(
                        ps[:, :], wkr[:, k, ab * 128:(ab + 1) * 128], ht[:, k, :],
                        start=(k == 0), stop=(k == NCH - 1)), [wkr, ht], [ps])
                pab.append(ps)
            rotate(pab[0], pab[1], lambda: kr_t[:], [kr_t], t0)
            P.dma("sp", lambda e, t0=t0: e.dma_start(out=kr_d[:, t0:t0 + n], in_=kr_t[:]), [kr_t], [tk["kr"][ti]], kr_t)
            for h in range(8):
                ps = self.psum.next()
                for kc in range(2):
                    P.op("pe", lambda e, ps=ps, kc=kc, h=h: e.matmul(
                        ps[:, :], wqn[:, kc, h, :], cqn[:, kc, :], start=(kc == 0), stop=(kc == 1)), [wqn, cqn], [ps])
                if h % 2 == 0:
                    P.op("act", lambda e, ps=ps, h=h: e.copy(qn_t[:, h, :], ps[:, :]), [ps], [qn_t])
                else:
                    P.op("dve", lambda e, ps=ps, h=h: e.tensor_copy(qn_t[:, h, :], ps[:, :]), [ps], [qn_t])
            P.dma("sp", lambda e, t0=t0: e.dma_start(out=qn_d[:, :, t0:t0 + n].rearrange("h p t -> p h t"), in_=qn_t[:]),
                  [qn_t], [tk["qn"][ti]], qn_t)
            for pr in range(4):
                pab = []
                for W in (wqrA, wqrB):
                    ps = self.psum.next()
                    for kc in range(2):
                        P.op("pe", lambda e, ps=ps, kc=kc, pr=pr, W=W: e.matmul(
                            ps[:, :], W[:, kc, 2 * pr:2 * pr + 2, :].rearrange("p h e -> p (h e)"), cqn[:, kc, :],
                            start=(kc == 0), stop=(kc == 1)), [W, cqn], [ps])
                    pab.append(ps)
                rotate(pab[0], pab[1], lambda pr=pr: qr_t[:, pr, :], [qr_t], t0)
            P.dma("sp", lambda e, t0=t0: e.dma_start(out=qr_d[:, :, t0:t0 + n].rearrange("h p t -> p h t"), in_=qr_t[:]),
                  [qr_t], [tk["qr"][ti]], qr_t)
            for h in range(8):
                ps = self.psum.next()
                P.op("pe", lambda e, ps=ps, h=h: e.matmul(ps[:, :], wkn[:, h, :], ckvn[:], start=True, stop=True),
                     [wkn, ckvn], [ps])
                if h % 2 == 0:
                    P.op("act", lambda e, ps=ps, h=h: e.copy(kn_t[:, h, :], ps[:, :]), [ps], [kn_t])
                else:
                    P.op("dve", lambda e, ps=ps, h=h: e.tensor_copy(kn_t[:, h, :], ps[:, :]), [ps], [kn_t])
            P.dma("sp", lambda e, t0=t0: e.dma_start(out=kn_d[:, :, t0:t0 + n].rearrange("h p t -> p h t"), in_=kn_t[:]),
                  [kn_t], [tk["kn"][ti]], kn_t)
            for s in range(4):
                for hh in range(2):
                    ps = self.psum.next()
                    P.op("pe", lambda e, ps=ps, s=s, hh=hh: e.matmul(
                        ps[:, :], ckvn[:, s * 128:(s + 1) * 128],
                        wv[:, 4 * hh:4 * hh + 4, :].rearrange("p h e -> p (h e)"), start=True, stop=True),
                        [wv, ckvn], [ps])
                    if hh == 0:
                        P.op("act", lambda e, ps=ps, s=s: e.copy(v_t[:, s, 0:512], ps[:, :]), [ps], [v_t])
                    else:
                        P.op("dve", lambda e, ps=ps, s=s: e.tensor_copy(v_t[:, s, 512:1024], ps[:, :]), [ps], [v_t])
            P.dma("sp", lambda e, t0=t0: e.dma_start(out=v_d[t0:t0 + n, :].rearrange("(s p) f -> p s f", p=128), in_=v_t[:]),
                  [v_t], [tk["v"][ti]], v_t)
        P.end_scope()

        P.begin_scope()
        NKB = T // 128
        qn_r = P.ring("qn", [128, T], BF16, 2)
        kn_r = P.ring("kn", [128, T], BF16, 2)
        qr_r = P.ring("qr", [128, T], BF16, 2)
        v_r = P.ring("v", [128, NKB, 128], BF16, 2)
        ob_r = P.ring("ob", [128, T], BF16, 2)
        kr2 = P.sb("kr2", [128, T], BF16)
        tri_b = P.sb("tri_b", [128, 128], BF16)
        P.op("dve", lambda e: e.tensor_copy(tri_b[:], self.tri_f[:]), [self.tri_f], [tri_b])
        pt_r = P.ring("pt", [128, 512], BF16, 4)
        rc_r = P.ring("rc", [128, 512], F32, 2)
        acc_r = Ring([(self.psum.bufs[0], self.psum.bufs[1]), (self.psum.bufs[2], self.psum.bufs[3])])
        s_r = Ring(self.psum.bufs[4:7])
        P.dma("sp", lambda e: e.dma_start(out=kr2[:], in_=kr_d[:, :]), tk["kr"], [kr2], kr2)
        SCALE = 192.0 ** -0.5
        qr = None
        for h in range(8):
            qn, kn, v, ob = qn_r.next(), kn_r.next(), v_r.next(), ob_r.next()
            P.dma("sp", lambda e, qn=qn, h=h: e.dma_start(out=qn[:], in_=qn_d[h, :, :]), tk["qn"], [qn], qn)
            P.dma("sp", lambda e, kn=kn, h=h: e.dma_start(out=kn[:], in_=kn_d[h, :, :]), tk["kn"], [kn], kn)
            P.dma("sp", lambda e, v=v, h=h: [e.dma_start(
                out=v[:, q * (NKB // 4):(q + 1) * (NKB // 4), :],
                in_=v_d[q * (T // 4):(q + 1) * (T // 4), h * 128:(h + 1) * 128].rearrange("(n p) e -> p n e", p=128))
                for q in range(4)], tk["v"], [v], v)
            if h % 2 == 0:
                qr = qr_r.next()
                P.dma("sp", lambda e, qr=qr, h=h: e.dma_start(out=qr[:], in_=qr_d[h // 2, :, :]), tk["qr"], [qr], qr)
            r0 = 64 * (h % 2)
            for qi in range(T // 512):
                O_ps, R_ps = acc_r.next()
                nkb = 4 * (qi + 1)
                q0 = qi * 512
                pend = None

                def flush(pend):
                    kb, c0, pt = pend
                    P.op("pe", lambda e, O_ps=O_ps, kb=kb, c0=c0, pt=pt, v=v, nkb=nkb: e.matmul(
                        O_ps[:, c0:512], v[:, kb, :], pt[:, c0:512], start=(kb == 0), stop=(kb == nkb - 1)),
                        [v, pt], [O_ps])
                    P.op("pe", lambda e, R_ps=R_ps, kb=kb, c0=c0, pt=pt, nkb=nkb: e.matmul(
                        R_ps[:, c0:512], self.ones_b[:], pt[:, c0:512], start=(kb == 0), stop=(kb == nkb - 1)),
                        [self.ones_b, pt], [R_ps])

                for kb in range(nkb):
                    j = kb - 4 * qi
                    c0 = max(0, j) * 128
                    S_ps = s_r.next()
                    P.op("pe", lambda e, S_ps=S_ps, kb=kb, c0=c0, q0=q0, kn=kn, qn=qn: e.matmul(
                        S_ps[:, c0:512], kn[:, kb * 128:(kb + 1) * 128], qn[:, q0 + c0:q0 + 512],
                        start=True, stop=False), [kn, qn], [S_ps])
                    P.op("pe", lambda e, S_ps=S_ps, kb=kb, c0=c0, q0=q0, qr=qr, r0=r0: e.matmul(
                        S_ps[:, c0:512], kr2[r0:r0 + 64, kb * 128:(kb + 1) * 128], qr[r0:r0 + 64, q0 + c0:q0 + 512],
                        start=False, stop=True), [kr2, qr], [S_ps])
                    if pend is not None:
                        flush(pend)
                    pt = pt_r.next()
                    P.op("act", lambda e, S_ps=S_ps, pt=pt, c0=c0: e.activation(
                        pt[:, c0:512], S_ps[:, c0:512], AF.Exp, scale=SCALE), [S_ps], [pt])
                    if j >= 0:
                        P.op("pool", lambda e, pt=pt, c0=c0: e.tensor_tensor(
                            pt[:, c0:c0 + 128], pt[:, c0:c0 + 128], tri_b[:], ALU.mult), [pt, tri_b], [pt])
                    pend = (kb, c0, pt)
                flush(pend)
                rc = rc_r.next()
                P.op("dve", lambda e, rc=rc, R_ps=R_ps: e.reciprocal(rc[:], R_ps[:, :]), [R_ps], [rc])
                P.op("dve", lambda e, rc=rc, O_ps=O_ps, ob=ob, q0=q0: e.tensor_tensor(
                    ob[:, q0:q0 + 512], O_ps[:, :], rc[:], ALU.mult), [O_ps, rc], [ob])
            P.dma("sp", lambda e, ob=ob, h=h: e.dma_start(out=oT_d[h, :, :], in_=ob[:]), [ob], [otk[h]], ob)
        P.end_scope()

        P.begin_scope()
        self.work(n)
        wo = P.sb("wo", [128, 8, D], BF16)
        P.dma("pool", lambda e: e.dma_start(out=wo[:], in_=w_o.rearrange("(h p) n -> p h n", p=128)), [], [wo], wo)
        ot_r = P.ring("ot", [128, 8, n], BF16, 2)
        for ti in range(NTL):
            t0 = ti * n
            xt = self.load_xt(t0, n)
            ot = ot_r.next()
            P.dma("sp", lambda e, ot=ot, t0=t0: e.dma_start(out=ot[:], in_=oT_d[:, :, t0:t0 + n].rearrange("h p t -> p h t")),
                  otk, [ot], ot)
            for c in range(NCH):
                ps = self.psum.next()
                for h in range(8):
                    P.op("pe", lambda e, ps=ps, h=h, c=c, ot=ot: e.matmul(
                        ps[:, :], wo[:, h, c * 128:(c + 1) * 128], ot[:, h, :], start=(h == 0), stop=(h == 7)),
                        [wo, ot], [ps])
                P.op("dve", lambda e, ps=ps, c=c, xt=xt: e.scalar_tensor_tensor(
                    xt[:, c, :], ps[:, :], GA1[:, c:c + 1], xt[:, c, :], ALU.mult, ALU.add), [ps, xt, GA1], [xt])
            self.store_xt(xt, t0, n)
        P.end_scope()

    def phase_hg(self, l):
        P, T = self.P, self.T
        n = 256
        NS = n // 128
        NC = n // 64
        w_in, ng_d, w_o = (self.L(k, 0) for k in ("hg_w_in", "hg_norm_g", "hg_w_o"))
        hg_lb = self.din("hg_lb", [DEPTH, NCH, 128])
        k_mask2 = self.din("k_mask2", [128, 128])
        ng = self.load_cols("hgng", ng_d[:, :], 1)
        lbc = [self.load_cols(f"hglb{d}", hg_lb[d, :, :], NCH) for d in range(DEPTH)]
        lb = P.sb("hg_lbv", [128, NCH], F32)
        oml = P.sb("hg_oml", [128, NCH], F32)
        esum = P.sb("hg_esum", [128, NCH], F32)
        enum_ = P.sb("hg_enum", [128, NCH], F32)
        for d in range(DEPTH):
            P.op("act", lambda e, d=d: e.activation(lbc[d][:], lbc[d][:], AF.Exp), [lbc[d]], [lbc[d]])
        P.op("dve", lambda e: e.tensor_tensor(esum[:], lbc[0][:], lbc[1][:], ALU.add), [lbc[0], lbc[1]], [esum])
        P.op("dve", lambda e: e.tensor_tensor(esum[:], esum[:], lbc[2][:], ALU.add), [esum, lbc[2]], [esum])
        P.op("dve", lambda e: e.tensor_tensor(esum[:], esum[:], lbc[3][:], ALU.add), [esum, lbc[3]], [esum])
        P.op("dve", lambda e: e.memset(enum_[:], 0.0), [], [enum_])
        for d in range(1, l + 1):
            P.op("dve", lambda e, d=d: e.tensor_tensor(enum_[:], enum_[:], lbc[d][:], ALU.add), [enum_, lbc[d]], [enum_])
        P.op("dve", lambda e: e.reciprocal(esum[:], esum[:]), [esum], [esum])
        P.op("dve", lambda e: e.tensor_tensor(lb[:], enum_[:], esum[:], ALU.mult), [enum_, esum], [lb])
        P.op("dve", lambda e: e.tensor_scalar(oml[:], lb[:], -1.0, 1.0, ALU.mult, ALU.add), [lb], [oml])
        G1, SH1, GA1 = self.G1[l], self.SH1[l], self.GA1[l]

        P.begin_scope()
        self.work(n, nxt=2)
        win = P.sb("hwin", [128, NCH, 4096], BF16)
        wo = P.sb("hwo", [128, 8, D], BF16)
        for h in range(4):
            src = w_in[:, h * 1024:(h + 1) * 1024].rearrange("(k p) n -> p k n", p=128)
            P.dma("pool", lambda e, src=src, h=h: e.dma_start(out=win[:, :, h * 1024:(h + 1) * 1024], in_=src),
                  [], [win], win)
        P.dma("pool", lambda e: e.dma_start(out=wo[:], in_=w_o.rearrange("(h p) n -> p h n", p=128)), [], [wo], wo)
        mask2 = P.sb("mask2", [128, 128], F32)
        P.dma("sp", lambda e: e.dma_start(out=mask2[:], in_=k_mask2[:, :]), [], [mask2], mask2)
        ident_b = P.sb("ident_b", [128, 128], BF16)
        P.op("dve", lambda e: e.tensor_copy(ident_b[:], self.ident[:]), [self.ident], [ident_b])
        rmask = P.sb("rmask", [128, n], F32)
        P.op("dve", lambda e: e.memset(rmask[:], 1.0), [], [rmask])
        for c in range(NC):
            P.op("dve", lambda e, c=c: e.memset(rmask[:, 64 * c:64 * c + 1], 0.0), [], [rmask])
        S = P.sb("hgS", [128, 8, 128], F32)
        S_bf = [P.sb(f"hgSb{g}", [128, 4, 128], BF16) for g in range(2)]
        P.op("dve", lambda e: e.memset(S[:], 0.0), [], [S])
        for g in range(2):
            P.op("pool", lambda e, g=g: e.memset(S_bf[g][:], 0.0), [], [S_bf[g]])
        ebs = P.ring("hgebA", [128, NC, 8], F32, 2)
        Qt = [P.sb(f"hgQ{h}", [128, n], BF16) for h in range(8)]
        Kt = [P.sb(f"hgK{h}", [128, n], BF16) for h in range(8)]
        Ktok = [P.sb(f"hgKt{h}", [128, NS, 128], BF16) for h in range(8)]
        sgate = [P.sb(f"hgsg{h}", [128, n], BF16) for h in range(8)]
        V_t = P.sb("hgV", [128, NS, 1024], BF16)
        o_f = P.sb("hgo", [128, 8, n], F32)
        ogT = P.sb("hgog", [128, 8, n], BF16)
        tA_r = P.ring("hgtA", [128, n], F32, 6)
        tB_r = P.ring("hgtB", [128, n], F32, 8)
        tC_r = P.ring("hgtC", [128, n], F32, 6)
        tD_r = P.ring("hgtD", [128, n], F32, 5)
        tr = P.ring("hgt", [128, n], F32, 6)
        at_r = P.ring("hgAT", [128, 128], BF16, 8)
        st_r = P.ring("hgSt", [128, 512], F32, 3)
        sqh = P.ring("hgsq", [128, n], BF16, 3)

        hts = P.ring("hght", [128, NCH, n], BF16, 2)

        def prep(ti):
            xt = self.load_xt(ti * n, n)
            return xt, self.norm_mod(xt, n, G1, SH1, ht=hts.next())

        nxt_prep = prep(0)
        for ti in range(T // n):
            t0 = ti * n
            xt, ht = nxt_prep
            ebA = ebs.next()
            for s in range(NS):
                for hh in range(2):
                    ps = self.psum.next()
                    for k in range(NCH):
                        P.op("pe", lambda e, ps=ps, k=k, s=s, hh=hh, ht=ht: e.matmul(
                            ps[:, :], ht[:, k, s * 128:(s + 1) * 128], win[:, k, 2048 + hh * 512:2048 + (hh + 1) * 512],
                            start=(k == 0), stop=(k == NCH - 1)), [win, ht], [ps])
                    if hh == 0:
                        P.op("act", lambda e, ps=ps, s=s: e.copy(V_t[:, s, 0:512], ps[:, :]), [ps], [V_t])
                    else:
                        P.op("dve", lambda e, ps=ps, s=s: e.tensor_copy(V_t[:, s, 512:1024], ps[:, :]), [ps], [V_t])
            st = {}

            def proj(off, h, ht=ht):
                ps = self.psum.next()
                for k in range(NCH):
                    P.op("pe", lambda e, ps=ps, k=k: e.matmul(
                        ps[:, 0:n], win[:, k, off + h * 128:off + (h + 1) * 128], ht[:, k, :],
                        start=(k == 0), stop=(k == NCH - 1)), [win, ht], [ps])
                return ps

            def h0(h):
                st[h] = {"fl": proj(1024, h), "g": proj(3072, h)}

            def h1(h):
                d = st[h]
                d["A"], d["B"] = tA_r.next(), tB_r.next()
                P.op("act", lambda e: e.activation(d["A"][:], d["fl"][:, 0:n], AF.Exp, scale=-1.0), [d["fl"]], [d["A"]])
                P.op("act", lambda e: e.activation(d["B"][:], d["g"][:, 0:n], AF.Exp, scale=-1.0), [d["g"]], [d["B"]])
                P.op("act", lambda e: e.copy(sgate[h][:], d["g"][:, 0:n]), [d["g"]], [sgate[h]])

            def h1b(h):
                d = st[h]
                P.op("act", lambda e: e.activation(d["A"][:], d["A"][:], AF.Ln, bias=1.0), [d["A"]], [d["A"]])
                P.op("act", lambda e: e.activation(d["B"][:], d["B"][:], AF.Ln, bias=1.0), [d["B"]], [d["B"]])

            def h1c(h):
                d = st[h]
                P.op("act", lambda e: e.activation(d["A"][:], d["A"][:], AF.Exp, scale=-1.0), [d["A"]], [d["A"]])
                P.op("act", lambda e: e.activation(d["B"][:], d["B"][:], AF.Exp, scale=-1.0), [d["B"]], [d["B"]])

            def h2(h):
                d = st[h]
                A, B = d["A"], d["B"]
                P.op("dve", lambda e: e.tensor_scalar(A[:], A[:], oml[:, h:h + 1], lb[:, h:h + 1], ALU.mult, ALU.add),
                     [A, oml, lb], [A])
                P.op("dve", lambda e: e.tensor_tensor(sgate[h][:], sgate[h][:], B[:], ALU.mult), [sgate[h], B], [sgate[h]])

            def h3(h):
                d = st[h]
                d["C"] = tC_r.next()
                P.op("act", lambda e: e.activation(d["C"][:], d["A"][:], AF.Ln), [d["A"]], [d["C"]])
                P.op("pool", lambda e: e.tensor_scalar(d["B"][:], d["A"][:], -1.0, 1.0, ALU.mult, ALU.add), [d["A"]], [d["B"]])

            def h4(h):
                d = st[h]
                d["D"] = tD_r.next()
                P.op("dve", lambda e: e.tensor_tensor_scan(d["D"][:], rmask[:], d["C"][:], 0.0, ALU.mult, ALU.add),
                     [rmask, d["C"]], [d["D"]])

            def h5(h):
                d = st[h]
                P.op("act", lambda e: e.activation(d["C"][:], d["D"][:], AF.Exp), [d["D"]], [d["C"]])
                P.op("act", lambda e: e.activation(d["D"][:], d["D"][:], AF.Exp, scale=-1.0), [d["D"]], [d["D"]])

            def h6(h):
                st[h]["q"] = proj(0, h)

            def h7(h, ebA=ebA):
                d = st[h]
                P.op("dve", lambda e: e.tensor_tensor(Qt[h][:], d["q"][:, 0:n], d["C"][:], ALU.mult), [d["q"], d["C"]], [Qt[h]])
                P.op("dve", lambda e, ebA=ebA: e.tensor_copy(ebA[:, :, h], d["C"][:].rearrange("p (c t) -> p c t", t=64)[:, :, 63]),
                     [d["C"]], [ebA])
                P.op("pool", lambda e: e.tensor_tensor(Kt[h][:], d["B"][:], d["D"][:], ALU.mult), [d["B"], d["D"]], [Kt[h]])

            def h8(h):
                pb = self.psb
                hb = (h % 2) * 512
                for s in range(NS):
                    P.op("pe", lambda e, s=s: e.transpose(
                        pb[:, hb + s * 128:hb + (s + 1) * 128], Kt[h][:, s * 128:(s + 1) * 128], ident_b[:]),
                        [Kt[h], ident_b], [pb])
                P.op("act", lambda e: e.copy(Ktok[h][:].rearrange("p s d -> p (s d)"), pb[:, hb:hb + NS * 128]),
                     [pb], [Ktok[h]])

            self.pipeline(list(range(8)), [h0, h1, h1b, h1c, h2, h3, h4, h5, h6, h7, h8])
            if ti + 1 < T // n:
                nxt_prep = prep(ti + 1)

            for s in range(NS):
                cols = slice(s * 128, (s + 1) * 128)
                bA = [self.psum.next(), self.psum.next()]
                for h in range(8):
                    P.op("pe", lambda e, h=h, cols=cols, b=bA[h // 4]: e.matmul(
                        b[:, (h % 4) * 128:(h % 4 + 1) * 128], Kt[h][:, cols], Qt[h][:, cols], start=True, stop=True),
                        [Kt[h], Qt[h]], [bA[h // 4]])
                ats = []
                for h in range(8):
                    at = at_r.next()
                    ats.append(at)
                    P.op("dve", lambda e, h=h, at=at, b=bA[h // 4]: e.tensor_tensor(
                        at[:], b[:, (h % 4) * 128:(h % 4 + 1) * 128], mask2[:], ALU.mult), [bA[h // 4], mask2], [at])
                bO = [self.psum.next(), self.psum.next()]
                for h in range(8):
                    P.op("pe", lambda e, h=h, at=ats[h], s=s, b=bO[h // 4]: e.matmul(
                        b[:, (h % 4) * 128:(h % 4 + 1) * 128], V_t[:, s, h * 128:(h + 1) * 128], at[:],
                        start=(h % 4 == 0), stop=False, skip_group_check=True), [V_t, ats[h]], [bO[h // 4]])
                for half in range(2):
                    po = 64 * half
                    cc = slice(s * 128 + po, s * 128 + po + 64)
                    c = 2 * s + half
                    bS = [self.psum.next(), self.psum.next()]
                    for h in range(8):
                        P.op("pe", lambda e, h=h, cc=cc, po=po, half=half, b=bO[h // 4]: e.matmul(
                            b[:, (h % 4) * 128 + po:(h % 4) * 128 + po + 64], S_bf[h // 4][:, h % 4, :], Qt[h][:, cc],
                            start=False, stop=(half == 1), skip_group_check=True), [S_bf[h // 4], Qt[h]], [bO[h // 4]])
                        P.op("pe", lambda e, h=h, s=s, po=po, b=bS[h // 4]: e.matmul(
                            b[:, (h % 4) * 128:(h % 4 + 1) * 128], Ktok[h][po:po + 64, s, :],
                            V_t[po:po + 64, s, h * 128:(h + 1) * 128], start=True, stop=True),
                            [Ktok[h], V_t], [bS[h // 4]])
                    for g in range(2):
                        stt = st_r.next()
                        P.op("dve", lambda e, g=g, stt=stt, b=bS[g]: e.tensor_tensor(
                            stt[:], b[:, :], S[:, 4 * g:4 * g + 4, :].rearrange("p h d -> p (h d)"), ALU.add),
                            [bS[g], S], [stt])
                        P.op("dve", lambda e, g=g, stt=stt, c=c, ebA=ebA: e.tensor_tensor(
                            S[:, 4 * g:4 * g + 4, :], stt[:].rearrange("p (h d) -> p h d", h=4),
                            ebA[:, c, 4 * g:4 * g + 4].rearrange("p (h o) -> p h o", o=1).to_broadcast([128, 4, 128]),
                            ALU.mult), [stt, ebA], [S])
                        P.op("act", lambda e, g=g: e.copy(S_bf[g][:], S[:, 4 * g:4 * g + 4, :]), [S], [S_bf[g]])
                for h in range(8):
                    P.op("act", lambda e, h=h, cols=cols, b=bO[h // 4]: e.copy(
                        o_f[:, h, cols], b[:, (h % 4) * 128:(h % 4 + 1) * 128]), [bO[h // 4]], [o_f])
            nd = {}

            def n0(h, nd=nd):
                sq = sqh.next()
                nd[h] = {"sq": sq}
                P.op("pool", lambda e: e.tensor_tensor(sq[:], o_f[:, h, :], o_f[:, h, :], ALU.mult), [o_f], [sq])

            def n1(h, nd=nd):
                ps, sq = self.psum.next(), nd[h]["sq"]
                nd[h]["ps"] = ps
                P.op("pe", lambda e: e.matmul(ps[:, 0:n], self.ones_b[:], sq[:], start=True, stop=True),
                     [sq, self.ones_b], [ps])

            def n2(h, nd=nd):
                t1, ps = tr.next(), nd[h]["ps"]
                nd[h]["t1"] = t1
                P.op("dve", lambda e: e.tensor_scalar(t1[:], ps[:, 0:n], 1.0 / 128, EPS, ALU.mult, ALU.add), [ps], [t1])

            def n3(h, nd=nd):
                t1 = nd[h]["t1"]
                P.op("act", lambda e: e.activation(t1[:], t1[:], AF.Ln), [t1], [t1])

            def n4(h, nd=nd):
                t1 = nd[h]["t1"]
                P.op("act", lambda e: e.activation(t1[:], t1[:], AF.Exp, scale=-0.5), [t1], [t1])

            def n5(h, nd=nd):
                t1 = nd[h]["t1"]
                P.op("dve", lambda e: e.scalar_tensor_tensor(
                    t1[:], o_f[:, h, :], ng[:, 0:1], t1[:], ALU.mult, ALU.mult), [o_f, ng, t1], [t1])

            def n6(h, nd=nd):
                t1 = nd[h]["t1"]
                P.op("pool", lambda e: e.tensor_tensor(ogT[:, h, :], t1[:], sgate[h][:], ALU.mult), [t1, sgate[h]], [ogT])

            self.pipeline(list(range(8)), [n0, n1, n2, n3, n4, n5, n6])
            for c in range(NCH):
                ps = self.psum.next()
                for h in range(8):
                    P.op("pe", lambda e, ps=ps, h=h, c=c: e.matmul(
                        ps[:, 0:n], wo[:, h, c * 128:(c + 1) * 128], ogT[:, h, :], start=(h == 0), stop=(h == 7)),
                        [wo, ogT], [ps])
                P.op("dve", lambda e, ps=ps, c=c, xt=xt: e.scalar_tensor_tensor(
                    xt[:, c, 0:n], ps[:, 0:n], GA1[:, c:c + 1], xt[:, c, 0:n], ALU.mult, ALU.add), [ps, xt, GA1], [xt])
            self.store_xt(xt, t0, n)
        P.end_scope()


FULL_PHASES = [("gm", 0), ("ffn", 0), ("mla", 1), ("ffn", 1), ("hg", 2), ("ffn", 2), ("gm", 3), ("ffn", 3)]


def make_consts():
    ident = np.eye(128, dtype=np.float32)
    tri = np.triu(np.ones((128, 128), dtype=np.float32))
    j = np.arange(128) % 32
    invf = np.zeros((2, 128), dtype=np.float32)
    invf[0] = (10000.0 ** (-(2.0 * j) / 64.0)).astype(np.float32)
    invf[1] = np.where((np.arange(128) // 32) % 2 == 0, -1.0, 1.0)
    sel = np.zeros((8, 1024), dtype=np.float32)
    for g in range(8):
        sel[g, g * 128:(g + 1) * 128] = 1.0
    mask2 = np.zeros((128, 128), dtype=np.float32)
    mask2[0:64, 0:64] = tri[0:64, 0:64]
    mask2[64:128, 64:128] = tri[0:64, 0:64]
    return {"k_ident": ident, "k_tri": tri, "k_invf": invf, "k_sel": sel, "k_mask2": mask2}


LAYER_SHAPES = {
    "ada_w": [D, 6 * D], "ada_b": [48, 128], "mix_norm_g": [NCH, 128], "ffn_norm_g": [NCH, 128],
    "gm_w_in": [D, 4096], "gm_ln_g": [16, 128], "gm_ln_b": [16, 128], "gm_w_s": [8, 128, 128],
    "gm_b_s": [8, 128], "gm_w_out": [GM_INNER, D],
    "mla_w_a": [D, 448], "mla_q_norm_g": [2, 128], "mla_kv_norm_g": [1, 128], "mla_w_qb": [256, 1536],
    "mla_w_kvb": [128, 2048], "mla_w_o": [1024, 1024],
    "hg_w_in": [D, 4096], "hg_norm_g": [1, 128], "hg_w_o": [1024, 1024],
    "ff_w_up": [D, 2 * FF], "ff_conv_w": [3, 44, 128], "ff_conv_b": [44, 128], "ff_w_down": [FF, D],
}


def core_inputs(inp, names, b, T, consts):
    f = np.ascontiguousarray
    m = {}
    for nm in names:
        if nm in consts:
            m[nm] = consts[nm]
        elif "__" in nm:
            base, idx = nm.split("__")
            m[nm] = f(np.asarray(inp[base][int(idx)], dtype=np.float32).reshape(LAYER_SHAPES[base]))
        elif nm == "x":
            m[nm] = f(inp["x"][b, :T])
        elif nm == "c":
            m[nm] = f(inp["c"][b].reshape(NCH, 128))
        elif nm == "positions":
            m[nm] = f(inp["positions"][b, :T].reshape(1, T).astype(np.int32))
        elif nm == "final_g":
            m[nm] = f(inp["final_g"].reshape(NCH, 128))
        elif nm == "hg_lb":
            m[nm] = f(inp["hg_lb"].reshape(DEPTH, NCH, 128))
        else:
            raise KeyError(nm)
    return m


def run(inp, T, phases, trace=False, debug=False):
    bld = Builder(T, phases)
    bld.debug = debug
    nc = bld.build()
    names = list(bld.decl.keys())
    B = inp["x"].shape[0]
    consts = make_consts()
    per_b = [core_inputs(inp, names, b, T, consts) for b in range(B)]
    in_maps = [per_b[c % B] for c in range(N_CORES)]
    res = run_bass_kernel_spmd(nc, in_maps, core_ids=list(range(N_CORES)), trace=trace)
    out = np.stack([res.results[b]["y"] for b in range(B)], axis=0)
    return out, res


def kernel(**inputs):
    inp = {k: np.asarray(v) for k, v in inputs.items()}
    out, _ = run(inp, inp["x"].shape[1], FULL_PHASES)
    return out.astype(np.float32)
```

```python
import contextlib
import numpy as np
import concourse.bass as bass
import concourse.mybir as mybir
from concourse.bass_utils import run_bass_kernel_spmd

F32 = mybir.dt.float32
BF16 = mybir.dt.bfloat16
I32 = mybir.dt.int32
ALU = mybir.AluOpType
AF = mybir.ActivationFunctionType

D = 1024
NCH = 8
DEPTH = 4
FF = 2816
NFC = 22
EPS = 1e-6
GM_INNER = 2048
TT = 512
TF = 256
N_CORES = 8


class Tok:
    __slots__ = ("name", "last_w", "readers", "sem", "cnt")

    def __init__(self, name):
        self.name = name
        self.last_w = None
        self.readers = {}
        self.sem = None
        self.cnt = 0


class Op:
    __slots__ = ("engine", "fn", "reads", "writes", "dma", "chain", "deps", "signal", "ev")

    def __init__(self, engine, fn, reads, writes, dma, chain):
        self.engine = engine
        self.fn = fn
        self.reads = reads
        self.writes = writes
        self.dma = dma
        self.chain = chain
        self.deps = []
        self.signal = False
        self.ev = None


def _tok(x):
    return x.tok if isinstance(x, Buf) else x


class Ring:
    def __init__(self, bufs):
        self.bufs = bufs
        self.i = 0

    def next(self):
        b = self.bufs[self.i % len(self.bufs)]
        self.i += 1
        return b


class Buf:
    def __init__(self, name, shape, dt, psum):
        self.name, self.shape, self.dt, self.psum = name, list(shape), dt, psum
        self.t = None
        self.tok = Tok(name)

    def __getitem__(self, k):
        return self.t[k]


class AliasBuf(Buf):
    def __init__(self, base, fn):
        self.base, self.fn = base, fn
        self.tok = base.tok
        self.name = base.name + "_alias"

    def __getitem__(self, k):
        return self.fn(self.base.t)[k]


class Prog:
    def __init__(self, nc):
        self.nc = nc
        self.ops = []
        self.eng = {"pe": nc.tensor, "act": nc.scalar, "dve": nc.vector,
                    "pool": nc.gpsimd, "sp": nc.sync}
        self.in_scope = False

    def _mk(self, name, shape, dt, psum):
        self.uid = getattr(self, "uid", 0) + 1
        b = Buf(f"{name}_{self.uid}", shape, dt, psum)
        self.ops.append(("alloc", b, self.in_scope))
        return b

    def sb(self, name, shape, dt):
        return self._mk(name, shape, dt, False)

    def ps(self, name, shape, dt=F32):
        return self._mk(name, shape, dt, True)

    def ring(self, name, shape, dt, n, psum=False):
        return Ring([self._mk(f"{name}{i}", shape, dt, psum) for i in range(n)])

    def begin_scope(self):
        assert not self.in_scope
        self.in_scope = True
        self.ops.append(("begin",))

    def end_scope(self):
        assert self.in_scope
        self.in_scope = False
        self.ops.append(("end",))

    def op(self, engine, fn, reads=(), writes=(), dma=False, chain=None):
        o = Op(engine, fn, [_tok(r) for r in reads], [_tok(w) for w in writes], dma,
               _tok(chain) if chain is not None else None)
        self.ops.append(o)
        return o

    def dma(self, engine, fn, reads, writes, chain):
        return self.op(engine, fn, reads, writes, dma=True, chain=chain)

    def finalize(self):
        order = {}
        pending = {}
        scope_toks = []
        for idx, op in enumerate(self.ops):
            if isinstance(op, tuple):
                if op[0] == "alloc":
                    op[1].tok.readers = dict(pending)
                    if op[2]:
                        scope_toks.append(op[1].tok)
                elif op[0] == "begin":
                    scope_toks = []
                elif op[0] == "end":
                    for t in scope_toks:
                        cands = list(t.readers.items())
                        if t.last_w is not None:
                            lw = t.last_w
                            cands.append(((("dma", id(lw.chain)) if lw.dma else lw.engine), lw))
                        for k, o in cands:
                            if k not in pending or order[id(pending[k])] < order[id(o)]:
                                pending[k] = o
                    scope_toks = []
                continue
            order[id(op)] = idx
            deps = []
            for t in op.reads:
                if t.last_w is not None:
                    deps.append((t.last_w, True))
            for t in op.writes:
                if t.last_w is not None:
                    deps.append((t.last_w, False))
                deps.extend((r, False) for r in t.readers.values())
            need = []
            seen = set()
            for d, raw in deps:
                if d is op:
                    continue
                if (not d.dma) and (not op.dma) and d.engine == op.engine:
                    if not raw or op.engine == "pe":
                        continue
                if id(d) in seen:
                    continue
                seen.add(id(d))
                need.append(d)
                d.signal = True
            op.deps = need
            key = ("dma", id(op.chain)) if op.dma else op.engine
            for t in op.reads:
                t.readers[key] = op
            for t in op.writes:
                t.last_w = op
                t.readers = {}
        es_global = contextlib.ExitStack()
        es_scope = None

        def newsem(name):
            return es_global.enter_context(self.nc.semaphore(name))

        engsem = {e: newsem(f"sem_{e}") for e in ("pe", "act", "dve", "pool")}
        engcnt = {e: 0 for e in engsem}
        waited = {e: {} for e in self.eng}
        for op in self.ops:
            if isinstance(op, tuple):
                if op[0] == "alloc":
                    b = op[1]
                    st = es_scope if op[2] else es_global
                    f = self.nc.psum_tensor if b.psum else self.nc.sbuf_tensor
                    b.t = st.enter_context(f(b.name, b.shape, b.dt))
                elif op[0] == "begin":
                    es_scope = contextlib.ExitStack()
                elif op[0] == "end":
                    es_scope.close()
                    es_scope = None
                continue
            E = self.eng[op.engine]
            w = waited[op.engine]
            for d in op.deps:
                sem, val = d.ev
                k = id(sem)
                if w.get(k, 0) >= val:
                    continue
                E.wait_ge(sem, val)
                w[k] = val
            insts = op.fn(E) if op.fn is not None else None
            if op.dma:
                ch = op.chain
                if ch.sem is None:
                    ch.sem = newsem(f"dsem_{ch.name}")
                if not isinstance(insts, (list, tuple)):
                    insts = [insts]
                for ins in insts:
                    ins.then_inc(ch.sem, 16)
                    ch.cnt += 1
                op.ev = (ch.sem, 16 * ch.cnt)
            elif op.signal:
                assert insts is not None, "signalling op must emit an instruction"
                if isinstance(insts, (list, tuple)):
                    insts = insts[-1]
                engcnt[op.engine] += 1
                insts.then_inc(engsem[op.engine], 1)
                op.ev = (engsem[op.engine], engcnt[op.engine])
        es_global.close()


class Builder:
    def __init__(self, T, phases):
        self.T = T
        self.phases = phases
        nc = bass.Bass("TRN2", target_bir_lowering=False)
        self.nc = nc
        self.P = Prog(nc)
        self.decl = {}
        self.dbg_toks = []

    def din(self, name, shape, dt=F32):
        if name not in self.decl:
            self.decl[name] = self.nc.dram_tensor(name, list(shape), dt, kind="ExternalInput").ap()
        return self.decl[name]

    def dump(self, name, buf, shape, dt=F32, sl=None):
        if not getattr(self, "debug", False):
            return
        ap = self.nc.dram_tensor(name, list(shape), dt, kind="ExternalOutput").ap()
        tk = Tok("dbg" + name)
        self.dbg_toks.append(tk)
        self.P.dma("sp", lambda e: e.dma_start(out=ap, in_=(sl(buf) if sl else buf[:])), [buf], [tk], buf)

    def L(self, name, idx):
        return self.din(f"{name}__{idx}", LAYER_SHAPES[name])

    def build(self):
        nc, P, T = self.nc, self.P, self.T
        x = self.din("x", [T, D])
        cvec = self.din("c", [NCH, 128])
        final_g = self.din("final_g", [NCH, 128])
        k_ident = self.din("k_ident", [128, 128])
        k_tri = self.din("k_tri", [128, 128])
        y = nc.dram_tensor("y", [T, D], F32, kind="ExternalOutput").ap()
        self.xT_d = nc.dram_tensor("xT_scratch", [NCH, 128, T], F32, kind="Internal").ap()
        self.xT_tok = [Tok(f"xTd{i}") for i in range(T // TF)]

        ident = P.sb("ident", [128, 128], F32)
        ones_b = P.sb("ones_b", [128, 128], BF16)
        tri_f = P.sb("tri_f", [128, 128], F32)
        self.ident, self.ones_b, self.tri_f = ident, ones_b, tri_f
        P.dma("sp", lambda e: e.dma_start(out=ident[:], in_=k_ident[:, :]), [], [ident], ident)
        P.dma("sp", lambda e: e.dma_start(out=tri_f[:], in_=k_tri[:, :]), [], [tri_f], tri_f)
        P.op("dve", lambda e: e.memset(ones_b[:], 1.0), [], [ones_b])
        self.psum = P.ring("ps", [128, 512], F32, 7, psum=True)
        self.psb = P.ps("psb", [128, 1024], BF16)
        self.colscr = P.ring("colscr", [128, 128], F32, 2)

        self.phase_in(x)
        self.prologue(cvec, final_g)
        for kind, l in self.phases:
            if kind == "ffn":
                self.phase_ffn(l)
            elif kind == "gm":
                self.phase_gm(l, l // 3)
            elif kind == "mla":
                self.phase_mla(l)
            elif kind == "hg":
                self.phase_hg(l)
        self.phase_out(y)
        P.finalize()
        return nc

    def work(self, n, nxt=2, sq=True, nfw=3, nrstd=2):
        P = self.P
        self.n = n
        self.xt_ring = P.ring("xt", [128, NCH, n], F32, nxt)
        self.ht_ring = P.ring("ht", [128, NCH, n], BF16, 1)
        if sq:
            self.sq_ring = P.ring("sq", [128, NCH, n], BF16, 1)
        self.fw = P.ring("fw", [128, n], F32, nfw)
        self.rstd_ring = P.ring("rstd", [128, n], F32, nrstd)

    def load_cols(self, name, src_rows_ap, n):
        P = self.P
        dst = P.sb(name, [128, n], F32)
        scr = self.colscr.next()
        P.dma("sp", lambda e: e.dma_start(out=scr[0:n, :], in_=src_rows_ap), [], [scr], scr)
        ps = self.psum.next()
        P.op("pe", lambda e: e.transpose(ps[:, 0:n], scr[0:n, :], self.ident[0:n, 0:n]),
             [scr, self.ident], [ps])
        P.op("dve", lambda e: e.tensor_copy(dst[:], ps[:, 0:n]), [ps], [dst])
        return dst

    def prologue(self, cvec, final_g):
        P = self.P
        cT = self.load_cols("cT", cvec[:, :], NCH)
        cact = P.sb("cact", [128, NCH], F32)
        cact_b = P.sb("cact_b", [128, NCH], BF16)
        P.op("act", lambda e: e.activation(cact[:], cT[:], AF.Silu), [cT], [cact])
        P.op("dve", lambda e: e.tensor_copy(cact_b[:], cact[:]), [cact], [cact_b])
        self.final_g = self.load_cols("fing", final_g[:, :], NCH)
        layers = sorted(set(l for _, l in self.phases))
        self.G1, self.SH1, self.GA1, self.G2, self.SH2, self.GA2 = {}, {}, {}, {}, {}, {}
        mods, mgs, fgs, abs_ = {}, {}, {}, {}
        for l in layers:
            abs_[l] = self.load_cols(f"adab{l}", self.L("ada_b", l)[:, :], 48)
            mgs[l] = self.load_cols(f"mixg{l}", self.L("mix_norm_g", l)[:, :], NCH)
            fgs[l] = self.load_cols(f"ffng{l}", self.L("ffn_norm_g", l)[:, :], NCH)
            mods[l] = P.sb(f"mod{l}", [128, 48], F32)
            for nm, dd in (("Ga", self.G1), ("GAa", self.GA1), ("Gb", self.G2), ("GAb", self.GA2)):
                dd[l] = P.sb(f"{nm}{l}", [128, NCH], F32)
        P.begin_scope()
        wring = P.ring("adaw", [128, NCH, 1536], BF16, 2)
        for l in layers:
            mod, ab = mods[l], abs_[l]
            ps = self.psum.next()
            for q in range(4):
                wb = wring.next()
                src = self.L("ada_w", l)[:, q * 1536:(q + 1) * 1536].rearrange("(k p) n -> p k n", p=128)
                P.dma("pool", lambda e, wb=wb, src=src: e.dma_start(out=wb[:], in_=src), [], [wb], wb)
                for cc in range(12):
                    col = q * 12 + cc
                    for k in range(NCH):
                        P.op("pe", lambda e, wb=wb, cc=cc, k=k, col=col, ps=ps: e.matmul(
                            ps[:, col:col + 1], wb[:, k, cc * 128:(cc + 1) * 128], cact_b[:, k:k + 1],
                            start=(k == 0), stop=(k == NCH - 1)), [wb, cact_b], [ps])
            P.op("dve", lambda e, mod=mod, ps=ps, ab=ab: e.tensor_tensor(mod[:], ps[:, 0:48], ab[:], ALU.add),
                 [ps, ab], [mod])

            def derive(G, GA, gsrc, sc_off, g_off, mod=mod):
                P.op("dve", lambda e: e.scalar_tensor_tensor(
                    G[:], mod[:, sc_off:sc_off + 8], 1.0, gsrc[:], ALU.add, ALU.mult), [mod, gsrc], [G])
                P.op("dve", lambda e: e.tensor_scalar(GA[:], mod[:, g_off:g_off + 8], 1.0, None, ALU.add),
                     [mod], [GA])
            derive(self.G1[l], self.GA1[l], mgs[l], 8, 16)
            derive(self.G2[l], self.GA2[l], fgs[l], 32, 40)
            self.SH1[l] = (mod, 0)
            self.SH2[l] = (mod, 24)
        P.end_scope()

    def slab_toks(self, t0, n):
        return self.xT_tok[t0 // TF:(t0 + n) // TF]

    def load_xt(self, t0, n):
        P = self.P
        xt = self.xt_ring.next()
        src = self.xT_d[:, :, t0:t0 + n].rearrange("c p t -> p c t")
        P.dma("sp", lambda e: e.dma_start(out=xt[:, :, 0:n], in_=src), self.slab_toks(t0, n), [xt], xt)
        return xt

    def store_xt(self, xt, t0, n):
        P = self.P
        dst = self.xT_d[:, :, t0:t0 + n].rearrange("c p t -> p c t")
        P.dma("sp", lambda e: e.dma_start(out=dst, in_=xt[:, :, 0:n]), [xt], self.slab_toks(t0, n), xt)

    def rstd_of(self, xt, n, sq=None):
        P = self.P
        if sq is None:
            sq = self.sq_ring.next()
        P.op("act", lambda e: e.activation(sq[:, 0:NCH, 0:n], xt[:, :, 0:n], AF.Square), [xt], [sq])
        ps = self.psum.next()
        for c in range(NCH):
            P.op("pe", lambda e, c=c: e.matmul(ps[:, 0:n], self.ones_b[:], sq[:, c, 0:n],
                                               start=(c == 0), stop=(c == NCH - 1)), [sq, self.ones_b], [ps])
        tmp = self.fw.next()
        P.op("act", lambda e: e.activation(tmp[:, 0:n], ps[:, 0:n], AF.Sqrt, scale=1.0 / D, bias=EPS), [ps], [tmp])
        rstd = self.rstd_ring.next()
        P.op("dve", lambda e: e.reciprocal(rstd[:, 0:n], tmp[:, 0:n]), [tmp], [rstd])
        return rstd

    def norm_mod(self, xt, n, G, SH, ht=None, c0=0, sq=None):
        P = self.P
        rstd = self.rstd_of(xt, n, sq=sq)
        if ht is None:
            ht = self.ht_ring.next()
        shb, sho = SH
        for c in range(NCH):
            tmp = self.fw.next()
            P.op("dve" if c % 2 == 0 else "pool", lambda e, c=c, tmp=tmp: e.tensor_tensor(
                tmp[:, 0:n], xt[:, c, 0:n], rstd[:, 0:n], ALU.mult), [xt, rstd], [tmp])
            P.op("act", lambda e, c=c, tmp=tmp: e.activation(
                ht[:, c, c0:c0 + n], tmp[:, 0:n], AF.Identity, scale=G[:, c:c + 1], bias=shb[:, sho + c:sho + c + 1]),
                [tmp, G, shb], [ht])
        return ht

    def phase_in(self, x):
        P, T = self.P, self.T
        P.begin_scope()
        self.work(TT)
        xin_ring = P.ring("xin", [128, 4, D], F32, 2)
        def load_in(ti):
            xin = xin_ring.next()
            src = x[ti * TT:(ti + 1) * TT, :].rearrange("(s p) d -> p s d", p=128)
            P.dma("sp", lambda e, xin=xin, src=src: e.dma_start(out=xin[:], in_=src), [], [xin], xin)
            return xin

        nxt = load_in(0)
        for ti in range(T // TT):
            t0 = ti * TT
            xin = nxt
            if ti + 1 < T // TT:
                nxt = load_in(ti + 1)
            xt = self.xt_ring.next()
            for c in range(NCH):
                ps = self.psum.next()
                for s in range(4):
                    P.op("pe", lambda e, ps=ps, s=s, c=c, xin=xin: e.transpose(
                        ps[:, s * 128:(s + 1) * 128], xin[:, s, c * 128:(c + 1) * 128], self.ident[:]),
                        [xin, self.ident], [ps])
                if c % 2 == 0:
                    P.op("act", lambda e, ps=ps, c=c, xt=xt: e.copy(xt[:, c, :], ps[:]), [ps], [xt])
                else:
                    P.op("dve", lambda e, ps=ps, c=c, xt=xt: e.tensor_copy(xt[:, c, :], ps[:]), [ps], [xt])
            self.store_xt(xt, t0, TT)
        P.end_scope()

    def phase_out(self, y):
        P, T = self.P, self.T
        P.begin_scope()
        self.work(TT)
        yo_ring = P.ring("yo", [128, 4, D], F32, 2)
        yts = P.ring("yT", [128, NCH, TT], F32, 1)
        out_toks = []
        nxt = self.load_xt(0, TT)
        for ti in range(T // TT):
            t0 = ti * TT
            xt = nxt
            if ti + 1 < T // TT:
                nxt = self.load_xt(t0 + TT, TT)
            rstd = self.rstd_of(xt, TT)
            yT = yts.next()
            for c in range(NCH):
                P.op("dve", lambda e, c=c, xt=xt, yT=yT, rstd=rstd: e.scalar_tensor_tensor(
                    yT[:, c, :], xt[:, c, :], self.final_g[:, c:c + 1], rstd[:], ALU.mult, ALU.mult),
                    [xt, rstd, self.final_g], [yT])
            yo = yo_ring.next()
            for s in range(4):
                for h in range(2):
                    ps = self.psum.next()
                    for cc in range(4):
                        c = h * 4 + cc
                        P.op("pe", lambda e, ps=ps, s=s, c=c, cc=cc, yT=yT: e.transpose(
                            ps[:, cc * 128:(cc + 1) * 128], yT[:, c, s * 128:(s + 1) * 128], self.ident[:]),
                            [yT, self.ident], [ps])
                    if (s + h) % 2 == 0:
                        P.op("act", lambda e, ps=ps, s=s, h=h, yo=yo: e.copy(yo[:, s, h * 512:(h + 1) * 512], ps[:]),
                             [ps], [yo])
                    else:
                        P.op("dve", lambda e, ps=ps, s=s, h=h, yo=yo: e.tensor_copy(yo[:, s, h * 512:(h + 1) * 512], ps[:]),
                             [ps], [yo])
            dst = y[t0:t0 + TT, :].rearrange("(s p) d -> p s d", p=128)
            tk = Tok(f"yout{ti}")
            out_toks.append(tk)
            P.dma("sp", lambda e, yo=yo, dst=dst: e.dma_start(out=dst, in_=yo[:]), [yo], [tk], yo)
        P.op("sp", None, reads=out_toks + self.dbg_toks, writes=[])
        P.end_scope()

    def phase_ffn(self, l):
        P, T = self.P, self.T
        ff_w_up, ff_conv_w, ff_conv_b, ff_w_down = (self.L(k, l) for k in ("ff_w_up", "ff_conv_w", "ff_conv_b", "ff_w_down"))
        n = TF
        cw = [self.load_cols(f"cw{l}_{j}", ff_conv_w[j, :, :], 44) for j in range(3)]
        cb = self.load_cols(f"cb{l}", ff_conv_b[:, :], 44)
        P.begin_scope()
        self.work(n)
        wup = P.sb("wup", [128, NCH, 2 * FF], BF16)
        wdn = P.sb("wdn", [128, NFC, D], BF16)
        hts = P.ring("hth", [128, NCH, n + 2], BF16, 2)
        ybufs = P.ring("ybuf", [128, n], F32, 8)
        sgs = P.ring("sgb", [128, n], F32, 3)
        actT = P.sb("actT", [128, NFC, n], BF16)
        for h in range(4):
            src = ff_w_up[:, h * 1408:(h + 1) * 1408].rearrange("(k p) n -> p k n", p=128)
            P.dma("pool", lambda e, src=src, h=h: e.dma_start(out=wup[:, :, h * 1408:(h + 1) * 1408], in_=src),
                  [], [wup], wup)
        for h in range(2):
            src = ff_w_down[h * 1408:(h + 1) * 1408, :].rearrange("(j p) n -> p j n", p=128)
            P.dma("pool", lambda e, src=src, h=h: e.dma_start(out=wdn[:, h * 11:(h + 1) * 11, :], in_=src),
                  [], [wdn], wdn)
        G2, SH2, GA2 = self.G2[l], self.SH2[l], self.GA2[l]
        def prep(ti, prev):
            xt = self.load_xt(ti * n, n)
            ht = hts.next()
            if prev is None:
                P.op("dve", lambda e, ht=ht: e.memset(ht[:, :, 0:2], 0.0), [], [ht])
            else:
                P.op("dve", lambda e, ht=ht, prev=prev: e.tensor_copy(ht[:, :, 0:2], prev[:, :, n:n + 2]), [prev], [ht])
            self.norm_mod(xt, n, G2, SH2, ht=ht, c0=2)
            return xt, ht

        nxt_prep = prep(0, None)
        for ti in range(T // n):
            t0 = ti * n
            xt, ht = nxt_prep
            pend_g = None

            def gate(yg, yv, j):
                sg = sgs.next()
                P.op("act", lambda e: e.activation(sg[:], yg[:], AF.Silu), [yg], [sg])
                P.op("pool", lambda e: e.tensor_tensor(actT[:, j, :], sg[:], yv[:], ALU.mult), [sg, yv], [actT])
            for j in range(NFC):
                ys = []
                for half in range(2):
                    ch = half * NFC + j
                    ps = self.psum.next()
                    for k in range(NCH):
                        P.op("pe", lambda e, ps=ps, k=k, ch=ch, ht=ht: e.matmul(
                            ps[:, 0:n + 2], wup[:, k, ch * 128:(ch + 1) * 128], ht[:, k, :],
                            start=(k == 0), stop=(k == NCH - 1)), [wup, ht], [ps])
                    yb = ybufs.next()
                    P.op("act", lambda e, yb=yb, ps=ps, ch=ch: e.activation(
                        yb[:], ps[:, 2:n + 2], AF.Identity, scale=cw[2][:, ch:ch + 1], bias=cb[:, ch:ch + 1]),
                        [ps, cw[2], cb], [yb])
                    P.op("dve", lambda e, yb=yb, ps=ps, ch=ch: e.scalar_tensor_tensor(
                        yb[:], ps[:, 1:n + 1], cw[1][:, ch:ch + 1], yb[:], ALU.mult, ALU.add), [ps, yb, cw[1]], [yb])
                    P.op("dve", lambda e, yb=yb, ps=ps, ch=ch: e.scalar_tensor_tensor(
                        yb[:], ps[:, 0:n], cw[0][:, ch:ch + 1], yb[:], ALU.mult, ALU.add), [ps, yb, cw[0]], [yb])
                    ys.append(yb)
                if pend_g is not None:
                    gate(*pend_g)
                pend_g = (ys[0], ys[1], j)
            gate(*pend_g)
            pend_g = None
            if ti + 1 < T // n:
                nxt_prep = prep(ti + 1, ht)
            for c in range(NCH):
                ps = self.psum.next()
                for j in range(NFC):
                    P.op("pe", lambda e, ps=ps, j=j, c=c: e.matmul(
                        ps[:, 0:n], wdn[:, j, c * 128:(c + 1) * 128], actT[:, j, :],
                        start=(j == 0), stop=(j == NFC - 1)), [wdn, actT], [ps])
                P.op("dve", lambda e, ps=ps, c=c, xt=xt: e.scalar_tensor_tensor(
                    xt[:, c, 0:n], ps[:, 0:n], GA2[:, c:c + 1], xt[:, c, 0:n], ALU.mult, ALU.add),
                    [ps, xt, GA2], [xt])
            self.store_xt(xt, t0, n)
        P.end_scope()

    def pipeline(self, items, stages):
        ns = len(stages)
        for step in range(len(items) + ns - 1):
            for s in reversed(range(ns)):
                i = step - s
                if 0 <= i < len(items):
                    stages[s](items[i])

    def phase_gm(self, l, j):
        P, T = self.P, self.T
        gm_w_in, gm_ln_g, gm_ln_b, gm_w_s, gm_b_s, gm_w_out = (self.L(k, j) for k in ("gm_w_in", "gm_ln_g", "gm_ln_b", "gm_w_s", "gm_b_s", "gm_w_out"))
        k_sel = self.din("k_sel", [8, 1024])
        n = TT
        lng = self.load_cols(f"lng{l}", gm_ln_g[:, :], 16)
        lnb = self.load_cols(f"lnb{l}", gm_ln_b[:, :], 16)
        P.begin_scope()
        self.work(n, nxt=1, sq=False, nfw=2, nrstd=1)
        xs_r = P.ring("gxs", [128, n], F32, 4)
        t_r = P.ring("gt", [128, n], F32, 6)
        win = P.sb("win", [128, NCH, 4096], BF16)
        wout = P.sb("wout", [128, 16, D], BF16)
        wsT = P.sb("wsT", [128, 8, 128], BF16)
        CT = P.sb("CT", [128, 16, 128], F32)
        uT = P.sb("uT", [128, 16, n], BF16)
        self.sq_ring = Ring([uT])
        vg = P.sb("vg", [128, GM_INNER], F32)
        vhat = P.sb("vhat", [128, 4, GM_INNER], BF16)
        stats = P.sb("stats", [128, 4, 6], F32)
        mv = P.sb("mv", [128, 2], F32)
        for h in range(4):
            src = gm_w_in[:, h * 1024:(h + 1) * 1024].rearrange("(k p) n -> p k n", p=128)
            P.dma("pool", lambda e, src=src, h=h: e.dma_start(out=win[:, :, h * 1024:(h + 1) * 1024], in_=src),
                  [], [win], win)
        src = gm_w_out[:, :].rearrange("(f p) n -> p f n", p=128)
        P.dma("pool", lambda e, src=src: e.dma_start(out=wout[:], in_=src), [], [wout], wout)
        wsl_r = P.ring("wsl", [128, 128], F32, 2)
        bsl = P.sb("bsl", [8, 128], F32)
        sel_r = P.ring("sel", [8, 128], F32, 2)
        P.dma("sp", lambda e: e.dma_start(out=bsl[:], in_=gm_b_s[:, :]), [], [bsl], bsl)
        bsb = P.sb("bsb", [128, 128], F32)
        for g in range(8):
            wsl = wsl_r.next()
            sel = sel_r.next()
            P.dma("sp", lambda e, wsl=wsl, g=g: e.dma_start(out=wsl[:], in_=gm_w_s[g, :, :]), [], [wsl], wsl)
            P.dma("sp", lambda e, sel=sel, g=g: e.dma_start(out=sel[:], in_=k_sel[:, g * 128:(g + 1) * 128]), [], [sel], sel)
            ps = self.psum.next()
            P.op("pe", lambda e, ps=ps, wsl=wsl: e.transpose(ps[:, 0:128], wsl[:], self.ident[:]),
                 [wsl, self.ident], [ps])
            P.op("dve", lambda e, ps=ps, g=g: e.tensor_tensor(wsT[:, g, :], ps[:, 0:128], self.tri_f[:], ALU.mult),
                 [ps, self.tri_f], [wsT])
            ps2 = self.psum.next()
            P.op("pe", lambda e, ps2=ps2, g=g: e.matmul(ps2[:, 0:128], self.ones_b[:], wsT[:, g, :], start=True, stop=True),
                 [wsT, self.ones_b], [ps2])
            P.op("pe", lambda e, ps2=ps2, sel=sel: e.matmul(ps2[:, 128:256], sel[:, :], bsl[:, :],
                                                            start=True, stop=True), [sel, bsl], [ps2])
            P.op("act", lambda e, ps2=ps2: e.copy(bsb[:], ps2[:, 128:256]), [ps2], [bsb])
            for q in range(2):
                fc = 2 * g + q
                P.op("dve", lambda e, ps2=ps2, fc=fc: e.scalar_tensor_tensor(
                    CT[:, fc, :], ps2[:, 0:128], lnb[:, fc:fc + 1], bsb[:], ALU.mult, ALU.add),
                    [ps2, lnb, bsb], [CT])
        G1, SH1, GA1 = self.G1[l], self.SH1[l], self.GA1[l]
        XV = AliasBuf(vhat, lambda t: t[:].bitcast(F32).rearrange("p a (b t) -> p (a b) t", t=n))
        SQV = AliasBuf(vg, lambda t: t[:].bitcast(BF16).rearrange("p (c t) -> p c t", t=n))
        xt = None
        for ti in range(T // n):
            t0 = ti * n
            if ti == 0:
                xt = self.load_xt(t0, n)
                ht = self.norm_mod(xt, n, G1, SH1)
            items = [("v", s_, nb) for s_ in range(4) for nb in range(4)] + [("u", fc, 0) for fc in range(16)]
            st = {}

            def s_mm(it, ht=ht):
                kind, a_, b_ = it
                ps = self.psum.next()
                st[it] = {"ps": ps}
                for k in range(NCH):
                    if kind == "u":
                        P.op("pe", lambda e, ps=ps, k=k, fc=a_: e.matmul(
                            ps[:, :], win[:, k, fc * 128:(fc + 1) * 128], ht[:, k, :],
                            start=(k == 0), stop=(k == NCH - 1)), [win, ht], [ps])
                    else:
                        P.op("pe", lambda e, ps=ps, k=k, s_=a_, nb=b_: e.matmul(
                            ps[:, :], ht[:, k, s_ * 128:(s_ + 1) * 128], win[:, k, 2048 + nb * 512:2048 + (nb + 1) * 512],
                            start=(k == 0), stop=(k == NCH - 1)), [win, ht], [ps])

            def s_copy(it):
                d = st[it]
                d["xs"] = xs_r.next()
                d["t"] = t_r.next()
                P.op("act", lambda e, d=d: e.activation(d["xs"][:], d["ps"][:, :], AF.Identity), [d["ps"]], [d["xs"]])
                P.op("act", lambda e, d=d: e.activation(d["t"][:], d["ps"][:, :], AF.Square, scale=0.21145921592590375),
                     [d["ps"]], [d["t"]])

            def s_poly(it):
                d = st[it]
                P.op("dve", lambda e, d=d: e.scalar_tensor_tensor(
                    d["t"][:], d["t"][:], 1.0, d["xs"][:], ALU.add, ALU.mult), [d["t"], d["xs"]], [d["t"]])

            def s_sig(it):
                d = st[it]
                P.op("act", lambda e, d=d: e.activation(d["t"][:], d["t"][:], AF.Sigmoid, scale=1.5957691216057308),
                     [d["t"]], [d["t"]])

            def s_out(it):
                kind, a_, b_ = it
                d = st[it]
                if kind == "u":
                    P.op("dve", lambda e, d=d, fc=a_: e.tensor_tensor(uT[:, fc, :], d["t"][:], d["xs"][:], ALU.mult),
                         [d["t"], d["xs"]], [uT])
                else:
                    P.op("dve", lambda e, d=d, nb=b_: e.tensor_tensor(
                        vg[:, nb * 512:(nb + 1) * 512], d["t"][:], d["xs"][:], ALU.mult), [d["t"], d["xs"]], [vg])

            def s_sp_mm(it):
                kind, fc, _ = it
                d = st[it]
                d["sp"] = self.psum.next()
                for s2 in range(4):
                    P.op("pe", lambda e, d=d, s2=s2, fc=fc: e.matmul(
                        d["sp"][:, s2 * 128:(s2 + 1) * 128], vhat[:, s2, fc * 128:(fc + 1) * 128], wsT[:, fc // 2, :],
                        start=True, stop=True), [vhat, wsT], [d["sp"]])

            def s_sp_evac(it):
                kind, fc, _ = it
                if kind != "u":
                    return
                d = st[it]
                d["tmp"] = t_r.next()
                P.op("dve", lambda e, d=d, fc=fc: e.scalar_tensor_tensor(
                    d["tmp"][:].rearrange("p (s t) -> p s t", s=4), d["sp"][:, :].rearrange("p (s t) -> p s t", s=4),
                    lng[:, fc:fc + 1], CT[:, fc:fc + 1, :].to_broadcast([128, 4, 128]),
                    ALU.mult, ALU.add), [d["sp"], lng, CT], [d["tmp"]])

            def s_ymul(it):
                kind, fc, _ = it
                if kind != "u":
                    return
                d = st[it]
                P.op("dve", lambda e, d=d, fc=fc: e.tensor_tensor(uT[:, fc, :], uT[:, fc, :], d["tmp"][:], ALU.mult),
                     [uT, d["tmp"]], [uT])

            def s_ln(it):
                kind, s_, nb = it
                if kind != "v":
                    s_sp_mm(it)
                    return
                P.op("dve", lambda e, nb=nb: e.bn_stats(stats[:, nb, :], vg[:, nb * 512:(nb + 1) * 512]), [vg], [stats])
                if nb == 3:
                    P.op("dve", lambda e: e.bn_aggr(mv[:], stats[:].rearrange("p a b -> p (a b)")), [stats], [mv])
                    P.op("act", lambda e: e.activation(mv[:, 1:2], mv[:, 1:2], AF.Sqrt, bias=EPS), [mv], [mv])
                    P.op("dve", lambda e: e.reciprocal(mv[:, 1:2], mv[:, 1:2]), [mv], [mv])
                    P.op("dve", lambda e: e.scalar_tensor_tensor(mv[:, 0:1], mv[:, 0:1], -1.0, mv[:, 1:2], ALU.mult, ALU.mult),
                         [mv], [mv])
                    P.op("act", lambda e, s_=s_: e.activation(vhat[:, s_, :], vg[:], AF.Identity, scale=mv[:, 1:2], bias=mv[:, 0:1]),
                         [vg, mv], [vhat])

            self.pipeline(items, [s_mm, s_copy, s_poly, s_sig, s_out, s_ln, s_sp_evac, s_ymul])
            last = ti + 1 == T // n
            if not last:
                src = self.xT_d[:, :, t0 + n:t0 + 2 * n].rearrange("c p t -> p c t")
                P.dma("sp", lambda e, src=src: e.dma_start(out=XV[:, :, :], in_=src), self.slab_toks(t0 + n, n), [XV], XV)
                ht = self.norm_mod(XV, n, G1, SH1, sq=SQV)
            for c in range(NCH):
                ps = self.psum.next()
                for fc in range(16):
                    P.op("pe", lambda e, ps=ps, fc=fc, c=c: e.matmul(
                        ps[:, :], wout[:, fc, c * 128:(c + 1) * 128], uT[:, fc, :],
                        start=(fc == 0), stop=(fc == 15)), [wout, uT], [ps])
                P.op("dve", lambda e, ps=ps, c=c, xt=xt: e.scalar_tensor_tensor(
                    xt[:, c, :], ps[:, :], GA1[:, c:c + 1], xt[:, c, :], ALU.mult, ALU.add),
                    [ps, xt, GA1], [xt])
            self.store_xt(xt, t0, n)
            if not last:
                P.op("act", lambda e, xt=xt: e.copy(xt[:, :, :], XV[:, :, :]), [XV], [xt])
        P.end_scope()

    def rope_tables(self, cos2, sin2):
        P, T = self.P, self.T
        pos = self.din("positions", [1, T], I32)
        k_invf = self.din("k_invf", [2, 128])
        posi = P.sb("posi", [1, T], I32)
        posf = P.sb("posf", [1, T], F32)
        invf = P.sb("invf", [1, 128], F32)
        sgn = self.load_cols("ropesgn", k_invf[:, :], 2)
        P.dma("sp", lambda e: e.dma_start(out=posi[:], in_=pos[:, :]), [], [posi], posi)
        P.dma("sp", lambda e: e.dma_start(out=invf[:], in_=k_invf[0:1, :]), [], [invf], invf)
        P.op("dve", lambda e: e.tensor_copy(posf[:], posi[:]), [posi], [posf])
        ki = P.sb("ropek", [128, 512], I32)
        kf = P.sb("ropekf", [128, 512], F32)
        r = P.sb("roper", [128, 512], F32)
        m = P.sb("ropem", [128, 512], F32)
        TWO_PI = 6.283185307179586
        for b in range(T // 512):
            ps = self.psum.next()
            P.op("pe", lambda e, ps=ps, b=b: e.matmul(ps[:, :], invf[:, :], posf[:, b * 512:(b + 1) * 512],
                                                      start=True, stop=True), [invf, posf], [ps])
            for which, dst in ((0, sin2), (1, cos2)):
                P.op("dve", lambda e, ps=ps, which=which: e.tensor_scalar(
                    r[:], ps[:, :], 1.0 / TWO_PI, 0.25 * which, ALU.mult, ALU.add), [ps], [r])
                P.op("dve", lambda e: e.tensor_copy(ki[:], r[:]), [r], [ki])
                P.op("dve", lambda e: e.tensor_copy(kf[:], ki[:]), [ki], [kf])
                P.op("dve", lambda e: e.tensor_tensor(r[:], r[:], kf[:], ALU.subtract), [r, kf], [r])
                P.op("dve", lambda e: e.tensor_scalar(m[:], r[:], 0.5, None, ALU.is_gt), [r], [m])
                P.op("dve", lambda e: e.tensor_tensor(r[:], r[:], m[:], ALU.subtract), [r, m], [r])
                P.op("dve", lambda e: e.tensor_scalar(m[:], r[:], -0.5, None, ALU.is_lt), [r], [m])
                P.op("dve", lambda e: e.tensor_tensor(r[:], r[:], m[:], ALU.add), [r, m], [r])
                if which == 0:
                    P.op("act", lambda e, b=b: e.activation(m[:], r[:], AF.Sin, scale=6.28318), [r], [m])
                    P.op("dve", lambda e, b=b, dst=dst: e.tensor_scalar(
                        dst[:, b * 512:(b + 1) * 512], m[:], sgn[:, 1:2], None, ALU.mult), [m, sgn], [dst])
                else:
                    P.op("act", lambda e, b=b, dst=dst: e.activation(
                        dst[:, b * 512:(b + 1) * 512], r[:], AF.Sin, scale=6.28318), [r], [dst])

    def phase_mla(self, l):
        P, T = self.P, self.T
        nc = self.nc
        n = TT
        NTL = T // n
        w_a, qg_d, kvg_d, w_qb, w_kvb, w_o = (self.L(k, 0) for k in (
            "mla_w_a", "mla_q_norm_g", "mla_kv_norm_g", "mla_w_qb", "mla_w_kvb", "mla_w_o"))
        qg = self.load_cols("mlaqg", qg_d[:, :], 2)
        kvg = self.load_cols("mlakvg", kvg_d[:, :], 1)
        qn_d = nc.dram_tensor("qn_d", [8, 128, T], BF16, kind="Internal").ap()
        qr_d = nc.dram_tensor("qr_d", [4, 128, T], BF16, kind="Internal").ap()
        kn_d = nc.dram_tensor("kn_d", [8, 128, T], BF16, kind="Internal").ap()
        kr_d = nc.dram_tensor("kr_d", [128, T], BF16, kind="Internal").ap()
        v_d = nc.dram_tensor("v_d", [T, 1024], BF16, kind="Internal").ap()
        oT_d = nc.dram_tensor("oT_d", [8, 128, T], BF16, kind="Internal").ap()
        tk = {nm: [Tok(f"{nm}{i}") for i in range(NTL)] for nm in ("qn", "qr", "kn", "kr", "v")}
        otk = [Tok(f"oT{h}") for h in range(8)]
        G1, SH1, GA1 = self.G1[l], self.SH1[l], self.GA1[l]

        P.begin_scope()
        self.work(n)
        cos2 = P.sb("cos2", [128, T], F32)
        sin2 = P.sb("sin2", [128, T], F32)
        self.rope_tables(cos2, sin2)
        wa = P.sb("wa", [128, NCH, 384], BF16)
        wkr = P.sb("wkr", [128, NCH, 256], BF16)
        wqn = P.sb("wqn", [128, 2, 8, 128], BF16)
        wqrA = P.sb("wqrA", [128, 2, 8, 64], BF16)
        wqrB = P.sb("wqrB", [128, 2, 8, 64], BF16)
        wkn = P.sb("wkn", [128, 8, 128], BF16)
        wv = P.sb("wv", [128, 8, 128], BF16)
        wa_v = w_a.rearrange("(k p) n -> p k n", p=128)
        P.dma("pool", lambda e: e.dma_start(out=wa[:], in_=wa_v[:, :, 0:384]), [], [wa], wa)

        def wkr_load(e):
            r = []
            for dup in range(2):
                r.append(e.dma_start(out=wkr[:, :, dup * 64:dup * 64 + 64], in_=wa_v[:, :, 384:448]))
                r.append(e.dma_start(out=wkr[:, :, 128 + dup * 64:128 + dup * 64 + 32], in_=wa_v[:, :, 416:448]))
                r.append(e.dma_start(out=wkr[:, :, 160 + dup * 64:160 + dup * 64 + 32], in_=wa_v[:, :, 384:416]))
            return r
        P.dma("pool", wkr_load, [], [wkr], wkr)
        wq_v = w_qb.rearrange("(k p) (h e) -> p k h e", p=128, e=192)
        P.dma("pool", lambda e: [e.dma_start(out=wqn[:, kc, :, :], in_=wq_v[:, kc, :, 0:128]) for kc in range(2)],
              [], [wqn], wqn)
        P.dma("pool", lambda e: [e.dma_start(out=wqrA[:, kc, :, :], in_=wq_v[:, kc, :, 128:192]) for kc in range(2)],
              [], [wqrA], wqrA)
        P.dma("pool", lambda e: [e.dma_start(out=wqrB[:, kc, :, 0:32], in_=wq_v[:, kc, :, 160:192]) for kc in range(2)] +
                                [e.dma_start(out=wqrB[:, kc, :, 32:64], in_=wq_v[:, kc, :, 128:160]) for kc in range(2)],
              [], [wqrB], wqrB)
        wkv_v = w_kvb.rearrange("p (h two e) -> p two h e", two=2, e=128)
        P.dma("pool", lambda e: e.dma_start(out=wkn[:], in_=wkv_v[:, 0, :, :]), [], [wkn], wkn)
        P.dma("pool", lambda e: e.dma_start(out=wv[:], in_=wkv_v[:, 1, :, :]), [], [wv], wv)
        cqn = P.sb("cqn", [128, 2, n], BF16)
        ckvn = P.sb("ckvn", [128, n], BF16)
        sqs = P.ring("sqs", [128, n], BF16, 2)
        qn_t = P.sb("qn_t", [128, 8, n], BF16)
        qr_t = P.sb("qr_t", [128, 4, n], BF16)
        kn_t = P.sb("kn_t", [128, 8, n], BF16)
        kr_t = P.sb("kr_t", [128, n], BF16)
        v_t = P.sb("v_t", [128, 4, 1024], BF16)
        rt = P.ring("rt", [128, n], F32, 3)

        def small_rstd(pss, dim):
            ps_s = self.psum.next()
            for i, pz in enumerate(pss):
                sq = sqs.next()
                P.op("act", lambda e, sq=sq, pz=pz: e.activation(sq[:], pz[:, :], AF.Square), [pz], [sq])
                P.op("pe", lambda e, sq=sq, i=i: e.matmul(ps_s[:, :], self.ones_b[:], sq[:],
                                                          start=(i == 0), stop=(i == len(pss) - 1)),
                     [sq, self.ones_b], [ps_s])
            tmp = self.fw.next()
            P.op("act", lambda e: e.activation(tmp[:], ps_s[:, :], AF.Sqrt, scale=1.0 / dim, bias=EPS), [ps_s], [tmp])
            rs = self.rstd_ring.next()
            P.op("dve", lambda e: e.reciprocal(rs[:], tmp[:]), [tmp], [rs])
            return rs

        def rotate(psA, psB, dst_fn, writes, t0):
            ta, tb = rt.next(), rt.next()
            P.op("dve", lambda e: e.tensor_tensor(ta[:], psA[:, :], cos2[:, t0:t0 + n], ALU.mult), [psA, cos2], [ta])
            P.op("dve", lambda e: e.tensor_tensor(tb[:], psB[:, :], sin2[:, t0:t0 + n], ALU.mult), [psB, sin2], [tb])
            P.op("pool", lambda e: e.tensor_tensor(dst_fn(), ta[:], tb[:], ALU.add), [ta, tb], writes)

        nxt = self.load_xt(0, n)
        for ti in range(NTL):
            t0 = ti * n
            xt = nxt
            if ti + 1 < NTL:
                nxt = self.load_xt(t0 + n, n)
            ht = self.norm_mod(xt, n, G1, SH1)
            pcq = []
            for c in range(3):
                ps = self.psum.next()
                for k in range(NCH):
                    P.op("pe", lambda e, ps=ps, k=k, c=c, ht=ht: e.matmul(
                        ps[:, :], wa[:, k, c * 128:(c + 1) * 128], ht[:, k, :],
                        start=(k == 0), stop=(k == NCH - 1)), [wa, ht], [ps])
                pcq.append(ps)
            rs_q = small_rstd(pcq[0:2], 256)
            for c in range(2):
                P.op("dve", lambda e, c=c, rs_q=rs_q, ps=pcq[c]: e.scalar_tensor_tensor(
                    cqn[:, c, :], ps[:, :], qg[:, c:c + 1], rs_q[:], ALU.mult, ALU.mult), [pcq[c], qg, rs_q], [cqn])
            rs_kv = small_rstd(pcq[2:3], 128)
            P.op("dve", lambda e, rs_kv=rs_kv, ps=pcq[2]: e.scalar_tensor_tensor(
                ckvn[:], ps[:, :], kvg[:, 0:1], rs_kv[:], ALU.mult, ALU.mult), [pcq[2], kvg, rs_kv], [ckvn])
            pab = []
            for ab in range(2):
                ps = self.psum.next()
                for k in range(NCH):
                    P.op("pe", lambda e, ps=ps, k=k, ab=ab, ht=ht: e.matmul(
                        ps[:, :], wkr[:, k, ab * 128:(ab + 1) * 128], ht[:, k, :],
                        start=(k == 0), stop=(k == NCH - 1)), [wkr, ht], [ps])
                pab.append(ps)
            rotate(pab[0], pab[1], lambda: kr_t[:], [kr_t], t0)
            P.dma("sp", lambda e, t0=t0: e.dma_start(out=kr_d[:, t0:t0 + n], in_=kr_t[:]), [kr_t], [tk["kr"][ti]], kr_t)
            for h in range(8):
                ps = self.psum.next()
                for kc in range(2):
                    P.op("pe", lambda e, ps=ps, kc=kc, h=h: e.matmul(
                        ps[:, :], wqn[:, kc, h, :], cqn[:, kc, :], start=(kc == 0), stop=(kc == 1)), [wqn, cqn], [ps])
                if h % 2 == 0:
                    P.op("act", lambda e, ps=ps, h=h: e.copy(qn_t[:, h, :], ps[:, :]), [ps], [qn_t])
                else:
                    P.op("dve", lambda e, ps=ps, h=h: e.tensor_copy(qn_t[:, h, :], ps[:, :]), [ps], [qn_t])
            P.dma("sp", lambda e, t0=t0: e.dma_start(out=qn_d[:, :, t0:t0 + n].rearrange("h p t -> p h t"), in_=qn_t[:]),
                  [qn_t], [tk["qn"][ti]], qn_t)
            for pr in range(4):
                pab = []
                for W in (wqrA, wqrB):
                    ps = self.psum.next()
                    for kc in range(2):
                        P.op("pe", lambda e, ps=ps, kc=kc, pr=pr, W=W: e.matmul(
                            ps[:, :], W[:, kc, 2 * pr:2 * pr + 2, :].rearrange("p h e -> p (h e)"), cqn[:, kc, :],
                            start=(kc == 0), stop=(kc == 1)), [W, cqn], [ps])
                    pab.append(ps)
                rotate(pab[0], pab[1], lambda pr=pr: qr_t[:, pr, :], [qr_t], t0)
            P.dma("sp", lambda e, t0=t0: e.dma_start(out=qr_d[:, :, t0:t0 + n].rearrange("h p t -> p h t"), in_=qr_t[:]),
                  [qr_t], [tk["qr"][ti]], qr_t)
            for h in range(8):
                ps = self.psum.next()
                P.op("pe", lambda e, ps=ps, h=h: e.matmul(ps[:, :], wkn[:, h, :], ckvn[:], start=True, stop=True),
                     [wkn, ckvn], [ps])
                if h % 2 == 0:
                    P.op("act", lambda e, ps=ps, h=h: e.copy(kn_t[:, h, :], ps[:, :]), [ps], [kn_t])
                else:
                    P.op("dve", lambda e, ps=ps, h=h: e.tensor_copy(kn_t[:, h, :], ps[:, :]), [ps], [kn_t])
            P.dma("sp", lambda e, t0=t0: e.dma_start(out=kn_d[:, :, t0:t0 + n].rearrange("h p t -> p h t"), in_=kn_t[:]),
                  [kn_t], [tk["kn"][ti]], kn_t)
            for s in range(4):
                for hh in range(2):
                    ps = self.psum.next()
                    P.op("pe", lambda e, ps=ps, s=s, hh=hh: e.matmul(
                        ps[:, :], ckvn[:, s * 128:(s + 1) * 128],
                        wv[:, 4 * hh:4 * hh + 4, :].rearrange("p h e -> p (h e)"), start=True, stop=True),
                        [wv, ckvn], [ps])
                    if hh == 0:
                        P.op("act", lambda e, ps=ps, s=s: e.copy(v_t[:, s, 0:512], ps[:, :]), [ps], [v_t])
                    else:
                        P.op("dve", lambda e, ps=ps, s=s: e.tensor_copy(v_t[:, s, 512:1024], ps[:, :]), [ps], [v_t])
            P.dma("sp", lambda e, t0=t0: e.dma_start(out=v_d[t0:t0 + n, :].rearrange("(s p) f -> p s f", p=128), in_=v_t[:]),
                  [v_t], [tk["v"][ti]], v_t)
        P.end_scope()

        P.begin_scope()
        NKB = T // 128
        qn_r = P.ring("qn", [128, T], BF16, 2)
        kn_r = P.ring("kn", [128, T], BF16, 2)
        qr_r = P.ring("qr", [128, T], BF16, 2)
        v_r = P.ring("v", [128, NKB, 128], BF16, 2)
        ob_r = P.ring("ob", [128, T], BF16, 2)
        kr2 = P.sb("kr2", [128, T], BF16)
        tri_b = P.sb("tri_b", [128, 128], BF16)
        P.op("dve", lambda e: e.tensor_copy(tri_b[:], self.tri_f[:]), [self.tri_f], [tri_b])
        pt_r = P.ring("pt", [128, 512], BF16, 4)
        rc_r = P.ring("rc", [128, 512], F32, 2)
        acc_r = Ring([(self.psum.bufs[0], self.psum.bufs[1]), (self.psum.bufs[2], self.psum.bufs[3])])
        s_r = Ring(self.psum.bufs[4:7])
        P.dma("sp", lambda e: e.dma_start(out=kr2[:], in_=kr_d[:, :]), tk["kr"], [kr2], kr2)
        SCALE = 192.0 ** -0.5
        qr = None
        for h in range(8):
            qn, kn, v, ob = qn_r.next(), kn_r.next(), v_r.next(), ob_r.next()
            P.dma("sp", lambda e, qn=qn, h=h: e.dma_start(out=qn[:], in_=qn_d[h, :, :]), tk["qn"], [qn], qn)
            P.dma("sp", lambda e, kn=kn, h=h: e.dma_start(out=kn[:], in_=kn_d[h, :, :]), tk["kn"], [kn], kn)
            P.dma("sp", lambda e, v=v, h=h: [e.dma_start(
                out=v[:, q * (NKB // 4):(q + 1) * (NKB // 4), :],
                in_=v_d[q * (T // 4):(q + 1) * (T // 4), h * 128:(h + 1) * 128].rearrange("(n p) e -> p n e", p=128))
                for q in range(4)], tk["v"], [v], v)
            if h % 2 == 0:
                qr = qr_r.next()
                P.dma("sp", lambda e, qr=qr, h=h: e.dma_start(out=qr[:], in_=qr_d[h // 2, :, :]), tk["qr"], [qr], qr)
            r0 = 64 * (h % 2)
            for qi in range(T // 512):
                O_ps, R_ps = acc_r.next()
                nkb = 4 * (qi + 1)
                q0 = qi * 512
                pend = None

                def flush(pend):
                    kb, c0, pt = pend
                    P.op("pe", lambda e, O_ps=O_ps, kb=kb, c0=c0, pt=pt, v=v, nkb=nkb: e.matmul(
                        O_ps[:, c0:512], v[:, kb, :], pt[:, c0:512], start=(kb == 0), stop=(kb == nkb - 1)),
                        [v, pt], [O_ps])
                    P.op("pe", lambda e, R_ps=R_ps, kb=kb, c0=c0, pt=pt, nkb=nkb: e.matmul(
                        R_ps[:, c0:512], self.ones_b[:], pt[:, c0:512], start=(kb == 0), stop=(kb == nkb - 1)),
                        [self.ones_b, pt], [R_ps])

                for kb in range(nkb):
                    j = kb - 4 * qi
                    c0 = max(0, j) * 128
                    S_ps = s_r.next()
                    P.op("pe", lambda e, S_ps=S_ps, kb=kb, c0=c0, q0=q0, kn=kn, qn=qn: e.matmul(
                        S_ps[:, c0:512], kn[:, kb * 128:(kb + 1) * 128], qn[:, q0 + c0:q0 + 512],
                        start=True, stop=False), [kn, qn], [S_ps])
                    P.op("pe", lambda e, S_ps=S_ps, kb=kb, c0=c0, q0=q0, qr=qr, r0=r0: e.matmul(
                        S_ps[:, c0:512], kr2[r0:r0 + 64, kb * 128:(kb + 1) * 128], qr[r0:r0 + 64, q0 + c0:q0 + 512],
                        start=False, stop=True), [kr2, qr], [S_ps])
                    if pend is not None:
                        flush(pend)
                    pt = pt_r.next()
                    P.op("act", lambda e, S_ps=S_ps, pt=pt, c0=c0: e.activation(
                        pt[:, c0:512], S_ps[:, c0:512], AF.Exp, scale=SCALE), [S_ps], [pt])
                    if j >= 0:
                        P.op("pool", lambda e, pt=pt, c0=c0: e.tensor_tensor(
                            pt[:, c0:c0 + 128], pt[:, c0:c0 + 128], tri_b[:], ALU.mult), [pt, tri_b], [pt])
                    pend = (kb, c0, pt)
                flush(pend)
                rc = rc_r.next()
                P.op("dve", lambda e, rc=rc, R_ps=R_ps: e.reciprocal(rc[:], R_ps[:, :]), [R_ps], [rc])
                P.op("dve", lambda e, rc=rc, O_ps=O_ps, ob=ob, q0=q0: e.tensor_tensor(
                    ob[:, q0:q0 + 512], O_ps[:, :], rc[:], ALU.mult), [O_ps, rc], [ob])
            P.dma("sp", lambda e, ob=ob, h=h: e.dma_start(out=oT_d[h, :, :], in_=ob[:]), [ob], [otk[h]], ob)
        P.end_scope()

        P.begin_scope()
        self.work(n)
        wo = P.sb("wo", [128, 8, D], BF16)
        P.dma("pool", lambda e: e.dma_start(out=wo[:], in_=w_o.rearrange("(h p) n -> p h n", p=128)), [], [wo], wo)
        ot_r = P.ring("ot", [128, 8, n], BF16, 2)
        def load3(ti):
            t0 = ti * n
            xt = self.load_xt(t0, n)
            ot = ot_r.next()
            P.dma("sp", lambda e, ot=ot, t0=t0: e.dma_start(out=ot[:], in_=oT_d[:, :, t0:t0 + n].rearrange("h p t -> p h t")),
                  otk, [ot], ot)
            return xt, ot

        nxt = load3(0)
        for ti in range(NTL):
            t0 = ti * n
            xt, ot = nxt
            if ti + 1 < NTL:
                nxt = load3(ti + 1)
            for c in range(NCH):
                ps = self.psum.next()
                for h in range(8):
                    P.op("pe", lambda e, ps=ps, h=h, c=c, ot=ot: e.matmul(
                        ps[:, :], wo[:, h, c * 128:(c + 1) * 128], ot[:, h, :], start=(h == 0), stop=(h == 7)),
                        [wo, ot], [ps])
                P.op("dve", lambda e, ps=ps, c=c, xt=xt: e.scalar_tensor_tensor(
                    xt[:, c, :], ps[:, :], GA1[:, c:c + 1], xt[:, c, :], ALU.mult, ALU.add), [ps, xt, GA1], [xt])
            self.store_xt(xt, t0, n)
        P.end_scope()

    def phase_hg(self, l):
        P, T = self.P, self.T
        n = 256
        NS = n // 128
        NC = n // 64
        w_in, ng_d, w_o = (self.L(k, 0) for k in ("hg_w_in", "hg_norm_g", "hg_w_o"))
        hg_lb = self.din("hg_lb", [DEPTH, NCH, 128])
        k_mask2 = self.din("k_mask2", [128, 128])
        ng = self.load_cols("hgng", ng_d[:, :], 1)
        lbc = [self.load_cols(f"hglb{d}", hg_lb[d, :, :], NCH) for d in range(DEPTH)]
        lb = P.sb("hg_lbv", [128, NCH], F32)
        oml = P.sb("hg_oml", [128, NCH], F32)
        esum = P.sb("hg_esum", [128, NCH], F32)
        enum_ = P.sb("hg_enum", [128, NCH], F32)
        for d in range(DEPTH):
            P.op("act", lambda e, d=d: e.activation(lbc[d][:], lbc[d][:], AF.Exp), [lbc[d]], [lbc[d]])
        P.op("dve", lambda e: e.tensor_tensor(esum[:], lbc[0][:], lbc[1][:], ALU.add), [lbc[0], lbc[1]], [esum])
        P.op("dve", lambda e: e.tensor_tensor(esum[:], esum[:], lbc[2][:], ALU.add), [esum, lbc[2]], [esum])
        P.op("dve", lambda e: e.tensor_tensor(esum[:], esum[:], lbc[3][:], ALU.add), [esum, lbc[3]], [esum])
        P.op("dve", lambda e: e.memset(enum_[:], 0.0), [], [enum_])
        for d in range(1, l + 1):
            P.op("dve", lambda e, d=d: e.tensor_tensor(enum_[:], enum_[:], lbc[d][:], ALU.add), [enum_, lbc[d]], [enum_])
        P.op("dve", lambda e: e.reciprocal(esum[:], esum[:]), [esum], [esum])
        P.op("dve", lambda e: e.tensor_tensor(lb[:], enum_[:], esum[:], ALU.mult), [enum_, esum], [lb])
        P.op("dve", lambda e: e.tensor_scalar(oml[:], lb[:], -1.0, 1.0, ALU.mult, ALU.add), [lb], [oml])
        G1, SH1, GA1 = self.G1[l], self.SH1[l], self.GA1[l]

        P.begin_scope()
        self.work(n, nxt=2)
        win = P.sb("hwin", [128, NCH, 4096], BF16)
        wo = P.sb("hwo", [128, 8, D], BF16)
        for h in range(4):
            src = w_in[:, h * 1024:(h + 1) * 1024].rearrange("(k p) n -> p k n", p=128)
            P.dma("pool", lambda e, src=src, h=h: e.dma_start(out=win[:, :, h * 1024:(h + 1) * 1024], in_=src),
                  [], [win], win)
        P.dma("pool", lambda e: e.dma_start(out=wo[:], in_=w_o.rearrange("(h p) n -> p h n", p=128)), [], [wo], wo)
        mask2 = P.sb("mask2", [128, 128], F32)
        P.dma("sp", lambda e: e.dma_start(out=mask2[:], in_=k_mask2[:, :]), [], [mask2], mask2)
        ident_b = P.sb("ident_b", [128, 128], BF16)
        P.op("dve", lambda e: e.tensor_copy(ident_b[:], self.ident[:]), [self.ident], [ident_b])
        rmask = P.sb("rmask", [128, n], F32)
        P.op("dve", lambda e: e.memset(rmask[:], 1.0), [], [rmask])
        for c in range(NC):
            P.op("dve", lambda e, c=c: e.memset(rmask[:, 64 * c:64 * c + 1], 0.0), [], [rmask])
        S = P.sb("hgS", [128, 8, 128], F32)
        S_bf = [P.sb(f"hgSb{g}", [128, 4, 128], BF16) for g in range(2)]
        P.op("dve", lambda e: e.memset(S[:], 0.0), [], [S])
        for g in range(2):
            P.op("pool", lambda e, g=g: e.memset(S_bf[g][:], 0.0), [], [S_bf[g]])
        ebs = P.ring("hgebA", [128, NC, 8], F32, 2)
        Qt = [P.sb(f"hgQ{h}", [128, n], BF16) for h in range(8)]
        Kt = [P.sb(f"hgK{h}", [128, n], BF16) for h in range(8)]
        Ktok = [P.sb(f"hgKt{h}", [128, NS, 128], BF16) for h in range(8)]
        sgate = [P.sb(f"hgsg{h}", [128, n], BF16) for h in range(8)]
        V_t = P.sb("hgV", [128, NS, 1024], BF16)
        o_f = P.sb("hgo", [128, 8, n], F32)
        ogT = P.sb("hgog", [128, 8, n], BF16)
        tA_r = P.ring("hgtA", [128, n], F32, 6)
        tB_r = P.ring("hgtB", [128, n], F32, 8)
        tC_r = P.ring("hgtC", [128, n], F32, 6)
        tD_r = P.ring("hgtD", [128, n], F32, 5)
        tr = P.ring("hgt", [128, n], F32, 6)
        at_r = P.ring("hgAT", [128, 128], BF16, 8)
        st_r = P.ring("hgSt", [128, 512], F32, 3)
        sqh = P.ring("hgsq", [128, n], BF16, 3)

        hts = P.ring("hght", [128, NCH, n], BF16, 2)

        def prep(ti):
            xt = self.load_xt(ti * n, n)
            return xt, self.norm_mod(xt, n, G1, SH1, ht=hts.next())

        nxt_prep = prep(0)
        for ti in range(T // n):
            t0 = ti * n
            xt, ht = nxt_prep
            ebA = ebs.next()
            for s in range(NS):
                for hh in range(2):
                    ps = self.psum.next()
                    for k in range(NCH):
                        P.op("pe", lambda e, ps=ps, k=k, s=s, hh=hh, ht=ht: e.matmul(
                            ps[:, :], ht[:, k, s * 128:(s + 1) * 128], win[:, k, 2048 + hh * 512:2048 + (hh + 1) * 512],
                            start=(k == 0), stop=(k == NCH - 1)), [win, ht], [ps])
                    if hh == 0:
                        P.op("act", lambda e, ps=ps, s=s: e.copy(V_t[:, s, 0:512], ps[:, :]), [ps], [V_t])
                    else:
                        P.op("dve", lambda e, ps=ps, s=s: e.tensor_copy(V_t[:, s, 512:1024], ps[:, :]), [ps], [V_t])
            st = {}

            def proj(off, h, ht=ht):
                ps = self.psum.next()
                for k in range(NCH):
                    P.op("pe", lambda e, ps=ps, k=k: e.matmul(
                        ps[:, 0:n], win[:, k, off + h * 128:off + (h + 1) * 128], ht[:, k, :],
                        start=(k == 0), stop=(k == NCH - 1)), [win, ht], [ps])
                return ps

            def h0(h):
                st[h] = {"fl": proj(1024, h), "g": proj(3072, h)}

            def h1(h):
                d = st[h]
                d["A"], d["B"] = tA_r.next(), tB_r.next()
                P.op("act", lambda e: e.activation(d["A"][:], d["fl"][:, 0:n], AF.Exp, scale=-1.0), [d["fl"]], [d["A"]])
                P.op("act", lambda e: e.activation(d["B"][:], d["g"][:, 0:n], AF.Exp, scale=-1.0), [d["g"]], [d["B"]])
                P.op("act", lambda e: e.copy(sgate[h][:], d["g"][:, 0:n]), [d["g"]], [sgate[h]])

            def h1b(h):
                d = st[h]
                P.op("act", lambda e: e.activation(d["A"][:], d["A"][:], AF.Ln, bias=1.0), [d["A"]], [d["A"]])
                P.op("act", lambda e: e.activation(d["B"][:], d["B"][:], AF.Ln, bias=1.0), [d["B"]], [d["B"]])

            def h1c(h):
                d = st[h]
                P.op("act", lambda e: e.activation(d["A"][:], d["A"][:], AF.Exp, scale=-1.0), [d["A"]], [d["A"]])
                P.op("act", lambda e: e.activation(d["B"][:], d["B"][:], AF.Exp, scale=-1.0), [d["B"]], [d["B"]])

            def h2(h):
                d = st[h]
                A, B = d["A"], d["B"]
                P.op("dve", lambda e: e.tensor_scalar(A[:], A[:], oml[:, h:h + 1], lb[:, h:h + 1], ALU.mult, ALU.add),
                     [A, oml, lb], [A])
                P.op("dve", lambda e: e.tensor_tensor(sgate[h][:], sgate[h][:], B[:], ALU.mult), [sgate[h], B], [sgate[h]])

            def h3(h):
                d = st[h]
                d["C"] = tC_r.next()
                P.op("act", lambda e: e.activation(d["C"][:], d["A"][:], AF.Ln), [d["A"]], [d["C"]])
                P.op("pool", lambda e: e.tensor_scalar(d["B"][:], d["A"][:], -1.0, 1.0, ALU.mult, ALU.add), [d["A"]], [d["B"]])

            def h4(h):
                d = st[h]
                d["D"] = tD_r.next()
                P.op("dve", lambda e: e.tensor_tensor_scan(d["D"][:], rmask[:], d["C"][:], 0.0, ALU.mult, ALU.add),
                     [rmask, d["C"]], [d["D"]])

            def h5(h):
                d = st[h]
                P.op("act", lambda e: e.activation(d["C"][:], d["D"][:], AF.Exp), [d["D"]], [d["C"]])
                P.op("act", lambda e: e.activation(d["D"][:], d["D"][:], AF.Exp, scale=-1.0), [d["D"]], [d["D"]])

            def h6(h):
                st[h]["q"] = proj(0, h)

            def h7(h, ebA=ebA):
                d = st[h]
                P.op("dve", lambda e: e.tensor_tensor(Qt[h][:], d["q"][:, 0:n], d["C"][:], ALU.mult), [d["q"], d["C"]], [Qt[h]])
                P.op("dve", lambda e, ebA=ebA: e.tensor_copy(ebA[:, :, h], d["C"][:].rearrange("p (c t) -> p c t", t=64)[:, :, 63]),
                     [d["C"]], [ebA])
                P.op("pool", lambda e: e.tensor_tensor(Kt[h][:], d["B"][:], d["D"][:], ALU.mult), [d["B"], d["D"]], [Kt[h]])

            def h8(h):
                pb = self.psb
                hb = (h % 2) * 512
                for s in range(NS):
                    P.op("pe", lambda e, s=s: e.transpose(
                        pb[:, hb + s * 128:hb + (s + 1) * 128], Kt[h][:, s * 128:(s + 1) * 128], ident_b[:]),
                        [Kt[h], ident_b], [pb])
                P.op("act", lambda e: e.copy(Ktok[h][:].rearrange("p s d -> p (s d)"), pb[:, hb:hb + NS * 128]),
                     [pb], [Ktok[h]])

            self.pipeline(list(range(8)), [h0, h1, h1b, h1c, h2, h3, h4, h5, h6, h7, h8])
            if ti + 1 < T // n:
                nxt_prep = prep(ti + 1)

            for s in range(NS):
                cols = slice(s * 128, (s + 1) * 128)
                bA = [self.psum.next(), self.psum.next()]
                for h in range(8):
                    P.op("pe", lambda e, h=h, cols=cols, b=bA[h // 4]: e.matmul(
                        b[:, (h % 4) * 128:(h % 4 + 1) * 128], Kt[h][:, cols], Qt[h][:, cols], start=True, stop=True),
                        [Kt[h], Qt[h]], [bA[h // 4]])
                ats = []
                for h in range(8):
                    at = at_r.next()
                    ats.append(at)
                    P.op("dve", lambda e, h=h, at=at, b=bA[h // 4]: e.tensor_tensor(
                        at[:], b[:, (h % 4) * 128:(h % 4 + 1) * 128], mask2[:], ALU.mult), [bA[h // 4], mask2], [at])
                bO = [self.psum.next(), self.psum.next()]
                for h in range(8):
                    P.op("pe", lambda e, h=h, at=ats[h], s=s, b=bO[h // 4]: e.matmul(
                        b[:, (h % 4) * 128:(h % 4 + 1) * 128], V_t[:, s, h * 128:(h + 1) * 128], at[:],
                        start=(h % 4 == 0), stop=False, skip_group_check=True), [V_t, ats[h]], [bO[h // 4]])
                for half in range(2):
                    po = 64 * half
                    cc = slice(s * 128 + po, s * 128 + po + 64)
                    c = 2 * s + half
                    bS = [self.psum.next(), self.psum.next()]
                    for h in range(8):
                        P.op("pe", lambda e, h=h, cc=cc, po=po, half=half, b=bO[h // 4]: e.matmul(
                            b[:, (h % 4) * 128 + po:(h % 4) * 128 + po + 64], S_bf[h // 4][:, h % 4, :], Qt[h][:, cc],
                            start=False, stop=(half == 1), skip_group_check=True), [S_bf[h // 4], Qt[h]], [bO[h // 4]])
                        P.op("pe", lambda e, h=h, s=s, po=po, b=bS[h // 4]: e.matmul(
                            b[:, (h % 4) * 128:(h % 4 + 1) * 128], Ktok[h][po:po + 64, s, :],
                            V_t[po:po + 64, s, h * 128:(h + 1) * 128], start=True, stop=True),
                            [Ktok[h], V_t], [bS[h // 4]])
                    for g in range(2):
                        stt = st_r.next()
                        P.op("dve", lambda e, g=g, stt=stt, b=bS[g]: e.tensor_tensor(
                            stt[:], b[:, :], S[:, 4 * g:4 * g + 4, :].rearrange("p h d -> p (h d)"), ALU.add),
                            [bS[g], S], [stt])
                        P.op("dve", lambda e, g=g, stt=stt, c=c, ebA=ebA: e.tensor_tensor(
                            S[:, 4 * g:4 * g + 4, :], stt[:].rearrange("p (h d) -> p h d", h=4),
                            ebA[:, c, 4 * g:4 * g + 4].rearrange("p (h o) -> p h o", o=1).to_broadcast([128, 4, 128]),
                            ALU.mult), [stt, ebA], [S])
                        P.op("act", lambda e, g=g: e.copy(S_bf[g][:], S[:, 4 * g:4 * g + 4, :]), [S], [S_bf[g]])
                for h in range(8):
                    P.op("act", lambda e, h=h, cols=cols, b=bO[h // 4]: e.copy(
                        o_f[:, h, cols], b[:, (h % 4) * 128:(h % 4 + 1) * 128]), [bO[h // 4]], [o_f])
            nd = {}

            def n0(h, nd=nd):
                sq = sqh.next()
                nd[h] = {"sq": sq}
                P.op("pool", lambda e: e.tensor_tensor(sq[:], o_f[:, h, :], o_f[:, h, :], ALU.mult), [o_f], [sq])

            def n1(h, nd=nd):
                ps, sq = self.psum.next(), nd[h]["sq"]
                nd[h]["ps"] = ps
                P.op("pe", lambda e: e.matmul(ps[:, 0:n], self.ones_b[:], sq[:], start=True, stop=True),
                     [sq, self.ones_b], [ps])

            def n2(h, nd=nd):
                t1, ps = tr.next(), nd[h]["ps"]
                nd[h]["t1"] = t1
                P.op("dve", lambda e: e.tensor_scalar(t1[:], ps[:, 0:n], 1.0 / 128, EPS, ALU.mult, ALU.add), [ps], [t1])

            def n3(h, nd=nd):
                t1 = nd[h]["t1"]
                P.op("act", lambda e: e.activation(t1[:], t1[:], AF.Ln), [t1], [t1])

            def n4(h, nd=nd):
                t1 = nd[h]["t1"]
                P.op("act", lambda e: e.activation(t1[:], t1[:], AF.Exp, scale=-0.5), [t1], [t1])

            def n5(h, nd=nd):
                t1 = nd[h]["t1"]
                P.op("dve", lambda e: e.scalar_tensor_tensor(
                    t1[:], o_f[:, h, :], ng[:, 0:1], t1[:], ALU.mult, ALU.mult), [o_f, ng, t1], [t1])

            def n6(h, nd=nd):
                t1 = nd[h]["t1"]
                P.op("pool", lambda e: e.tensor_tensor(ogT[:, h, :], t1[:], sgate[h][:], ALU.mult), [t1, sgate[h]], [ogT])

            self.pipeline(list(range(8)), [n0, n1, n2, n3, n4, n5, n6])
            for c in range(NCH):
                ps = self.psum.next()
                for h in range(8):
                    P.op("pe", lambda e, ps=ps, h=h, c=c: e.matmul(
                        ps[:, 0:n], wo[:, h, c * 128:(c + 1) * 128], ogT[:, h, :], start=(h == 0), stop=(h == 7)),
                        [wo, ogT], [ps])
                P.op("dve", lambda e, ps=ps, c=c, xt=xt: e.scalar_tensor_tensor(
                    xt[:, c, 0:n], ps[:, 0:n], GA1[:, c:c + 1], xt[:, c, 0:n], ALU.mult, ALU.add), [ps, xt, GA1], [xt])
            self.store_xt(xt, t0, n)
        P.end_scope()


FULL_PHASES = [("gm", 0), ("ffn", 0), ("mla", 1), ("ffn", 1), ("hg", 2), ("ffn", 2), ("gm", 3), ("ffn", 3)]


def make_consts():
    ident = np.eye(128, dtype=np.float32)
    tri = np.triu(np.ones((128, 128), dtype=np.float32))
    j = np.arange(128) % 32
    invf = np.zeros((2, 128), dtype=np.float32)
    invf[0] = (10000.0 ** (-(2.0 * j) / 64.0)).astype(np.float32)
    invf[1] = np.where((np.arange(128) // 32) % 2 == 0, -1.0, 1.0)
    sel = np.zeros((8, 1024), dtype=np.float32)
    for g in range(8):
        sel[g, g * 128:(g + 1) * 128] = 1.0
    mask2 = np.zeros((128, 128), dtype=np.float32)
    mask2[0:64, 0:64] = tri[0:64, 0:64]
    mask2[64:128, 64:128] = tri[0:64, 0:64]
    return {"k_ident": ident, "k_tri": tri, "k_invf": invf, "k_sel": sel, "k_mask2": mask2}


LAYER_SHAPES = {
    "ada_w": [D, 6 * D], "ada_b": [48, 128], "mix_norm_g": [NCH, 128], "ffn_norm_g": [NCH, 128],
    "gm_w_in": [D, 4096], "gm_ln_g": [16, 128], "gm_ln_b": [16, 128], "gm_w_s": [8, 128, 128],
    "gm_b_s": [8, 128], "gm_w_out": [GM_INNER, D],
    "mla_w_a": [D, 448], "mla_q_norm_g": [2, 128], "mla_kv_norm_g": [1, 128], "mla_w_qb": [256, 1536],
    "mla_w_kvb": [128, 2048], "mla_w_o": [1024, 1024],
    "hg_w_in": [D, 4096], "hg_norm_g": [1, 128], "hg_w_o": [1024, 1024],
    "ff_w_up": [D, 2 * FF], "ff_conv_w": [3, 44, 128], "ff_conv_b": [44, 128], "ff_w_down": [FF, D],
}


def core_inputs(inp, names, b, T, consts):
    f = np.ascontiguousarray
    m = {}
    for nm in names:
        if nm in consts:
            m[nm] = consts[nm]
        elif "__" in nm:
            base, idx = nm.split("__")
            m[nm] = f(np.asarray(inp[base][int(idx)], dtype=np.float32).reshape(LAYER_SHAPES[base]))
        elif nm == "x":
            m[nm] = f(inp["x"][b, :T])
        elif nm == "c":
            m[nm] = f(inp["c"][b].reshape(NCH, 128))
        elif nm == "positions":
            m[nm] = f(inp["positions"][b, :T].reshape(1, T).astype(np.int32))
        elif nm == "final_g":
            m[nm] = f(inp["final_g"].reshape(NCH, 128))
        elif nm == "hg_lb":
            m[nm] = f(inp["hg_lb"].reshape(DEPTH, NCH, 128))
        else:
            raise KeyError(nm)
    return m


def run(inp, T, phases, trace=False, debug=False):
    bld = Builder(T, phases)
    bld.debug = debug
    nc = bld.build()
    names = list(bld.decl.keys())
    B = inp["x"].shape[0]
    consts = make_consts()
    per_b = [core_inputs(inp, names, b, T, consts) for b in range(B)]
    in_maps = [per_b[c % B] for c in range(N_CORES)]
    res = run_bass_kernel_spmd(nc, in_maps, core_ids=list(range(N_CORES)), trace=trace)
    out = np.stack([res.results[b]["y"] for b in range(B)], axis=0)
    return out, res


def kernel(**inputs):
    inp = {k: np.asarray(v) for k, v in inputs.items()}
    out, _ = run(inp, inp["x"].shape[1], FULL_PHASES)
    return out.astype(np.float32)
```
